# Optimizing a Trainium2 kernel written in Bass

```python
import jax, jax.numpy as jnp
from jax import lax
import numpy as np

D_MODEL = 1024
BATCH = 8
SEQ = 8192
DEPTH = 1
DEC_BATCH = 16
DEC_SEQ = 2048
PAST_LEN = 128

DN_HEADS = 4
DN_DK = 128
DN_DV = 128
DN_QK = DN_HEADS * DN_DK
DN_V = DN_HEADS * DN_DV
DN_CHUNK = 64
CONV_K = 5
DN_CONV_CH = 2 * DN_QK + DN_V
SG_GROUPS = 4
SG_GROUP_DIM = 128
SG_WIDTH = SG_GROUPS * SG_GROUP_DIM
SG_CHUNK = 128
N_MEM = 256
XA_HEADS = 4
XA_HEAD_DIM = D_MODEL // XA_HEADS
D_FF = ((8 * D_MODEL + 3 * 256 - 1) // (3 * 256)) * 256
N_IN = DN_CONV_CH + 4 * DN_HEADS + DN_V + 2 * SG_WIDTH + 2 * D_MODEL
EPS = 1e-6

kernel_name = 'hybrid_deltanet_gmlp_memxattn_encoder'


def rmsnorm(x, w):
    xf = x.astype(jnp.float32)
    y = xf * lax.rsqrt(jnp.mean(xf * xf, -1, keepdims=True) + EPS)
    return (y * w.astype(jnp.float32)).astype(x.dtype)


def layernorm(x, w, b):
    xf = x.astype(jnp.float32)
    mu = jnp.mean(xf, -1, keepdims=True)
    xc = xf - mu
    y = xc * lax.rsqrt(jnp.mean(xc * xc, -1, keepdims=True) + EPS)
    return (y * w.astype(jnp.float32) + b.astype(jnp.float32)).astype(x.dtype)


def l2norm(x):
    xf = x.astype(jnp.float32)
    return (xf * lax.rsqrt(jnp.sum(xf * xf, -1, keepdims=True) + EPS)).astype(x.dtype)


def depthwise_conv(x, w):
    c = x.shape[-1]
    return lax.conv_general_dilated(
        x, w[:, None, :].astype(x.dtype), window_strides=(1,),
        padding=[(CONV_K // 2, CONV_K // 2)],
        dimension_numbers=('NWC', 'WIO', 'NWC'), feature_group_count=c)


def gated_delta_chunked(q, k, v, g, beta):
    out_dtype = v.dtype
    bsz, nh, seq, dk = q.shape
    dv = v.shape[-1]
    c = DN_CHUNK
    n = seq // c
    q = q.astype(jnp.float32).reshape(bsz, nh, n, c, dk) * (dk ** -0.5)
    k = k.astype(jnp.float32).reshape(bsz, nh, n, c, dk)
    v = v.astype(jnp.float32).reshape(bsz, nh, n, c, dv)
    g = jnp.cumsum(g.astype(jnp.float32).reshape(bsz, nh, n, c), -1)
    beta = beta.astype(jnp.float32).reshape(bsz, nh, n, c)
    incl = jnp.tril(jnp.ones((c, c), dtype=bool))
    strict = jnp.tril(jnp.ones((c, c), dtype=bool), -1)
    diff = g[..., :, None] - g[..., None, :]
    decay = jnp.where(incl, jnp.exp(jnp.where(incl, diff, 0.0)), 0.0)
    kb = k * beta[..., None]
    lower = jnp.where(strict, jnp.einsum('bhncd,bhnsd->bhncs', kb, k) * decay, 0.0)
    eye = jnp.eye(c, dtype=jnp.float32)
    tinv = lax.linalg.triangular_solve(lower + eye, jnp.broadcast_to(eye, lower.shape),
                                       left_side=True, lower=True, unit_diagonal=True)
    u = jnp.einsum('bhncs,bhnsd->bhncd', tinv, v * beta[..., None])
    w = jnp.einsum('bhncs,bhnsd->bhncd', tinv, kb * jnp.exp(g)[..., None])
    a_intra = jnp.einsum('bhncd,bhnsd->bhncs', q, k) * decay
    g_last = g[..., -1]
    qg = q * jnp.exp(g)[..., None]
    kg = k * jnp.exp(g_last[..., None] - g)[..., None]

    def step(state, xs):
        qg_i, kg_i, u_i, w_i, a_i, gl_i = xs
        v_new = u_i - jnp.einsum('bhcd,bhde->bhce', w_i, state)
        o = jnp.einsum('bhcd,bhde->bhce', qg_i, state) + jnp.einsum('bhcs,bhse->bhce', a_i, v_new)
        state = state * jnp.exp(gl_i)[..., None, None] + jnp.einsum('bhcd,bhce->bhde', kg_i, v_new)
        return state, o

    xs = tuple(jnp.moveaxis(t, 2, 0) for t in (qg, kg, u, w, a_intra, g_last))
    s0 = jnp.zeros((bsz, nh, dk, dv), jnp.float32)
    _, o = lax.scan(step, s0, xs)
    o = jnp.moveaxis(o, 0, 2).reshape(bsz, nh, seq, dv)
    return o.astype(out_dtype)


def deltanet_branch(qkv_raw, ab, gate, conv_w, a_log, dt_bias, norm_w):
    bsz, seq, _ = qkv_raw.shape
    qkv = jax.nn.silu(depthwise_conv(qkv_raw, conv_w))
    q = qkv[..., :DN_QK]
    k = qkv[..., DN_QK:2 * DN_QK]
    v = qkv[..., 2 * DN_QK:]
    heads = lambda t, d: jnp.transpose(t.reshape(bsz, seq, DN_HEADS, d), (0, 2, 1, 3))
    q = l2norm(heads(q, DN_DK))
    k = l2norm(heads(k, DN_DK))
    v = heads(v, DN_DV)
    ab = jnp.transpose(ab.astype(jnp.float32).reshape(bsz, seq, 4, DN_HEADS), (2, 0, 3, 1))
    beta_f = jax.nn.sigmoid(ab[0])
    beta_b = jax.nn.sigmoid(ab[1])
    a_log = a_log.astype(jnp.float32)
    dt_bias = dt_bias.astype(jnp.float32)
    g_f = -jnp.exp(a_log[0])[None, :, None] * jax.nn.softplus(ab[2] + dt_bias[0][None, :, None])
    g_b = -jnp.exp(a_log[1])[None, :, None] * jax.nn.softplus(ab[3] + dt_bias[1][None, :, None])
    o_f = gated_delta_chunked(q, k, v, g_f, beta_f)
    flip = lambda t: jnp.flip(t, axis=2)
    o_b = flip(gated_delta_chunked(flip(q), flip(k), flip(v), flip(g_b), flip(beta_b)))
    o = jnp.transpose(o_f + o_b, (0, 2, 1, 3))
    o = rmsnorm(o, norm_w) * jax.nn.silu(gate.reshape(bsz, seq, DN_HEADS, DN_DV))
    return o.reshape(bsz, seq, DN_V)


def spatial_gating_branch(u, v, ln_w, ln_b, w_s, b_s):
    bsz, seq, _ = u.shape
    u = jax.nn.gelu(u)
    v = layernorm(jax.nn.gelu(v), ln_w, ln_b)
    v = v.reshape(bsz, seq // SG_CHUNK, SG_CHUNK, SG_GROUPS, SG_GROUP_DIM)
    mixed = jnp.einsum('gts,bnsgc->bntgc', w_s, v) + jnp.transpose(b_s)[None, None, :, :, None]
    return u * mixed.reshape(bsz, seq, SG_WIDTH)


def memory_cross_attention(h, mem, w_q, w_kv, w_o):
    bsz, seq, _ = h.shape
    q = (h @ w_q).reshape(bsz, seq, XA_HEADS, XA_HEAD_DIM)
    kv = (mem @ w_kv).reshape(bsz, mem.shape[1], 2, XA_HEADS, XA_HEAD_DIM)
    k = kv[:, :, 0]
    v = kv[:, :, 1]
    s = jnp.einsum('blhd,bmhd->bhlm', q, k).astype(jnp.float32) * (XA_HEAD_DIM ** -0.5)
    p = jax.nn.softmax(s, axis=-1).astype(v.dtype)
    o = jnp.einsum('bhlm,bmhd->blhd', p, v).reshape(bsz, seq, D_MODEL)
    return o @ w_o


def encoder_layer(x, mem, norm_mix_w, w_in, conv_w, dn_a_log, dn_dt_bias, dn_norm_w, w_up_a,
                  sg_ln_w, sg_ln_b, sg_w, sg_b, w_up_b, w_out, norm_xa_w, norm_mem_w,
                  xa_w_q, xa_w_kv, xa_w_o, norm_ffn_w, ffn_w_gate_up, ffn_w_down):
    h = rmsnorm(x, norm_mix_w)
    proj = h @ w_in
    sizes = [DN_CONV_CH, 4 * DN_HEADS, DN_V, SG_WIDTH, SG_WIDTH, D_MODEL]
    qkv_raw, ab, dn_gate, sg_u, sg_v, gate_a, gate_b = jnp.split(proj, np.cumsum(sizes).tolist(), axis=-1)
    y_a = deltanet_branch(qkv_raw, ab, dn_gate, conv_w, dn_a_log, dn_dt_bias, dn_norm_w) @ w_up_a
    y_b = spatial_gating_branch(sg_u, sg_v, sg_ln_w, sg_ln_b, sg_w, sg_b) @ w_up_b
    merged = jax.nn.sigmoid(gate_a) * y_a + jax.nn.sigmoid(gate_b) * y_b
    x = x + merged @ w_out
    x = x + memory_cross_attention(rmsnorm(x, norm_xa_w), rmsnorm(mem, norm_mem_w), xa_w_q, xa_w_kv, xa_w_o)
    h = rmsnorm(x, norm_ffn_w)
    gu = h @ ffn_w_gate_up
    x = x + (jax.nn.silu(gu[..., :D_FF]) * gu[..., D_FF:]) @ ffn_w_down
    return x


def setup_inputs(seed: int = 0) -> dict:
    key = jax.random.key(seed)
    ks = iter(jax.random.split(key, 48))
    f32 = jnp.float32
    nrm = lambda shape, fan: jax.random.normal(next(ks), shape, f32) * (fan ** -0.5)
    gain = lambda shape: 1.0 + 0.02 * jax.random.normal(next(ks), shape, f32)
    small = lambda shape: 0.02 * jax.random.normal(next(ks), shape, f32)
    nl = DEPTH
    a_log = jnp.log(jax.random.uniform(next(ks), (nl, 2, DN_HEADS), f32, 1.0, 16.0))
    dt = jnp.exp(jax.random.uniform(next(ks), (nl, 2, DN_HEADS), f32, np.log(1e-3), np.log(1e-1)))
    dt_bias = dt + jnp.log(-jnp.expm1(-dt))
    return {
        'x_prompt': jax.random.normal(next(ks), (BATCH, SEQ, D_MODEL), f32),
        'x_sample': jax.random.normal(next(ks), (DEC_BATCH, DEC_SEQ, D_MODEL), f32),
        'mem_prompt': jax.random.normal(next(ks), (BATCH, N_MEM, D_MODEL), f32),
        'mem_sample': jax.random.normal(next(ks), (DEC_BATCH, N_MEM, D_MODEL), f32),
        'norm_mix_w': gain((nl, D_MODEL)),
        'w_in': nrm((nl, D_MODEL, N_IN), D_MODEL),
        'conv_w': nrm((nl, CONV_K, DN_CONV_CH), CONV_K),
        'dn_a_log': a_log,
        'dn_dt_bias': dt_bias,
        'dn_norm_w': gain((nl, DN_DV)),
        'w_up_a': nrm((nl, DN_V, D_MODEL), DN_V),
        'sg_ln_w': gain((nl, SG_WIDTH)),
        'sg_ln_b': small((nl, SG_WIDTH)),
        'sg_w': nrm((nl, SG_GROUPS, SG_CHUNK, SG_CHUNK), SG_CHUNK),
        'sg_b': small((nl, SG_GROUPS, SG_CHUNK)),
        'w_up_b': nrm((nl, SG_WIDTH, D_MODEL), SG_WIDTH),
        'w_out': nrm((nl, D_MODEL, D_MODEL), D_MODEL),
        'norm_xa_w': gain((nl, D_MODEL)),
        'norm_mem_w': gain((nl, D_MODEL)),
        'xa_w_q': nrm((nl, D_MODEL, D_MODEL), D_MODEL),
        'xa_w_kv': nrm((nl, D_MODEL, 2 * D_MODEL), D_MODEL),
        'xa_w_o': nrm((nl, D_MODEL, D_MODEL), D_MODEL),
        'norm_ffn_w': gain((nl, D_MODEL)),
        'ffn_w_gate_up': nrm((nl, D_MODEL, 2 * D_FF), D_MODEL),
        'ffn_w_down': nrm((nl, D_FF, D_MODEL), D_FF),
        'final_norm_w': gain((D_MODEL,)),
    }


def reference(x_prompt, x_sample, mem_prompt, mem_sample, norm_mix_w, w_in, conv_w, dn_a_log,
              dn_dt_bias, dn_norm_w, w_up_a, sg_ln_w, sg_ln_b, sg_w, sg_b, w_up_b, w_out,
              norm_xa_w, norm_mem_w, xa_w_q, xa_w_kv, xa_w_o, norm_ffn_w, ffn_w_gate_up,
              ffn_w_down, final_norm_w):
    def trunk(x, mem):
        for l in range(DEPTH):
            x = encoder_layer(x, mem, norm_mix_w[l], w_in[l], conv_w[l], dn_a_log[l], dn_dt_bias[l],
                              dn_norm_w[l], w_up_a[l], sg_ln_w[l], sg_ln_b[l], sg_w[l], sg_b[l],
                              w_up_b[l], w_out[l], norm_xa_w[l], norm_mem_w[l], xa_w_q[l],
                              xa_w_kv[l], xa_w_o[l], norm_ffn_w[l], ffn_w_gate_up[l], ffn_w_down[l])
        return rmsnorm(x, final_norm_w)

    y_prompt = trunk(x_prompt, mem_prompt)
    y_sample = trunk(x_sample, mem_sample)
    return (y_prompt, y_sample)
```

```python
import numpy as np
import concourse.bass as bass
import concourse.mybir as mybir
from concourse.bass_utils import run_bass_kernel_spmd
from contextlib import ExitStack

F32 = mybir.dt.float32
BF16 = mybir.dt.bfloat16
AF = mybir.ActivationFunctionType
ALU = mybir.AluOpType
AX = mybir.AxisListType

COMPUTE = ("pe", "act", "dve", "pool")
ALLENG = ("pe", "act", "dve", "pool", "sp")
SAME_ENGINE_SYNC = False
SAME_ENGINE_RAW = True


class Sched:
    def __init__(self, nc, es, n_dma_sems=18):
        self.nc = nc
        self.n_dma_sems = n_dma_sems
        self.csem = {e: es.enter_context(nc.semaphore("c_" + e)) for e in COMPUTE}
        self.dsem = [es.enter_context(nc.semaphore("d_%d" % s)) for s in range(n_dma_sems)]
        self.cnt = {e: 0 for e in COMPUTE}
        self.dma_cnt = [0] * n_dma_sems
        self.dma_rr = 0
        self.dma_rr_sw = 0
        self.seen = {e: {f: 0 for f in COMPUTE} for e in ALLENG}
        self.seen_d = {e: [0] * n_dma_sems for e in ALLENG}
        self.n_ops = 0
        self.n_waits = 0
        self._reset()

    def _reset(self):
        self.ops = []
        self.lastw = {}
        self.readers = {}

    def op(self, eng, fn, reads=(), writes=(), dma=False, strict=False):
        idx = len(self.ops)
        deps = set()
        raw = set()
        for r in reads:
            w = self.lastw.get(r)
            if w is not None:
                deps.add(w)
                raw.add(w)
        for k in writes:
            w = self.lastw.get(k)
            if w is not None:
                deps.add(w)
            rs = self.readers.get(k)
            if rs:
                deps.update(rs.values())
        rkey = ("dma", idx) if dma else eng
        for r in reads:
            self.readers.setdefault(r, {})[rkey] = idx
        for k in writes:
            self.lastw[k] = idx
            self.readers[k] = {}
        o = dict(eng=eng, fn=fn, deps=deps, raw=raw, dma=dma, sig=False, strict=strict)
        if dma:
            nsw = 6
            if eng == "pool":
                s = self.n_dma_sems - nsw + self.dma_rr_sw
                self.dma_rr_sw = (self.dma_rr_sw + 1) % nsw
            else:
                s = self.dma_rr
                self.dma_rr = (self.dma_rr + 1) % (self.n_dma_sems - nsw)
            self.dma_cnt[s] += 1
            o["dsem"] = s
            o["dval"] = 16 * self.dma_cnt[s]
        self.ops.append(o)
        return idx

    def flush(self):
        nc = self.nc
        ops = self.ops
        if not ops:
            return
        last = {}
        for i, o in enumerate(ops):
            if not o["dma"]:
                last[o["eng"]] = i
        for e, i in last.items():
            ops[i]["sig"] = True
        for o in ops:
            for d in o["deps"]:
                od = ops[d]
                if not od["dma"] and (od["eng"] != o["eng"] or SAME_ENGINE_SYNC or o["strict"]
                                      or (SAME_ENGINE_RAW and d in o["raw"] and o["eng"] != "pe")):
                    od["sig"] = True
        for o in ops:
            if not o["dma"] and o["sig"]:
                self.cnt[o["eng"]] += 1
                o["sval"] = self.cnt[o["eng"]]
        streams = {e: [] for e in ALLENG}
        seen, seen_d = self.seen, self.seen_d
        for o in ops:
            e = o["eng"]
            wc, wd = {}, {}
            for d in o["deps"]:
                od = ops[d]
                if od["dma"]:
                    wd[od["dsem"]] = max(wd.get(od["dsem"], 0), od["dval"])
                else:
                    f = od["eng"]
                    if f == e and not (SAME_ENGINE_SYNC or o["strict"] or (SAME_ENGINE_RAW and d in o["raw"] and e != "pe")):
                        continue
                    wc[f] = max(wc.get(f, 0), od["sval"])
            if o["dma"]:
                prev = o["dval"] - 16
                if prev > 0:
                    wd[o["dsem"]] = max(wd.get(o["dsem"], 0), prev)
            wl = []
            for f, v in wc.items():
                if v > seen[e][f]:
                    seen[e][f] = v
                    wl.append((self.csem[f], v))
            for s, v in wd.items():
                if v > seen_d[e][s]:
                    seen_d[e][s] = v
                    wl.append((self.dsem[s], v))
            streams[e].append((o, wl))
        end_wl = {}
        for e in ALLENG:
            wl = []
            for f in COMPUTE:
                v = self.cnt[f]
                if f != e and v > seen[e][f]:
                    seen[e][f] = v
                    wl.append((self.csem[f], v))
            for s in range(self.n_dma_sems):
                v = 16 * self.dma_cnt[s]
                if v > seen_d[e][s]:
                    seen_d[e][s] = v
                    wl.append((self.dsem[s], v))
            end_wl[e] = wl
        self.n_ops += len(ops)
        self.n_waits += sum(len(wl) for st in streams.values() for _, wl in st)
        csem, dsem = self.csem, self.dsem

        def run(eng_name, engine):
            for o, wl in streams[eng_name]:
                for sem, v in wl:
                    engine.wait_ge(sem, v)
                ins = o["fn"](engine)
                if o["dma"]:
                    ins.then_inc(dsem[o["dsem"]], 16)
                elif o["sig"]:
                    ins.then_inc(csem[eng_name], 1)
            for sem, v in end_wl[eng_name]:
                engine.wait_ge(sem, v)

        with nc.Block() as block:
            @block.tensor
            def _(eng):
                run("pe", eng)

            @block.scalar
            def _(eng):
                run("act", eng)

            @block.vector
            def _(eng):
                run("dve", eng)

            @block.gpsimd
            def _(eng):
                run("pool", eng)

            @block.sync
            def _(eng):
                run("sp", eng)
        self._reset()


D = 1024
DFF = 2816
NIN = 5136
EPS = 1e-6


def MM(out, lhsT, rhs, start, stop):
    return lambda e: e.matmul(out, lhsT, rhs, start=bool(start), stop=bool(stop))


def TR(out, in_, ident):
    return lambda e: e.transpose(out, in_, ident)


def ACTF(out, in_, func, bias=None, scale=None, accum_out=None):
    kw = {}
    if bias is not None:
        kw["bias"] = bias
    if scale is not None:
        kw["scale"] = scale
    if accum_out is not None:
        kw["accum_out"] = accum_out
    return lambda e: e.activation(out, in_, func, **kw)


def ACP(out, in_):
    return lambda e: e.copy(out, in_)


def SQRT(out, in_):
    return lambda e: e.sqrt(out, in_)


def CP(out, in_):
    return lambda e: e.tensor_copy(out, in_)


def RECIP(out, in_):
    return lambda e: e.reciprocal(out, in_)


def TTOP(out, a, b, op):
    return lambda e: e.tensor_tensor(out, a, b, op)


def TS(out, in0, s1, s2, op0, op1):
    return lambda e: e.tensor_scalar(out, in0, s1, s2, op0, op1)


def STT(out, in0, scalar, in1, op0, op1):
    return lambda e: e.scalar_tensor_tensor(out, in0, scalar, in1, op0, op1)


def MEMSET(ap, v):
    return lambda e: e.memset(ap, v)


class KB:
    def __init__(self, seqs, debug=False):
        self.seqs = list(seqs)
        self.T = sum(seqs)
        self.debug = debug
        self.nc = bass.Bass("TRN2", target_bir_lowering=False)
        self.es = ExitStack()
        self.S = Sched(self.nc, self.es)
        self.dram = {}
        self.rr = 0

    def din(self, name, shape, dt=F32):
        t = self.nc.dram_tensor(name, list(shape), dt, kind="ExternalInput").ap()
        self.dram[name] = t
        return t

    def dout(self, name, shape, dt=F32):
        t = self.nc.dram_tensor(name, list(shape), dt, kind="ExternalOutput").ap()
        self.dram[name] = t
        return t

    def dscr(self, name, shape, dt=F32):
        kind = "ExternalOutput" if self.debug else "Internal"
        t = self.nc.dram_tensor(name, list(shape), dt, kind=kind).ap()
        self.dram[name] = t
        return t

    def sb(self, st, name, shape, dt):
        self.uid = getattr(self, "uid", 0) + 1
        return st.enter_context(self.nc.sbuf_tensor("%s_%d" % (name, self.uid), list(shape), dt))

    def ps(self, st, name, shape, dt):
        self.uid = getattr(self, "uid", 0) + 1
        return st.enter_context(self.nc.psum_tensor("%s_%d" % (name, self.uid), list(shape), dt))

    def op(self, *a, **k):
        return self.S.op(*a, **k)

    def dma(self, q, out, in_, reads=(), writes=()):
        return self.S.op(q, lambda e: e.dma_start(out=out, in_=in_), reads=reads, writes=writes, dma=True)

    def tap(self, name, ap, reads, dt=F32):
        if not self.debug:
            return
        t = self.dout(name, list(ap.shape), dt)
        self.dma("sp", t, ap, reads=reads)

    def cast_eng(self):
        self.rr += 1
        return ("dve", "pool", "act")[self.rr % 3]

    def copy(self, eng, out, in_, reads, writes):
        if eng == "act":
            return self.op("act", ACP(out, in_), reads=reads, writes=writes)
        return self.op(eng, CP(out, in_), reads=reads, writes=writes)

    def load_weight(self, st, name, src2d, KC, N, stage=None):
        w = self.sb(st, name, [128, KC, N], BF16)
        for kc in range(KC):
            self.dma("pool", w[:, kc, :], src2d[kc * 128:(kc + 1) * 128, :], writes=[(name, kc)])
        return w

    def rmsnorm_tok(self, xb, xkey, wbc, junk, ss1, rs1, out, outkey, skey, wkey=("wbc",)):
        self.op("act", ACTF(junk, xb, AF.Square, accum_out=ss1), reads=[xkey], writes=[("junk",), skey])
        self.op("act", ACTF(rs1, ss1, AF.Sqrt, bias=EPS, scale=1.0 / D), reads=[skey], writes=[skey + ("r",)], strict=True)
        self.op("dve", RECIP(rs1, rs1), reads=[skey + ("r",)], writes=[skey + ("r",)])
        self.op("dve", STT(out, xb, rs1, wbc, ALU.mult, ALU.mult), reads=[xkey, skey + ("r",), wkey], writes=[outkey], strict=True)

    def consts(self, st, ident_f32):
        ident = self.sb(st, "ident", [128, 128], BF16)
        idf = self.sb(st, "identf", [128, 128], F32)
        self.dma("sp", idf[:, :], ident_f32, writes=[("idf",)])
        self.op("dve", CP(ident[:, :], idf[:, :]), reads=[("idf",)], writes=[("ident",)])
        return ident, idf

    def bcast_vec(self, st, name, vec):
        n = vec.shape[0]
        t = self.sb(st, name, [128, n], F32)
        self.dma("sp", t[:, :], vec.partition_broadcast(128), writes=[(name,)])
        return t

    def phase_ffn(self, xin, yout, w_gu, w_dn, nw_ffn, nw_fin, ident_f32, TT=256):
        T = self.T
        NB = TT // 128
        KF = DFF // 128
        with ExitStack() as st:
            ident, _ = self.consts(st, ident_f32)
            wbc = self.bcast_vec(st, "wbc", nw_ffn)
            wbc2 = self.bcast_vec(st, "wbc2", nw_fin)
            Wgu = self.load_weight(st, "Wgu", w_gu, 8, 2 * DFF)
            Wd = self.load_weight(st, "Wd", w_dn, KF, D)
            NX = 3
            xs = self.sb(st, "xs", [128, NX, NB, D], F32)
            h = self.sb(st, "h", [128, 2, D], BF16)
            hT = self.sb(st, "hT", [128, 2, 8, TT], BF16)
            act = self.sb(st, "act", [128, KF, TT], BF16)
            sg = self.sb(st, "sg", [128, 3, TT], F32)
            junk = self.sb(st, "junk", [128, D], BF16)
            ss = self.sb(st, "ss", [128, NX, NB], F32)
            rstd = self.sb(st, "rstd", [128, NX, NB], F32)
            ss2 = self.sb(st, "ss2", [128, NX, NB], F32)
            rstd2 = self.sb(st, "rstd2", [128, NX, NB], F32)
            pT = [self.ps(st, "pT%d" % i, [128, 8, 128], BF16) for i in range(2)]
            pG = [self.ps(st, "pG%d" % i, [128, 512], F32) for i in range(2)]
            pU = [self.ps(st, "pU%d" % i, [128, 512], F32) for i in range(2)]
            pD = [self.ps(st, "pD%d" % i, [128, 512], F32) for i in range(2)]
            ntiles = T // TT
            cT = cG = cD = 0

            def load(i):
                sl = i % NX
                src = xin[i * TT:(i + 1) * TT, :].rearrange("(b p) d -> p b d", p=128)
                self.dma("sp", xs[:, sl], src, writes=[("xs", sl)])

            def normA(i):
                sl = i % NX
                for b in range(NB):
                    hb = (i * NB + b) % 2
                    self.rmsnorm_tok(xs[:, sl, b, :], ("xs", sl), wbc[:, :], junk[:, :], ss[:, sl, b:b + 1],
                                     rstd[:, sl, b:b + 1], h[:, hb, :], ("h", hb), ("ss", sl, b))

            def normB(i):
                nonlocal cT
                hs = i % 2
                for b in range(NB):
                    hb = (i * NB + b) % 2
                    p = pT[cT % 2]
                    pk = ("pT", cT % 2)
                    cT += 1
                    for kc in range(8):
                        self.op("pe", TR(p[:, kc, :], h[:, hb, kc * 128:(kc + 1) * 128], ident[:, :]),
                                reads=[("h", hb), ("ident",)], writes=[pk])
                    self.op("act", ACP(hT[:, hs, :, b * 128:(b + 1) * 128], p[:, :, :]),
                            reads=[pk], writes=[("hT", hs, b)])

            load(0)
            if ntiles > 1:
                load(1)
            normA(0)
            normB(0)
            for i in range(ntiles):
                if i + 2 < ntiles:
                    load(i + 2)
                sl = i % NX
                hs = i % 2
                if i + 1 < ntiles:
                    normA(i + 1)
                hTk = [("hT", hs, b) for b in range(NB)]
                if i == 0:
                    self.tap("t_h", h[:, :, :], [("h", 0), ("h", 1)], BF16)
                    self.tap("t_hT", hT[:, hs], hTk, BF16)
                for j in range(KF):
                    g = pG[cG % 2]
                    u = pU[cG % 2]
                    gk = ("pG", cG % 2)
                    uk = ("pU", cG % 2)
                    sgs = cG % 3
                    cG += 1
                    for kc in range(8):
                        self.op("pe", MM(g[:, 0:TT], Wgu[:, kc, j * 128:(j + 1) * 128], hT[:, hs, kc, :], kc == 0, kc == 7),
                                reads=hTk + [("Wgu", kc)], writes=[gk])
                    for kc in range(8):
                        self.op("pe", MM(u[:, 0:TT], Wgu[:, kc, DFF + j * 128:DFF + (j + 1) * 128], hT[:, hs, kc, :], kc == 0, kc == 7),
                                reads=hTk + [("Wgu", kc)], writes=[uk])
                    self.op("act", ACTF(sg[:, sgs, :], g[:, 0:TT], AF.Silu), reads=[gk], writes=[("sg", sgs)])
                    if i == 0 and j == 0 and self.debug:
                        gcp = self.sb(st, "gcp", [128, 2, TT], F32)
                        self.op("act", ACP(gcp[:, 0, :], g[:, 0:TT]), reads=[gk], writes=[("gcp",)])
                        self.op("dve", CP(gcp[:, 1, :], u[:, 0:TT]), reads=[uk], writes=[("gcp2",)])
                        self.tap("t_g", gcp[:, :, :], [("gcp",), ("gcp2",)])
                        self.tap("t_sg", sg[:, sgs, :], [("sg", sgs)])
                    self.op("dve", TTOP(act[:, j, :], sg[:, sgs, :], u[:, 0:TT], ALU.mult),
                            reads=[uk, ("sg", sgs)], writes=[("act", j)])
                if i == 0:
                    self.tap("t_act", act[:, :, :], [("act", j) for j in range(KF)], BF16)
                if i + 1 < ntiles:
                    normB(i + 1)
                for b in range(NB):
                    for nh in range(2):
                        pd = pD[cD % 2]
                        dk = ("pD", cD % 2)
                        cD += 1
                        for kc in range(KF):
                            self.op("pe", MM(pd[:, :], act[:, kc, b * 128:(b + 1) * 128], Wd[:, kc, nh * 512:(nh + 1) * 512],
                                             kc == 0, kc == KF - 1),
                                    reads=[("act", kc), ("Wd", kc)], writes=[dk])
                        xsl = xs[:, sl, b, nh * 512:(nh + 1) * 512]
                        self.op("dve", TTOP(xsl, xsl, pd[:, :], ALU.add), reads=[dk, ("xs", sl)], writes=[("xs", sl)])
                    self.rmsnorm_tok(xs[:, sl, b, :], ("xs", sl), wbc2[:, :], junk[:, :], ss2[:, sl, b:b + 1],
                                     rstd2[:, sl, b:b + 1], xs[:, sl, b, :], ("xs", sl), ("fs", sl, b), wkey=("wbc2",))
                dst = yout[i * TT:(i + 1) * TT, :].rearrange("(b p) d -> p b d", p=128)
                self.dma("pool", dst, xs[:, sl], reads=[("xs", sl)])
            self.S.flush()

    def phase_in(self, x, w_in, nw_mix, ln_w, ln_b, sg_w, sg_b, w_up_b, a_log, dt_bias, ident_f32, scr, TT=512):
        T = self.T
        NB = TT // 128
        qkvT, bg, gdT, gaT, sbT = scr["qkvT"], scr["bg"], scr["gdT"], scr["gaT"], scr["sbT"]
        with ExitStack() as st:
            ident, _ = self.consts(st, ident_f32)
            wbc = self.bcast_vec(st, "wbc", nw_mix)
            Win = self.load_weight(st, "Win", w_in, 8, NIN)
            Wub = self.load_weight(st, "Wub", w_up_b, 4, D)
            wsn = self.sb(st, "wsn", [128, 4, 128], BF16)
            self.dma("pool", wsn[:, :, :], sg_w.rearrange("g t s -> t g s"), writes=[("wsn",)])
            WsT = self.sb(st, "WsT", [128, 4, 128], BF16)
            bsrow = self.sb(st, "bsrow", [1, 512], BF16)
            self.dma("pool", bsrow[:, :], sg_b.rearrange("(o g) t -> o (g t)", o=1), writes=[("bsrow",)])
            ones1 = self.sb(st, "ones1", [1, 128], BF16)
            self.op("dve", MEMSET(ones1[:, :], 1.0), writes=[("ones1",)])
            onesb = self.sb(st, "onesb", [128, 128], BF16)
            self.op("dve", MEMSET(onesb[:, :], 1.0), writes=[("onesb",)])
            lnw = self.sb(st, "lnw", [128, 4], F32)
            lnb = self.sb(st, "lnb", [128, 4], F32)
            self.op("sp", lambda e: e.dma_start(out=lnw[:, :], in_=ln_w.rearrange("(g p) -> p g", p=128), allow_slow_non_contiguous=True),
                    writes=[("lnw",)], dma=True)
            self.op("sp", lambda e: e.dma_start(out=lnb[:, :], in_=ln_b.rearrange("(g p) -> p g", p=128), allow_slow_non_contiguous=True),
                    writes=[("lnb",)], dma=True)
            alc = self.sb(st, "alc", [8, 1], F32)
            dtb = self.sb(st, "dtb", [8, 1], F32)
            negA = self.sb(st, "negA", [8, 1], F32)
            self.dma("sp", alc[:, :], a_log.rearrange("(p o) -> p o", o=1), writes=[("alc",)])
            self.dma("sp", dtb[:, :], dt_bias.rearrange("(p o) -> p o", o=1), writes=[("dtb",)])
            self.op("act", ACTF(negA[:, :], alc[:, :], AF.Exp), reads=[("alc",)], writes=[("negA",)])
            self.op("dve", TS(negA[:, :], negA[:, :], -1.0, None, ALU.mult, ALU.bypass) if False else
                    (lambda e: e.tensor_scalar_mul(negA[:, :], negA[:, :], -1.0)), reads=[("negA",)], writes=[("negA",)])
            xs = self.sb(st, "xs", [128, NB, D], F32)
            h = self.sb(st, "h", [128, 4, D], BF16)
            hT = self.sb(st, "hT", [128, 2, 8, TT], BF16)
            junk = self.sb(st, "junk", [128, D], BF16)
            ss = self.sb(st, "ss", [128, NB], F32)
            rstd = self.sb(st, "rstd", [128, NB], F32)
            ost = self.sb(st, "ost", [128, 4, TT], BF16)
            bst = self.sb(st, "bst", [8, 2, TT], F32)
            est = self.sb(st, "est", [8, TT], F32)
            uT = self.sb(st, "uT", [128, 4, TT], BF16)
            vT = self.sb(st, "vT", [128, 4, TT], F32)
            vb = self.sb(st, "vb", [128, 4, TT], BF16)
            vq = self.sb(st, "vq", [128, 4, TT], BF16)
            gbT = self.sb(st, "gbT", [128, 8, TT], BF16)
            vn = self.sb(st, "vn", [128, 4, TT], BF16)
            vtmp = self.sb(st, "vtmp", [128, 2, TT], F32)
            vtok = self.sb(st, "vtok", [128, 4, NB, 128], BF16)
            ubT = self.sb(st, "ubT", [128, 4, TT], BF16)
            mu = self.sb(st, "mu", [128, TT], F32)
            msq = self.sb(st, "msq", [128, TT], F32)
            lrs = self.sb(st, "lrs", [128, TT], F32)
            pT = [self.ps(st, "pT%d" % i, [128, 8, 128], BF16) for i in range(2)]
            pP = [self.ps(st, "pP%d" % i, [128, 512], F32) for i in range(4)]
            pS = [self.ps(st, "pS%d" % i, [128, 512], F32) for i in range(2)]
            for g in range(4):
                self.op("pe", TR(pT[0][:, g, :], wsn[:, g, :], ident[:, :]), reads=[("wsn",), ("ident",)], writes=[("pT", 0)])
            self.op("act", ACP(WsT[:, :, :], pT[0][:, 0:4, :]), reads=[("pT", 0)], writes=[("WsT",)])
            ntiles = T // TT
            cnt = dict(T=0, P=0, O=0, S=0)

            def load(i):
                for b in range(NB):
                    self.dma("sp", xs[:, b, :], x[i * TT + b * 128:i * TT + (b + 1) * 128, :], writes=[("xs", b)])

            def proj(i, hs, c0, M):
                b = cnt["P"] % 4
                cnt["P"] += 1
                p = pP[b]
                for kc in range(8):
                    self.op("pe", MM(p[0:M, 0:TT], Win[:, kc, c0:c0 + M], hT[:, hs, kc, :], kc == 0, kc == 7),
                            reads=[("hT", hs, bb) for bb in range(NB)] + [("Win", kc)], writes=[("pP", b)])
                return p[0:M, 0:TT], ("pP", b)

            def store(dst, src_ap, key):
                self.dma("sp", dst, src_ap, reads=[key])

            def ostage():
                o = cnt["O"] % 4
                cnt["O"] += 1
                return ost[:, o, :], ("ost", o)

            def normA(i):
                for b in range(NB):
                    self.rmsnorm_tok(xs[:, b, :], ("xs", b), wbc[:, :], junk[:, :], ss[:, b:b + 1],
                                     rstd[:, b:b + 1], h[:, b, :], ("h", b), ("ss", b))
                if i + 1 < ntiles:
                    load(i + 1)

            def normB(i):
                hs = i % 2
                for b in range(NB):
                    p = pT[cnt["T"] % 2]
                    pk = ("pT", cnt["T"] % 2)
                    cnt["T"] += 1
                    for kc in range(8):
                        self.op("pe", TR(p[:, kc, :], h[:, b, kc * 128:(kc + 1) * 128], ident[:, :]),
                                reads=[("h", b), ("ident",)], writes=[pk])
                    self.op("act", ACP(hT[:, hs, :, b * 128:(b + 1) * 128], p[:, :, :]), reads=[pk], writes=[("hT", hs, b)])

            load(0)
            normA(0)
            normB(0)
            for i in range(ntiles):
                hs = i % 2
                t0 = i * TT
                if i + 1 < ntiles:
                    normA(i + 1)
                for j in range(4):
                    p, pk = proj(i, hs, 2576 + j * 128, 128)
                    self.op("act", ACTF(vT[:, j, :], p, AF.Gelu_apprx_tanh), reads=[pk], writes=[("vT", j)])
                    self.op("pool", CP(vb[:, j, :], vT[:, j, :]), reads=[("vT", j)], writes=[("vb", j)])
                    self.op("pool", TTOP(vq[:, j, :], vT[:, j, :], vT[:, j, :], ALU.mult), reads=[("vT", j)], writes=[("vq", j)])
                for j in range(4):
                    p, pk = proj(i, hs, 2064 + j * 128, 128)
                    self.op("act", ACTF(uT[:, j, :], p, AF.Gelu_apprx_tanh), reads=[pk], writes=[("uT", j)])
                for c in range(12):
                    p, pk = proj(i, hs, c * 128, 128)
                    o, ok = ostage()
                    if c % 2 == 0:
                        self.op("act", ACP(o, p), reads=[pk], writes=[ok])
                    else:
                        self.op("dve", CP(o, p), reads=[pk], writes=[ok])
                    store(qkvT[c * 128:(c + 1) * 128, t0:t0 + TT], o, ok)
                s1, s2 = pS[0], pS[1]
                for j in range(4):
                    self.op("pe", MM(s1[:, 0:TT], onesb[:, :], vb[:, j, :], j == 0, j == 3), reads=[("vb", j), ("onesb",)], writes=[("pS", 0)])
                for j in range(4):
                    self.op("pe", MM(s2[:, 0:TT], onesb[:, :], vq[:, j, :], j == 0, j == 3), reads=[("vq", j), ("onesb",)], writes=[("pS", 1)])
                self.op("act", ACTF(mu[:, :], s1[:, 0:TT], AF.Copy, scale=1.0 / 512), reads=[("pS", 0)], writes=[("mu",)])
                self.op("pool", TTOP(msq[:, :], mu[:, :], mu[:, :], ALU.mult), reads=[("mu",)], writes=[("msq",)])
                self.op("dve", STT(lrs[:, :], s2[:, 0:TT], 1.0 / 512, msq[:, :], ALU.mult, ALU.subtract),
                        reads=[("pS", 1), ("msq",)], writes=[("lrs",)])
                self.op("act", ACTF(lrs[:, :], lrs[:, :], AF.Ln, bias=EPS, scale=1.0), reads=[("lrs",)], writes=[("lrs",)])
                self.op("act", ACTF(lrs[:, :], lrs[:, :], AF.Exp, scale=-0.5), reads=[("lrs",)], writes=[("lrs",)])
                p, pk = proj(i, hs, 1536, 8)
                self.op("act", ACTF(bst[:, 0, :], p, AF.Sigmoid), reads=[pk], writes=[("bst", 0)])
                store(bg[0:8, t0:t0 + TT], bst[:, 0, :], ("bst", 0))
                p, pk = proj(i, hs, 1544, 8)
                self.op("act", ACTF(est[:, :], p, AF.Exp, bias=dtb[:, 0:1], scale=1.0), reads=[pk, ("dtb",)], writes=[("est",)])
                self.op("act", ACTF(est[:, :], est[:, :], AF.Ln, bias=1.0, scale=1.0), reads=[("est",)], writes=[("est",)])
                self.op("dve", (lambda o_=bst[:, 1, :], i_=est[:, :], s_=negA[:, 0:1]: lambda e: e.tensor_scalar_mul(o_, i_, s_))(),
                        reads=[("est",), ("negA",)], writes=[("bst", 1)])
                store(bg[8:16, t0:t0 + TT], bst[:, 1, :], ("bst", 1))
                for j in range(4):
                    p, pk = proj(i, hs, 1552 + j * 128, 128)
                    o, ok = ostage()
                    self.op("act", ACTF(o, p, AF.Silu), reads=[pk], writes=[ok])
                    store(gdT[j * 128:(j + 1) * 128, t0:t0 + TT], o, ok)
                for j in range(8):
                    p, pk = proj(i, hs, 3088 + j * 128, 128)
                    o, ok = ostage()
                    self.op("act", ACTF(o, p, AF.Sigmoid), reads=[pk], writes=[ok])
                    store(gaT[j * 128:(j + 1) * 128, t0:t0 + TT], o, ok)
                for j in range(4):
                    tb = j % 2
                    self.op("pool", TTOP(vtmp[:, tb, :], vT[:, j, :], mu[:, :], ALU.subtract), reads=[("vT", j), ("mu",)], writes=[("vtmp", tb)])
                    self.op("dve", TTOP(vtmp[:, tb, :], vtmp[:, tb, :], lrs[:, :], ALU.mult), reads=[("vtmp", tb), ("lrs",)], writes=[("vtmp", tb)])
                    self.op("dve", TS(vn[:, j, :], vtmp[:, tb, :], lnw[:, j:j + 1], lnb[:, j:j + 1], ALU.mult, ALU.add),
                            reads=[("vtmp", tb), ("lnw",), ("lnb",)], writes=[("vn", j)])
                for j in range(8):
                    p, pk = proj(i, hs, 4112 + j * 128, 128)
                    self.op("act", ACTF(gbT[:, j, :], p, AF.Sigmoid), reads=[pk], writes=[("gbT", j)])
                if i + 1 < ntiles:
                    normB(i + 1)
                for b in range(NB):
                    p = pT[cnt["T"] % 2]
                    pk = ("pT", cnt["T"] % 2)
                    cnt["T"] += 1
                    for j in range(4):
                        self.op("pe", TR(p[:, j, :], vn[:, j, b * 128:(b + 1) * 128], ident[:, :]), reads=[("vn", j), ("ident",)], writes=[pk])
                    self.op("act", ACP(vtok[:, :, b, :], p[:, 0:4, :]), reads=[pk], writes=[("vtok", b)])
                for j in range(4):
                    sb_ = cnt["S"] % 2
                    cnt["S"] += 1
                    pm = pS[sb_]
                    for b in range(NB):
                        self.op("pe", MM(pm[:, b * 128:(b + 1) * 128], vtok[:, j, b, :], WsT[:, j, :], True, False),
                                reads=[("vtok", b), ("WsT",)], writes=[("pS", sb_)])
                        self.op("pe", MM(pm[:, b * 128:(b + 1) * 128], ones1[0:1, :], bsrow[0:1, j * 128:(j + 1) * 128], False, True),
                                reads=[("ones1",), ("bsrow",)], writes=[("pS", sb_)])
                    self.op("dve", TTOP(ubT[:, j, :], uT[:, j, :], pm[:, 0:TT], ALU.mult), reads=[("pS", sb_), ("uT", j)], writes=[("ubT", j)])
                for j in range(8):
                    b = cnt["P"] % 4
                    cnt["P"] += 1
                    p = pP[b]
                    for kc in range(4):
                        self.op("pe", MM(p[:, 0:TT], Wub[:, kc, j * 128:(j + 1) * 128], ubT[:, kc, :], kc == 0, kc == 3),
                                reads=[("ubT", kc), ("Wub", kc)], writes=[("pP", b)])
                    o, ok = ostage()
                    self.op("dve", TTOP(o, gbT[:, j, :], p[:, 0:TT], ALU.mult), reads=[("pP", b), ("gbT", j)], writes=[ok])
                    store(sbT[j * 128:(j + 1) * 128, t0:t0 + TT], o, ok)
            self.S.flush()

    def phase_prep(self, scr, conv_w, cst, pre):
        qkvT = scr["qkvT"]
        qn, kn, ktok, vtok = pre["qn"], pre["kn"], pre["ktok"], pre["vtok"]
        GT = 512
        SCALE = 128 ** -0.5
        with ExitStack() as st:
            C = self.sb(st, "C", [128, 128], F32)
            self.dma("sp", C[:, :], cst[0], writes=[("C",)])
            ident = self.sb(st, "ident", [128, 128], BF16)
            self.op("dve", CP(ident[:, :], C[:, :]), reads=[("C",)], writes=[("ident",)])
            onesb = self.sb(st, "onesb", [128, 128], BF16)
            self.op("dve", MEMSET(onesb[:, :], 1.0), writes=[("onesb",)])
            cw = self.sb(st, "cw", [128, 5, 12], F32)
            self.op("sp", lambda e: e.dma_start(out=cw[:, :, :], in_=conv_w.rearrange("j (c p) -> p j c", p=128), allow_slow_non_contiguous=True),
                    writes=[("cw",)], dma=True)
            Cd = self.sb(st, "Cd", [128, 12, 5, 128], BF16)
            for cc in range(12):
                for j in range(5):
                    self.op("dve", (lambda o_=Cd[:, cc, j, :], s_=cw[:, j, cc:cc + 1]: lambda e: e.tensor_scalar_mul(o_, C[:, :], s_))(),
                            reads=[("C",), ("cw",)], writes=[("Cd", cc)])
            NR = 4
            raw = self.sb(st, "raw", [128, NR, GT + 4], BF16)
            sil = self.sb(st, "sil", [128, 4, GT], F32)
            sq = self.sb(st, "sq", [128, 3, GT], BF16)
            rsn = self.sb(st, "rsn", [128, 3, GT], F32)
            fm = self.sb(st, "fm", [128, 8, GT], BF16)
            VT = self.sb(st, "VT", [128, 4, GT], BF16)
            tk = self.sb(st, "tk", [128, 2, 4, 512], BF16)
            pcv = [self.ps(st, "pcv%d" % i, [128, 512], F32) for i in range(3)]
            pss = [self.ps(st, "pss%d" % i, [128, 512], F32) for i in range(2)]
            ptr = [self.ps(st, "ptr%d" % i, [128, 1024], BF16) for i in range(2)]
            op = self.op
            cn = dict(r=0, c=0, s=0, f=0, t=0, x=0)
            its = []
            t_base = 0
            for L in self.seqs:
                ng = L // GT
                for gi in range(ng):
                    for qi in (1, 2, 0):
                        for h in range(4):
                            its.append((t_base + gi * GT, gi, ng, qi, h))
                t_base += L

            def load(n):
                t0, gi, ng, qi, h = its[n]
                cc = qi * 4 + h
                rs_ = n % NR
                rk = ("raw", rs_)
                lo = 2 if gi == 0 else 0
                hi = GT + 2 if gi == ng - 1 else GT + 4
                if gi == 0 or gi == ng - 1:
                    op("pool", MEMSET(raw[:, rs_, :], 0.0), writes=[rk])
                self.dma("sp", raw[:, rs_, lo:hi], qkvT[cc * 128:(cc + 1) * 128, t0 - 2 + lo:t0 - 2 + hi], writes=[rk])

            deferred = []

            stA = {}
            late = []

            def computeA(n):
                t0, gi, ng, qi, h = its[n]
                cc = qi * 4 + h
                rs_ = n % NR
                rk = ("raw", rs_)
                pb_ = cn["c"] % 3
                cn["c"] += 1
                pf, pfk = pcv[pb_], ("pcv", pb_)
                for j in range(5):
                    op("pe", MM(pf[:, :], Cd[:, cc, j, :], raw[:, rs_, j:j + GT], j == 0, j == 4), reads=[rk, ("Cd", cc)], writes=[pfk])
                if qi == 2:
                    op("act", ACTF(VT[:, h, :], pf[:, :], AF.Silu), reads=[pfk], writes=[("VT", h)])
                else:
                    op("act", ACTF(sil[:, h, :], pf[:, :], AF.Silu), reads=[pfk], writes=[("sil", h)])

            def computeB(n):
                t0, gi, ng, qi, h = its[n]
                if qi == 2:
                    src, srck = VT[:, h, :], ("VT", h)
                else:
                    ss_ = h % 3
                    sk = ("sil", h)
                    op("pool", TTOP(sq[:, ss_, :], sil[:, h, :], sil[:, h, :], ALU.mult), reads=[sk], writes=[("sq", ss_)])
                    ps_ = h % 2
                    p2, p2k = pss[ps_], ("pss", ps_)
                    op("pe", MM(p2[:, :], onesb[:, :], sq[:, ss_, :], True, True), reads=[("sq", ss_), ("onesb",)], writes=[p2k])
                    op("act", ACTF(rsn[:, ss_, :], p2[:, :], AF.Ln, bias=EPS, scale=1.0), reads=[p2k], writes=[("rsn", ss_)])
                    op("act", ACTF(rsn[:, ss_, :], rsn[:, ss_, :], AF.Exp, scale=-0.5), reads=[("rsn", ss_)], writes=[("rsn", ss_)])
                    fs = cn["f"] % 8
                    cn["f"] += 1
                    if qi == 0:
                        op("dve", STT(fm[:, fs, :], sil[:, h, :], SCALE, rsn[:, ss_, :], ALU.mult, ALU.mult),
                           reads=[sk, ("rsn", ss_)], writes=[("fm", fs)])
                        deferred.append((qn[h * 128:(h + 1) * 128, t0:t0 + GT], fm[:, fs, :], [("fm", fs)]))
                        return
                    op("dve", TTOP(fm[:, fs, :], sil[:, h, :], rsn[:, ss_, :], ALU.mult), reads=[sk, ("rsn", ss_)], writes=[("fm", fs)])
                    deferred.append((kn[h * 128:(h + 1) * 128, t0:t0 + GT], fm[:, fs, :], [("fm", fs)]))
                    src, srck = fm[:, fs, :], ("fm", fs)
                late.append((n, src, srck))

            def computeB2(n, src, srck):
                t0, gi, ng, qi, h = its[n]
                kind = 0 if qi == 1 else 1
                tb = cn["t"] % 2
                cn["t"] += 1
                pt, ptk = ptr[tb], ("ptr", tb)
                for c in range(4):
                    op("pe", TR(pt[:, c * 128:(c + 1) * 128], src[:, c * 128:(c + 1) * 128], ident[:, :]), reads=[srck, ("ident",)], writes=[ptk])
                evk = ("tk", kind, h)
                op("dve", CP(tk[:, kind, :, h * 128:(h + 1) * 128], pt[:, 0:512].rearrange("p (c d) -> p c d", c=4)),
                   reads=[ptk], writes=[evk])
                if h == 3:
                    dst = (ktok, vtok)[kind]
                    deferred.append((dst[t0:t0 + GT, :].rearrange("(c p) f -> p c f", p=128), tk[:, kind, :, :],
                                     [("tk", kind, hh) for hh in range(4)]))

            nit = len(its)
            load(0)
            if nit > 1:
                load(1)
            pending = []
            late_prev = []
            for blk in range(nit // 4):
                for n in range(blk * 4, blk * 4 + 4):
                    if n + 2 < nit:
                        load(n + 2)
                    computeA(n)
                for (n_, src_, srck_) in late_prev:
                    computeB2(n_, src_, srck_)
                for n in range(blk * 4, blk * 4 + 4):
                    computeB(n)
                late_prev = list(late)
                del late[:]
                for (dst_, src_, rd_) in pending:
                    self.dma("sp", dst_, src_, reads=rd_)
                pending = list(deferred)
                del deferred[:]
            for (n_, src_, srck_) in late_prev:
                computeB2(n_, src_, srck_)
            pending += list(deferred)
            for (dst_, src_, rd_) in pending:
                self.dma("sp", dst_, src_, reads=rd_)
            self.S.flush()

    def phase_dn(self, scr, pre, cst, of, ob, offset=0, ndummy=1):
        bg = scr["bg"]
        qn, kn, ktok, vtok = pre["qn"], pre["kn"], pre["ktok"], pre["vtok"]
        GT = 512
        IDX = dict(PA=(1, 3), PB=(2, 4), BD=5, M32=(8, 6), M64=(9, 7), TRI=(10, 11))
        SCALE = 128 ** -0.5
        with ExitStack() as st:
            C = self.sb(st, "C", [128, 12, 128], F32)
            self.dma("sp", C[:, :, :], cst.rearrange("n p f -> p n f"), writes=[("C",)])
            identf = C[:, 0, :]
            Cb = self.sb(st, "Cb", [128, 12, 128], BF16)
            self.op("dve", CP(Cb[:, :, :], C[:, :, :]), reads=[("C",)], writes=[("Cb",)])
            ident = Cb[:, 0, :]
            onesb = self.sb(st, "onesb", [128, 128], BF16)
            self.op("dve", MEMSET(onesb[:, :], 1.0), writes=[("onesb",)])
            onesf = self.sb(st, "onesf", [128, 128], F32)
            self.op("dve", MEMSET(onesf[:, :], 1.0), writes=[("onesf",)])
            def bc_col(a):
                return a.unsqueeze(2).to_broadcast([128, 4, 128])

            def bc_mat(a):
                return a.unsqueeze(1).to_broadcast([128, 4, 128])

            class Stream:
                pass

            streams = []
            for d in (0, 1):
                Z = Stream()
                Z.d = d
                n_ = "d%d_" % d
                Z.bgrow = self.sb(st, n_ + "bgrow", [16, GT], F32)
                Z.QTs = self.sb(st, n_ + "QT", [128, 2, 4, GT], BF16)
                Z.KTs = self.sb(st, n_ + "KT", [128, 2, 4, GT], BF16)
                Z.Ktoks = self.sb(st, n_ + "Ktok", [128, 2, 4, 512], BF16)
                Z.Vtoks = self.sb(st, n_ + "Vtok", [128, 2, 4, 512], BF16)
                Z.btoks = self.sb(st, n_ + "btoks", [128, 2, 4, 16], F32)
                Z.gslot = 0
                Z.cols = self.sb(st, n_ + "cols", [128, 2, 8, 4], F32)
                Z.ostg = self.sb(st, n_ + "ostg", [128, 2, 512], F32)
                Z.S = self.sb(st, n_ + "S", [128, 4, 128], F32)
                Z.Sd = self.sb(st, n_ + "Sd", [128, 4, 128], F32)
                Z.Sb = self.sb(st, n_ + "Sb", [128, 4, 128], BF16)
                Z.bufs = {}
                PAR2 = ("U", "Wt", "Aq", "qg", "kg")
                for nm in ("Gs", "T0", "Y1", "Y2", "E1", "E2", "Eg", "E1n", "U"):
                    Z.bufs[nm] = self.sb(st, n_ + nm, [128, 2 if nm in PAR2 else 1, 4, 128], F32)
                for nm in ("N", "N32", "N64", "Nd", "P0", "P1", "Q0", "Q1", "R", "Yb", "Xb", "tm", "vb", "kbg", "kg", "qg", "Aq", "Wt", "vn"):
                    Z.bufs[nm] = self.sb(st, n_ + nm, [128, 2 if nm in PAR2 else 1, 4, 128], BF16)
                Z.pA = self.ps(st, n_ + "pA", [128, 512], F32)
                Z.pB = self.ps(st, n_ + "pB", [128, 512], F32)
                Z.pC = self.ps(st, n_ + "pC", [128, 512], F32)
                Z.pT = Z.pC[:, :].bitcast(BF16)
                Z.PA = C[:, IDX["PA"][d], :]
                Z.PB = C[:, IDX["PB"][d], :]
                Z.BD = Cb[:, IDX["BD"], :]
                Z.M32 = Cb[:, IDX["M32"][1 - d], :]
                Z.M64 = Cb[:, IDX["M64"][1 - d], :]
                Z.TRI = C[:, IDX["TRI"][d], :]
                Z.oscr = (of, ob)[d]
                streams.append(Z)
            op = self.op
            pDum = self.ps(st, "pDum", [128, 512], F32)
            dsrc = self.sb(st, "dsrc", [128, 512], BF16)
            self.op("dve", MEMSET(dsrc[:, :], 0.001), writes=[("dsrc",)])

            DUMN = 128

            DUMEVERY = 2
            dcount = [0]

            def dummies(n=None):
                dcount[0] += 1
                if dcount[0] % DUMEVERY:
                    return
                for _ in range(ndummy if n is None else n):
                    op("pe", MM(pDum[:, 0:DUMN], onesb[:, :], dsrc[:, 0:DUMN], True, True), reads=[("dsrc",), ("onesb",)], writes=[("pDum",)])

            def K(Z, name, par=0):
                return ("dn", Z.d, name, par)

            def Bf(Z, name, par=0):
                return Z.bufs[name][:, par], K(Z, name, par)

            def v4(ps):
                return ps[:, :].rearrange("p (h c) -> p h c", h=4)

            def prologue(Z, gi, ng, t0):
                d = Z.d
                Z.gslot ^= 1
                gs = Z.gslot
                pk = lambda nm: ("dn", d, "ps" + nm)
                Z.QT = Z.QTs[:, gs]
                Z.KT = Z.KTs[:, gs]
                Z.Ktok = Z.Ktoks[:, gs].rearrange("p c (h d) -> p h c d", h=4)
                Z.Vtok = Z.Vtoks[:, gs].rearrange("p c (h d) -> p h c d", h=4)
                Z.btok = Z.btoks[:, gs]
                Z.gk = gs
                self.dma("sp", Z.QT, qn[:, t0:t0 + GT].rearrange("(h p) t -> p h t", p=128), writes=[K(Z, "QT", gs)])
                self.dma("sp", Z.KT, kn[:, t0:t0 + GT].rearrange("(h p) t -> p h t", p=128), writes=[K(Z, "KT", gs)])
                self.dma("sp", Z.Ktoks[:, gs], ktok[t0:t0 + GT, :].rearrange("(c p) f -> p c f", p=128), writes=[K(Z, "Ktok", gs)])
                self.dma("sp", Z.Vtoks[:, gs], vtok[t0:t0 + GT, :].rearrange("(c p) f -> p c f", p=128), writes=[K(Z, "Vtok", gs)])
                self.dma("sp", Z.bgrow[:, :], bg[:, t0:t0 + GT], writes=[K(Z, "bgrow")])
                for c in range(4):
                    op("pe", TR(Z.pC[:, c * 16:(c + 1) * 16], Z.bgrow[0:16, c * 128:(c + 1) * 128], C[0:16, 0, 0:16]),
                       reads=[K(Z, "bgrow"), ("C",)], writes=[pk("C")])
                op("act", ACP(Z.btok, Z.pC[:, 0:64].rearrange("p (a b) -> p a b", a=4)), reads=[pk("C")], writes=[K(Z, "btok", gs)])

            KGC, KGL, KNB, KEG, KBGE, KEGL, KKGC = range(7)

            def unit(Z, c):
                d = Z.d
                par = c % 2
                QT_, KT_, Ktok_, Vtok_, btok_, gk_ = Z.QT, Z.KT, Z.Ktok, Z.Vtok, Z.btok, Z.gk
                pk = lambda nm: ("dn", d, "ps" + nm)
                cs = slice(c * 128, (c + 1) * 128)
                gt4 = btok_[:, c, 8 + 4 * d:12 + 4 * d]
                be4 = btok_[:, c, 4 * d:4 * d + 4]
                cl = Z.cols[:, par]
                ck = K(Z, "cols", par)
                bk = K(Z, "btok", gk_)
                A4, B4, C4 = v4(Z.pA), v4(Z.pB), v4(Z.pC)
                T4 = Z.pT[:, 0:512].rearrange("p (h c) -> p h c", h=4)
                op("pe", MM(Z.pC[:, 0:4], Z.TRI, gt4, True, True), reads=[bk, ("C",)], writes=[pk("C")])
                op("pe", MM(Z.pC[:, 4:8], onesf[:, :], gt4, True, True), reads=[bk, ("onesf",)], writes=[pk("C")])
                op("act", ACP(cl[:, 0:2, :], Z.pC[:, 0:8].rearrange("p (a b) -> p a b", a=2)), reads=[pk("C")], writes=[ck])
                for h in range(4):
                    op("pe", MM(A4[:, h, :], gt4[:, h:h + 1].to_broadcast([128, 128]), Z.TRI, True, True), reads=[bk, ("C",)], writes=[pk("A")])
                dummies()
                yield
                op("dve", (lambda o_=cl[:, KNB, :], i_=be4: lambda e: e.tensor_scalar_mul(o_, i_, -1.0))(), reads=[bk], writes=[ck])
                op("act", ACTF(cl[:, KEG, :], cl[:, KGC, :], AF.Exp), reads=[ck], writes=[ck])
                op("act", ACTF(cl[:, KEGL, :], cl[:, KGL, :], AF.Exp), reads=[ck], writes=[ck])
                op("dve", TTOP(cl[:, KKGC, :], cl[:, KGL, :], cl[:, KGC, :], ALU.subtract), reads=[ck], writes=[ck])
                op("act", ACTF(cl[:, KKGC, :], cl[:, KKGC, :], AF.Exp), reads=[ck], writes=[ck])
                op("dve", TTOP(cl[:, KBGE, :], cl[:, KEG, :], be4, ALU.mult), reads=[ck, bk], writes=[ck])
                Gs, Gsk = Bf(Z, "Gs")
                op("act", ACP(Gs, A4), reads=[pk("A")], writes=[Gsk])
                op("pool", TTOP(Z.Sd[:, :, :], Z.S[:, :, :], bc_col(cl[:, KEGL, :]), ALU.mult), reads=[K(Z, "S"), ck], writes=[K(Z, "Sd")])
                dummies()
                yield
                T0, T0k = Bf(Z, "T0")
                Y1, Y1k = Bf(Z, "Y1")
                Y2, Y2k = Bf(Z, "Y2")
                Eg, Egk = Bf(Z, "Eg")
                E1, E1k = Bf(Z, "E1")
                E2, E2k = Bf(Z, "E2")
                op("act", ACTF(Eg, Gs, AF.Exp), reads=[Gsk], writes=[Egk])
                op("dve", TTOP(T0, Gs, bc_col(cl[:, KGC, :]), ALU.subtract), reads=[Gsk, ck], writes=[T0k])
                dummies()
                yield
                op("dve", TTOP(Y1, T0, bc_mat(Z.PA), ALU.add), reads=[T0k, ("C",)], writes=[Y1k])
                op("pool", TTOP(Y2, T0, bc_mat(Z.PB), ALU.add), reads=[T0k, ("C",)], writes=[Y2k])
                for h in range(4):
                    op("pe", MM(B4[:, h, :], KT_[:, h, cs], KT_[:, h, cs], True, True), reads=[K(Z, "KT", gk_)], writes=[pk("B")])
                for h in range(4):
                    op("pe", MM(C4[:, h, :], KT_[:, h, cs], QT_[:, h, cs], True, True), reads=[K(Z, "KT", gk_), K(Z, "QT", gk_)], writes=[pk("C")])
                dummies()
                yield
                op("act", ACTF(E1, Y1, AF.Exp, scale=-1.0), reads=[Y1k], writes=[E1k])
                op("act", ACTF(E2, Y2, AF.Exp), reads=[Y2k], writes=[E2k])
                qg, qgk = Bf(Z, "qg", par)
                op("pool", TTOP(qg, QT_[:, :, cs], Eg, ALU.mult), reads=[K(Z, "QT", gk_), Egk], writes=[qgk])
                dummies()
                yield
                E1n, E1nk = Bf(Z, "E1n")
                op("dve", TTOP(E1n, E1, bc_col(cl[:, KNB, :]), ALU.mult), reads=[E1k, ck], writes=[E1nk])
                N, Nk = Bf(Z, "N")
                op("dve", TTOP(N, B4, E1n, ALU.mult), reads=[pk("B"), E1nk], writes=[Nk])
                dummies()
                yield
                Nd, Ndk = Bf(Z, "Nd")
                op("dve", TTOP(Nd, N, bc_mat(Z.BD), ALU.mult), reads=[Nk, ("Cb",)], writes=[Ndk])
                N32, N32k = Bf(Z, "N32")
                N64, N64k = Bf(Z, "N64")
                op("pool", TTOP(N32, N, bc_mat(Z.M32), ALU.mult), reads=[Nk, ("Cb",)], writes=[N32k])
                op("pool", TTOP(N64, N, bc_mat(Z.M64), ALU.mult), reads=[Nk, ("Cb",)], writes=[N64k])
                Aq, Aqk = Bf(Z, "Aq", par)
                op("dve", TTOP(Aq, C4, E2, ALU.mult), reads=[pk("C"), E2k], writes=[Aqk])
                vb, vbk = Bf(Z, "vb")
                kbg, kbgk = Bf(Z, "kbg")
                kg, kgk = Bf(Z, "kg", par)
                op("pool", TTOP(vb, Vtok_[:, :, c, :], bc_col(be4), ALU.mult), reads=[K(Z, "Vtok", gk_), bk], writes=[vbk])
                op("pool", TTOP(kbg, Ktok_[:, :, c, :], bc_col(cl[:, KBGE, :]), ALU.mult), reads=[K(Z, "Ktok", gk_), ck], writes=[kbgk])
                op("pool", TTOP(kg, Ktok_[:, :, c, :], bc_col(cl[:, KKGC, :]), ALU.mult), reads=[K(Z, "Ktok", gk_), ck], writes=[kgk])
                dummies()
                yield
                for h in range(4):
                    op("pe", TR(T4[:, h, :], Nd[:, h, :], ident), reads=[Ndk, ("Cb",)], writes=[pk("C")])
                dummies()
                yield
                Q0, Q0k = Bf(Z, "Q0")
                op("act", ACP(Q0, T4), reads=[pk("C")], writes=[Q0k])
                dummies()
                yield
                R, Rk = Bf(Z, "R")
                op("dve", TTOP(R, Q0, bc_mat(ident), ALU.add), reads=[Q0k, ("Cb",)], writes=[Rk])
                Pc, Pck = Nd, Ndk
                Qc, Qck = Q0, Q0k
                for j in range(1, 5):
                    Pn, Pnk = Bf(Z, "P%d" % (j % 2))
                    for h in range(4):
                        op("pe", MM(A4[:, h, :], Qc[:, h, :], Pc[:, h, :], True, True), reads=[Qck, Pck], writes=[pk("A")])
                    if j < 4:
                        for h in range(4):
                            op("pe", MM(B4[:, h, :], Pc[:, h, :], Qc[:, h, :], True, True), reads=[Qck, Pck], writes=[pk("B")])
                    dummies()
                    yield
                    op("act", ACP(Pn, A4), reads=[pk("A")], writes=[Pnk])
                    if j < 4:
                        Qn, Qnk = Bf(Z, "Q%d" % (j % 2))
                        op("dve", CP(Qn, B4), reads=[pk("B")], writes=[Qnk])
                    dummies()
                    yield
                    for h in range(4):
                        op("pe", MM(C4[:, h, :], Pn[:, h, :], R[:, h, :], True, True), reads=[Pnk, Rk], writes=[pk("C")])
                    dummies()
                    yield
                    op("dve", TTOP(R, C4, R, ALU.add), reads=[pk("C"), Rk], writes=[Rk])
                    dummies()
                    yield
                    Pc, Pck = Pn, Pnk
                    if j < 4:
                        Qc, Qck = Qn, Qnk
                for (NM, NMk) in ((N32, N32k), (N64, N64k)):
                    Yb, Ybk = Bf(Z, "Yb")
                    Xb, Xbk = Bf(Z, "Xb")
                    for h in range(4):
                        op("pe", MM(A4[:, h, :], NM[:, h, :], R[:, h, :], True, True), reads=[NMk, Rk], writes=[pk("A")])
                    for h in range(4):
                        op("pe", TR(T4[:, h, :], R[:, h, :], ident), reads=[Rk, ("Cb",)], writes=[pk("C")])
                    dummies()
                    yield
                    op("act", ACP(Yb, A4), reads=[pk("A")], writes=[Ybk])
                    op("dve", CP(Xb, T4), reads=[pk("C")], writes=[Xbk])
                    dummies()
                    yield
                    for h in range(4):
                        op("pe", MM(B4[:, h, :], Xb[:, h, :], Yb[:, h, :], True, True), reads=[Xbk, Ybk], writes=[pk("B")])
                    dummies()
                    yield
                    op("dve", TTOP(R, B4, R, ALU.add), reads=[pk("B"), Rk], writes=[Rk])
                    dummies()
                    yield
                for h in range(4):
                    op("pe", MM(A4[:, h, :], R[:, h, :], vb[:, h, :], True, True), reads=[Rk, vbk], writes=[pk("A")])
                for h in range(4):
                    op("pe", MM(B4[:, h, :], kbg[:, h, :], R[:, h, :], True, True), reads=[Rk, kbgk], writes=[pk("B")])
                dummies()
                yield
                U, Uk = Bf(Z, "U", par)
                Wt, Wtk = Bf(Z, "Wt", par)
                op("act", ACP(U, A4), reads=[pk("A")], writes=[Uk])
                op("dve", CP(Wt, B4), reads=[pk("B")], writes=[Wtk])
                dummies()
                yield

            def scan(Z, c, t0):
                d = Z.d
                par = c % 2
                pk = lambda nm: ("dn", d, "ps" + nm)
                A4, B4, C4 = v4(Z.pA), v4(Z.pB), v4(Z.pC)
                U, Uk = Bf(Z, "U", par)
                Wt, Wtk = Bf(Z, "Wt", par)
                Aq, Aqk = Bf(Z, "Aq", par)
                qg, qgk = Bf(Z, "qg", par)
                kg, kgk = Bf(Z, "kg", par)
                vn, vnk = Bf(Z, "vn")
                for h in range(4):
                    op("pe", MM(A4[:, h, :], Wt[:, h, :], Z.Sb[:, h, :], True, True), reads=[Wtk, K(Z, "Sb")], writes=[pk("A")])
                dummies()
                yield
                op("dve", TTOP(vn, U, A4, ALU.subtract), reads=[Uk, pk("A")], writes=[vnk])
                dummies()
                yield
                for h in range(4):
                    op("pe", MM(C4[:, h, :], kg[:, h, :], vn[:, h, :], True, True), reads=[kgk, vnk], writes=[pk("C")])
                for h in range(4):
                    op("pe", MM(B4[:, h, :], qg[:, h, :], Z.Sb[:, h, :], True, False), reads=[qgk, K(Z, "Sb")], writes=[pk("B")])
                    op("pe", MM(B4[:, h, :], Aq[:, h, :], vn[:, h, :], False, True), reads=[Aqk, vnk], writes=[pk("B")])
                dummies()
                yield
                op("dve", TTOP(Z.S[:, :, :], Z.Sd[:, :, :], C4, ALU.add), reads=[K(Z, "Sd"), pk("C")], writes=[K(Z, "S")])
                op("act", ACP(Z.ostg[:, par, :], Z.pB[:, :]), reads=[pk("B")], writes=[K(Z, "ostg", par)])
                dummies()
                yield
                op("act", ACP(Z.Sb[:, :, :], Z.S[:, :, :]), reads=[K(Z, "S")], writes=[K(Z, "Sb")])
                tc = t0 + c * 128
                self.dma("sp", Z.oscr[tc:tc + 128, :], Z.ostg[:, par, :], reads=[K(Z, "ostg", par)])
                dummies()
                yield

            def par(*gens):
                gens = list(gens)
                while gens:
                    nxt = []
                    for g_ in gens:
                        try:
                            next(g_)
                            nxt.append(g_)
                        except StopIteration:
                            pass
                    gens = nxt
                    if gens:
                        yield

            def stream_gen(Z, L, t_base):
                ng = L // GT
                order = []
                for step in range(ng):
                    gi = step if Z.d == 0 else ng - 1 - step
                    for ci in range(4):
                        order.append((gi, ci if Z.d == 0 else 3 - ci))
                prev = None
                cur_g = None
                pend_scan = None
                for (gi, c) in order:
                    t0 = t_base + gi * GT
                    if gi != cur_g:
                        prologue(Z, gi, ng, t0)
                        cur_g = gi
                    yield from unit(Z, c)
                    yield from scan(Z, c, t0)

            t_base = 0
            for L in self.seqs:
                for Z in streams:
                    op("pool", MEMSET(Z.S[:, :, :], 0.0), writes=[K(Z, "S")])
                    op("pool", MEMSET(Z.Sb[:, :, :], 0.0), writes=[K(Z, "Sb")])
                ga = stream_gen(streams[0], L, t_base)
                gb = stream_gen(streams[1], L, t_base)
                for _ in range(offset):
                    next(ga)
                for _ in par(ga, gb):
                    pass
                t_base += L
            self.S.flush()

    def phase_mem(self, mem, nw_mem, w_kv, ident_f32, kT, vv):
        nS = len(self.seqs)
        with ExitStack() as st:
            ident, _ = self.consts(st, ident_f32)
            wbc = self.bcast_vec(st, "wbc", nw_mem)
            Wkv = self.load_weight(st, "Wkv", w_kv, 8, 2 * D)
            ms = self.sb(st, "ms", [128, 2, D], F32)
            h = self.sb(st, "h", [128, 2, D], BF16)
            mT = self.sb(st, "mT", [128, 8, 256], BF16)
            junk = self.sb(st, "junk", [128, D], BF16)
            ss = self.sb(st, "ss", [128, 2], F32)
            rstd = self.sb(st, "rstd", [128, 2], F32)
            ost = self.sb(st, "ost", [128, 4, 512], BF16)
            pT = [self.ps(st, "pT%d" % i, [128, 8, 128], BF16) for i in range(2)]
            pP = [self.ps(st, "pP%d" % i, [128, 512], F32) for i in range(4)]
            cP = cO = 0
            for s_ in range(nS):
                self.dma("sp", ms[:, :, :], mem[s_ * 256:(s_ + 1) * 256, :].rearrange("(b p) d -> p b d", p=128), writes=[("ms",)])
                for b in range(2):
                    self.rmsnorm_tok(ms[:, b, :], ("ms",), wbc[:, :], junk[:, :], ss[:, b:b + 1], rstd[:, b:b + 1],
                                     h[:, b, :], ("h", b), ("ss", b))
                    p = pT[b]
                    for kc in range(8):
                        self.op("pe", TR(p[:, kc, :], h[:, b, kc * 128:(kc + 1) * 128], ident[:, :]), reads=[("h", b), ("ident",)], writes=[("pT", b)])
                    self.op("act", ACP(mT[:, :, b * 128:(b + 1) * 128], p[:, :, :]), reads=[("pT", b)], writes=[("mT", b)])
                mk = [("mT", 0), ("mT", 1)]
                for j in range(8):
                    p = pP[cP % 4]; pk = ("pP", cP % 4); cP += 1
                    for kc in range(8):
                        self.op("pe", MM(p[:, 0:256], Wkv[:, kc, j * 128:(j + 1) * 128], mT[:, kc, :], kc == 0, kc == 7),
                                reads=mk + [("Wkv", kc)], writes=[pk])
                    o = cO % 4; cO += 1
                    self.op("act", ACP(ost[:, o, 0:256], p[:, 0:256]), reads=[pk], writes=[("ost", o)])
                    self.dma("sp", kT[s_, j * 128:(j + 1) * 128, :], ost[:, o, 0:256], reads=[("ost", o)])
                for mb in range(2):
                    for nh in range(2):
                        p = pP[cP % 4]; pk = ("pP", cP % 4); cP += 1
                        for kc in range(8):
                            self.op("pe", MM(p[:, :], mT[:, kc, mb * 128:(mb + 1) * 128], Wkv[:, kc, D + nh * 512:D + (nh + 1) * 512], kc == 0, kc == 7),
                                    reads=mk + [("Wkv", kc)], writes=[pk])
                        o = cO % 4; cO += 1
                        self.op("dve", CP(ost[:, o, :], p[:, :]), reads=[pk], writes=[("ost", o)])
                        self.dma("sp", vv[s_, mb * 128:(mb + 1) * 128, nh * 512:(nh + 1) * 512], ost[:, o, :], reads=[("ost", o)])
            self.S.flush()

    def phase_mid(self, x, scr, of, ob, kT, vv, dn_nw, w_up_a, w_out, nw_xa, w_q, w_o, ident_f32, x2out, TT=512):
        T = self.T
        NB = TT // 128
        gdT, gaT, sbT = scr["gdT"], scr["gaT"], scr["sbT"]
        nS = len(self.seqs)
        with ExitStack() as st:
            ident, _ = self.consts(st, ident_f32)
            wbc = self.bcast_vec(st, "wbc", nw_xa)
            Wua = self.load_weight(st, "Wua", w_up_a, 4, D)
            Wout = self.load_weight(st, "Wout", w_out, 8, D)
            Wq = self.load_weight(st, "Wq", w_q, 8, D)
            Wo = self.load_weight(st, "Wo", w_o, 8, D)
            KT1 = self.sb(st, "KT", [128, 8, 256], BF16)
            VV1 = self.sb(st, "VV", [128, 2, D], BF16)
            nwc = self.sb(st, "nwc", [128, 1], F32)
            self.dma("sp", nwc[:, :], dn_nw.rearrange("(p o) -> p o", o=1), writes=[("nwc",)])
            onesb = self.sb(st, "onesb", [128, 128], BF16)
            self.op("dve", MEMSET(onesb[:, :], 1.0), writes=[("onesb",)])
            xs2 = self.sb(st, "xs", [128, 2, NB, D], F32)
            ofs = self.sb(st, "ofs", [128, NB, 512], F32)
            obs = self.sb(st, "obs", [128, NB, 512], F32)
            on = self.sb(st, "on", [128, NB, 512], BF16)
            gds = self.sb(st, "gds", [128, 4, TT], BF16)
            gas = self.sb(st, "gas", [128, 8, TT], BF16)
            sbs = self.sb(st, "sbs", [128, 8, TT], BF16)
            aT = self.sb(st, "aT", [128, 4, TT], BF16)
            mg = self.sb(st, "mg", [128, 8, TT], BF16)
            tmpm = self.sb(st, "tmpm", [128, 2, TT], BF16)
            ss16 = self.sb(st, "ss16", [128, 16], F32)
            rs16 = self.sb(st, "rs16", [128, 16], F32)
            h = self.sb(st, "h", [128, 4, D], BF16)
            hT = self.sb(st, "hT", [128, 8, TT], BF16)
            qT = self.sb(st, "qT", [128, 8, TT], BF16)
            pex = self.sb(st, "pex", [128, 2, 2, TT], BF16)
            rinv = self.sb(st, "rinv", [128, 1, TT], F32)
            oT = self.sb(st, "oT", [128, 8, TT], BF16)
            junk = self.sb(st, "junk", [128, D], BF16)
            ss = self.sb(st, "ss", [128, NB], F32)
            rstd = self.sb(st, "rstd", [128, NB], F32)
            pT = [self.ps(st, "pT%d" % i, [128, 8, 128], BF16) for i in range(2)]
            pP = [self.ps(st, "pP%d" % i, [128, 512], F32) for i in range(6)]
            cnt = dict(T=0, P=0)

            def nP():
                b = cnt["P"] % 6
                cnt["P"] += 1
                return pP[b], ("pP", b)

            def nT():
                b = cnt["T"] % 2
                cnt["T"] += 1
                return pT[b], ("pT", b)

            seq_of_tile = []
            for si, L in enumerate(self.seqs):
                seq_of_tile += [si] * (L // TT)
            ntiles = T // TT

            def load_x(i):
                sl = i % 2
                self.dma("sp", xs2[:, sl], x[i * TT:(i + 1) * TT, :].rearrange("(b p) d -> p b d", p=128), writes=[("xs", sl)])

            def load_scr(i):
                t0 = i * TT
                self.dma("sp", ofs[:, :, :], of[t0:t0 + TT, :].rearrange("(b p) d -> p b d", p=128), writes=[("ofs",)])
                self.dma("sp", obs[:, :, :], ob[t0:t0 + TT, :].rearrange("(b p) d -> p b d", p=128), writes=[("obs",)])
                self.dma("sp", gds[:, :, :], gdT[:, t0:t0 + TT].rearrange("(j p) t -> p j t", p=128), writes=[("gds",)])
                self.dma("sp", gas[:, :, :], gaT[:, t0:t0 + TT].rearrange("(j p) t -> p j t", p=128), writes=[("gas",)])
                self.dma("sp", sbs[:, :, :], sbT[:, t0:t0 + TT].rearrange("(j p) t -> p j t", p=128), writes=[("sbs",)])

            load_x(0)
            load_scr(0)
            cur_seq = -1
            cur = dict(seq=-1)

            def tile_ctx(i):
                return i * TT, seq_of_tile[i], xs2[:, i % 2], ("xs", i % 2)

            def stageBD(i):
                t0, si, xs, xk = tile_ctx(i)
                self.op("dve", TTOP(ofs[:, :, :], ofs[:, :, :], obs[:, :, :], ALU.add), reads=[("ofs",), ("obs",)], writes=[("ofs",)])
                for b_ in range(NB):
                    for hh_ in range(4):
                        self.op("act", ACTF(junk[:, 0:128], ofs[:, b_, hh_ * 128:(hh_ + 1) * 128], AF.Square,
                                            accum_out=ss16[:, b_ * 4 + hh_:b_ * 4 + hh_ + 1]),
                                reads=[("ofs",)], writes=[("junk",), ("ss16",)])
                self.op("act", ACTF(rs16[:, :], ss16[:, :], AF.Sqrt, bias=EPS, scale=1.0 / 128), reads=[("ss16",)], writes=[("rs16",)])
                self.op("dve", RECIP(rs16[:, :], rs16[:, :]), reads=[("rs16",)], writes=[("rs16",)])
                self.op("dve", TTOP(on[:, :, :].rearrange("p b (h e) -> p (b h) e", e=128),
                                    ofs[:, :, :].rearrange("p b (h e) -> p (b h) e", e=128),
                                    rs16[:, :].unsqueeze(2).to_broadcast([128, 16, 128]), ALU.mult),
                        reads=[("ofs",), ("rs16",)], writes=[("on",)])
                for hp in range(2):
                    p, pk = nT()
                    for hh in range(2):
                        hd = hp * 2 + hh
                        for b in range(NB):
                            self.op("pe", TR(p[:, hh * 4 + b, :], on[:, b, hd * 128:(hd + 1) * 128], ident[:, :]),
                                    reads=[("on",), ("ident",)], writes=[pk])
                    for hh in range(2):
                        hd = hp * 2 + hh
                        self.op("dve", STT(aT[:, hd, :], p[:, hh * 4:hh * 4 + 4, :].rearrange("p a b -> p (a b)"), nwc[:, 0:1], gds[:, hd, :], ALU.mult, ALU.mult),
                                reads=[pk, ("nwc",), ("gds",)], writes=[("aT", hd)])
                for j in range(8):
                    p, pk = nP()
                    for kc in range(4):
                        self.op("pe", MM(p[:, 0:TT], Wua[:, kc, j * 128:(j + 1) * 128], aT[:, kc, :], kc == 0, kc == 3),
                                reads=[("aT", kc), ("Wua", kc)], writes=[pk])
                    tb = j % 2
                    self.op("dve", TTOP(tmpm[:, tb, :], p[:, 0:TT], gas[:, j, :], ALU.mult), reads=[pk, ("gas",)], writes=[("tmpm", tb)])
                    self.op("dve", TTOP(mg[:, j, :], tmpm[:, tb, :], sbs[:, j, :], ALU.add),
                            reads=[("tmpm", tb), ("sbs",)], writes=[("mg", j)])

                if i + 1 < ntiles:
                    load_scr(i + 1)

            def stageE(i):
                t0, si, xs, xk = tile_ctx(i)
                if i + 1 < ntiles:
                    load_x(i + 1)
                self.resid_proj(xs, xk, mg, "mg", Wout, "Wout", 8, nP, NB)

            def stageFG(i):
                t0, si, xs, xk = tile_ctx(i)
                if si != cur["seq"]:
                    cur["seq"] = si
                    self.dma("sp", KT1[:, :, :], kT[si].rearrange("(j p) m -> p j m", p=128), writes=[("KT",)])
                    self.dma("sp", VV1[:, :, :], vv[si].rearrange("(b p) d -> p b d", p=128), writes=[("VV",)])
                for b in range(NB):
                    self.rmsnorm_tok(xs[:, b, :], xk, wbc[:, :], junk[:, :], ss[:, b:b + 1], rstd[:, b:b + 1],
                                     h[:, b, :], ("h", b), ("ss", b))
                for b in range(NB):
                    p, pk = nT()
                    for kc in range(8):
                        self.op("pe", TR(p[:, kc, :], h[:, b, kc * 128:(kc + 1) * 128], ident[:, :]), reads=[("h", b), ("ident",)], writes=[pk])
                    self.op("act", ACP(hT[:, :, b * 128:(b + 1) * 128], p[:, :, :]), reads=[pk], writes=[("hT", b)])
                hTk = [("hT", b) for b in range(NB)]
                for j in range(8):
                    p, pk = nP()
                    for kc in range(8):
                        self.op("pe", MM(p[:, 0:TT], Wq[:, kc, j * 128:(j + 1) * 128], hT[:, kc, :], kc == 0, kc == 7),
                                reads=hTk + [("Wq", kc)], writes=[pk])
                    self.op("act", ACTF(qT[:, j, :], p[:, 0:TT], AF.Copy, scale=1.0 / 16), reads=[pk], writes=[("qT", j)])
                def att_scores(hd):
                    ps_ = hd % 2
                    for mb in range(2):
                        p, pk = nP()
                        for dc in range(2):
                            self.op("pe", MM(p[:, 0:TT], KT1[:, 2 * hd + dc, mb * 128:(mb + 1) * 128], qT[:, 2 * hd + dc, :], dc == 0, dc == 1),
                                    reads=[("KT",), ("qT", 2 * hd + dc)], writes=[pk])
                        self.op("act", ACTF(pex[:, ps_, mb, :], p[:, 0:TT], AF.Exp), reads=[pk], writes=[("pex", ps_, mb)])

                def att_out(hd):
                    ps_ = hd % 2
                    p, pk = nP()
                    for mb in range(2):
                        self.op("pe", MM(p[:, 0:TT], onesb[:, :], pex[:, ps_, mb, :], mb == 0, mb == 1),
                                reads=[("onesb",), ("pex", ps_, mb)], writes=[pk])
                    self.op("act", ACTF(rinv[:, 0, :], p[:, 0:TT], AF.Ln), reads=[pk], writes=[("rinv", 0)])
                    self.op("act", ACTF(rinv[:, 0, :], rinv[:, 0, :], AF.Exp, scale=-1.0), reads=[("rinv", 0)], writes=[("rinv", 0)])
                    for dc in range(2):
                        p, pk = nP()
                        for mb in range(2):
                            self.op("pe", MM(p[:, 0:TT], VV1[:, mb, (2 * hd + dc) * 128:(2 * hd + dc + 1) * 128], pex[:, ps_, mb, :], mb == 0, mb == 1),
                                    reads=[("VV",), ("pex", ps_, mb)], writes=[pk])
                        self.op("dve", TTOP(oT[:, 2 * hd + dc, :], p[:, 0:TT], rinv[:, 0, :], ALU.mult), reads=[pk, ("rinv", 0)], writes=[("oT", 2 * hd + dc)])

                att_scores(0)
                for hd in range(4):
                    if hd + 1 < 4:
                        att_scores(hd + 1)
                    att_out(hd)
                self.resid_proj(xs, xk, oT, "oT", Wo, "Wo", 8, nP, NB)
                self.dma("sp", x2out[t0:t0 + TT, :].rearrange("(b p) d -> p b d", p=128), xs, reads=[xk])

            stageBD(0)
            for i in range(ntiles):
                stageE(i)
                if i + 1 < ntiles:
                    stageBD(i + 1)
                stageFG(i)
            self.S.flush()

    def resid_proj(self, xs, xkey, aT, aname, W, wname, KC, nP, NB):
        for b in range(NB):
            for nh in range(2):
                p, pk = nP()
                for kc in range(KC):
                    self.op("pe", MM(p[:, :], aT[:, kc, b * 128:(b + 1) * 128], W[:, kc, nh * 512:(nh + 1) * 512], kc == 0, kc == KC - 1),
                            reads=[(aname, kc), (wname, kc)], writes=[pk])
                xsl = xs[:, b, nh * 512:(nh + 1) * 512]
                self.op("dve", TTOP(xsl, xsl, p[:, :], ALU.add), reads=[pk, xkey], writes=[xkey])


def make_consts():
    p = np.arange(128)[:, None]; f = np.arange(128)[None, :]
    BIG = 30000.0
    c = np.zeros((12, 128, 128), np.float32)
    c[0] = (p == f)
    c[1] = np.where(f >= p, BIG, 0.0)
    c[2] = np.where(f < p, -BIG, 0.0)
    c[3] = np.where(f <= p, BIG, 0.0)
    c[4] = np.where(f > p, -BIG, 0.0)
    c[5] = (p // 32 == f // 32)
    c[6] = (p // 64 == f // 64) & ((p % 64) // 32 == 1) & ((f % 64) // 32 == 0)
    c[7] = (p // 64 == 1) & (f // 64 == 0)
    c[8] = c[6].T
    c[9] = c[7].T
    c[10] = (p <= f)
    c[11] = (p >= f)
    return c


SEQS = (8192, 2048, 2048)
NCORES = 8
WNAMES = ["norm_mix_w", "w_in", "conv_w", "dn_a_log", "dn_dt_bias", "dn_norm_w", "w_up_a", "sg_ln_w", "sg_ln_b",
          "sg_w", "sg_b", "w_up_b", "w_out", "norm_xa_w", "norm_mem_w", "xa_w_q", "xa_w_kv", "xa_w_o",
          "norm_ffn_w", "ffn_w_gate_up", "ffn_w_down", "final_norm_w"]


def build_program(seqs=SEQS, wshapes=None):
    k = KB(seqs, debug=False)
    T = k.T
    nS = len(seqs)
    x = k.din("x", [T, D])
    mem = k.din("mem", [nS * 256, D])
    W = {n: k.din(n, list(wshapes[n])) for n in WNAMES}
    idf = k.din("idf", [128, 128])
    cst = k.din("cst", [12, 128, 128])
    y = k.dout("y", [T, D])
    scr = dict(qkvT=k.dscr("qkvT", [1536, T], BF16), bg=k.dscr("bg", [16, T]), gdT=k.dscr("gdT", [512, T], BF16),
               gaT=k.dscr("gaT", [1024, T], BF16), sbT=k.dscr("sbT", [1024, T], BF16))
    of = k.dscr("of", [T, 512])
    ob = k.dscr("ob", [T, 512])
    kT = k.dscr("kT", [nS, 1024, 256], BF16)
    vv = k.dscr("vv", [nS, 256, 1024], BF16)
    k.phase_mem(mem, W["norm_mem_w"], W["xa_w_kv"], idf, kT, vv)
    k.phase_in(x, W["w_in"], W["norm_mix_w"], W["sg_ln_w"], W["sg_ln_b"], W["sg_w"], W["sg_b"], W["w_up_b"],
               W["dn_a_log"], W["dn_dt_bias"], idf, scr)
    pre = dict(qn=k.dscr("qn", [512, T], BF16), kn=k.dscr("kn", [512, T], BF16),
               ktok=k.dscr("ktok", [T, 512], BF16), vtok=k.dscr("vtok", [T, 512], BF16))
    k.phase_prep(scr, W["conv_w"], cst, pre)
    k.phase_dn(scr, pre, cst, of, ob)
    k.phase_mid(x, scr, of, ob, kT, vv, W["dn_norm_w"], W["w_up_a"], W["w_out"], W["norm_xa_w"], W["xa_w_q"], W["xa_w_o"], idf, y)
    k.phase_ffn(y, y, W["ffn_w_gate_up"], W["ffn_w_down"], W["norm_ffn_w"], W["final_norm_w"], idf)
    k.es.close()
    return k


def kernel(**inputs):
    f32 = np.float32
    xp = np.asarray(inputs["x_prompt"], dtype=f32)
    xsm = np.asarray(inputs["x_sample"], dtype=f32)
    mp = np.asarray(inputs["mem_prompt"], dtype=f32)
    msm = np.asarray(inputs["mem_sample"], dtype=f32)
    w = {}
    for n in WNAMES:
        a = np.asarray(inputs[n], dtype=f32)
        if n != "final_norm_w":
            a = a[0]
        if n in ("dn_a_log", "dn_dt_bias"):
            a = a.reshape(8)
        w[n] = np.ascontiguousarray(a)
    k = build_program(SEQS, {n: w[n].shape for n in WNAMES})
    idf = np.eye(128, dtype=f32)
    cst = make_consts()
    in_maps = []
    for c in range(NCORES):
        m = dict(w)
        m["x"] = np.ascontiguousarray(np.concatenate([xp[c], xsm[2 * c], xsm[2 * c + 1]], 0))
        m["mem"] = np.ascontiguousarray(np.concatenate([mp[c], msm[2 * c], msm[2 * c + 1]], 0))
        m["idf"] = idf
        m["cst"] = cst
        in_maps.append(m)
    res = run_bass_kernel_spmd(k.nc, in_maps, core_ids=list(range(NCORES)))
    yp = np.empty(xp.shape, f32)
    ys = np.empty(xsm.shape, f32)
    for c in range(NCORES):
        y = np.asarray(res.results[c]["y"], dtype=f32)
        yp[c] = y[:8192]
        ys[2 * c] = y[8192:10240]
        ys[2 * c + 1] = y[10240:12288]
    return (yp, ys)
```

```python
import numpy as np
import concourse.bass as bass
import concourse.mybir as mybir
from concourse.bass_utils import run_bass_kernel_spmd
from contextlib import ExitStack

F32 = mybir.dt.float32
BF16 = mybir.dt.bfloat16
AF = mybir.ActivationFunctionType
ALU = mybir.AluOpType
AX = mybir.AxisListType

COMPUTE = ("pe", "act", "dve", "pool")
ALLENG = ("pe", "act", "dve", "pool", "sp")
SAME_ENGINE_SYNC = False
SAME_ENGINE_RAW = True


class Sched:
    def __init__(self, nc, es, n_dma_sems=12):
        self.nc = nc
        self.n_dma_sems = n_dma_sems
        self.csem = {e: es.enter_context(nc.semaphore("c_" + e)) for e in COMPUTE}
        self.dsem = [es.enter_context(nc.semaphore("d_%d" % s)) for s in range(n_dma_sems)]
        self.cnt = {e: 0 for e in COMPUTE}
        self.dma_cnt = [0] * n_dma_sems
        self.dma_rr = 0
        self.dma_rr_sw = 0
        self.seen = {e: {f: 0 for f in COMPUTE} for e in ALLENG}
        self.seen_d = {e: [0] * n_dma_sems for e in ALLENG}
        self.n_ops = 0
        self.n_waits = 0
        self._reset()

    def _reset(self):
        self.ops = []
        self.lastw = {}
        self.readers = {}

    def op(self, eng, fn, reads=(), writes=(), dma=False, strict=False):
        idx = len(self.ops)
        deps = set()
        raw = set()
        for r in reads:
            w = self.lastw.get(r)
            if w is not None:
                deps.add(w)
                raw.add(w)
        for k in writes:
            w = self.lastw.get(k)
            if w is not None:
                deps.add(w)
            rs = self.readers.get(k)
            if rs:
                deps.update(rs.values())
        rkey = ("dma", idx) if dma else eng
        for r in reads:
            self.readers.setdefault(r, {})[rkey] = idx
        for k in writes:
            self.lastw[k] = idx
            self.readers[k] = {}
        o = dict(eng=eng, fn=fn, deps=deps, raw=raw, dma=dma, sig=False, strict=strict)
        if dma:
            nsw = 4
            if eng == "pool":
                s = self.n_dma_sems - nsw + self.dma_rr_sw
                self.dma_rr_sw = (self.dma_rr_sw + 1) % nsw
            else:
                s = self.dma_rr
                self.dma_rr = (self.dma_rr + 1) % (self.n_dma_sems - nsw)
            self.dma_cnt[s] += 1
            o["dsem"] = s
            o["dval"] = 16 * self.dma_cnt[s]
        self.ops.append(o)
        return idx

    def flush(self):
        nc = self.nc
        ops = self.ops
        if not ops:
            return
        last = {}
        for i, o in enumerate(ops):
            if not o["dma"]:
                last[o["eng"]] = i
        for e, i in last.items():
            ops[i]["sig"] = True
        for o in ops:
            for d in o["deps"]:
                od = ops[d]
                if not od["dma"] and (od["eng"] != o["eng"] or SAME_ENGINE_SYNC or o["strict"]
                                      or (SAME_ENGINE_RAW and d in o["raw"] and o["eng"] != "pe")):
                    od["sig"] = True
        for o in ops:
            if not o["dma"] and o["sig"]:
                self.cnt[o["eng"]] += 1
                o["sval"] = self.cnt[o["eng"]]
        streams = {e: [] for e in ALLENG}
        seen, seen_d = self.seen, self.seen_d
        for o in ops:
            e = o["eng"]
            wc, wd = {}, {}
            for d in o["deps"]:
                od = ops[d]
                if od["dma"]:
                    wd[od["dsem"]] = max(wd.get(od["dsem"], 0), od["dval"])
                else:
                    f = od["eng"]
                    if f == e and not (SAME_ENGINE_SYNC or o["strict"] or (SAME_ENGINE_RAW and d in o["raw"] and e != "pe")):
                        continue
                    wc[f] = max(wc.get(f, 0), od["sval"])
            if o["dma"]:
                prev = o["dval"] - 16
                if prev > 0:
                    wd[o["dsem"]] = max(wd.get(o["dsem"], 0), prev)
            wl = []
            for f, v in wc.items():
                if v > seen[e][f]:
                    seen[e][f] = v
                    wl.append((self.csem[f], v))
            for s, v in wd.items():
                if v > seen_d[e][s]:
                    seen_d[e][s] = v
                    wl.append((self.dsem[s], v))
            streams[e].append((o, wl))
        end_wl = {}
        for e in ALLENG:
            wl = []
            for f in COMPUTE:
                v = self.cnt[f]
                if f != e and v > seen[e][f]:
                    seen[e][f] = v
                    wl.append((self.csem[f], v))
            for s in range(self.n_dma_sems):
                v = 16 * self.dma_cnt[s]
                if v > seen_d[e][s]:
                    seen_d[e][s] = v
                    wl.append((self.dsem[s], v))
            end_wl[e] = wl
        self.n_ops += len(ops)
        self.n_waits += sum(len(wl) for st in streams.values() for _, wl in st)
        csem, dsem = self.csem, self.dsem

        def run(eng_name, engine):
            for o, wl in streams[eng_name]:
                for sem, v in wl:
                    engine.wait_ge(sem, v)
                ins = o["fn"](engine)
                if o["dma"]:
                    ins.then_inc(dsem[o["dsem"]], 16)
                elif o["sig"]:
                    ins.then_inc(csem[eng_name], 1)
            for sem, v in end_wl[eng_name]:
                engine.wait_ge(sem, v)

        with nc.Block() as block:
            @block.tensor
            def _(eng):
                run("pe", eng)

            @block.scalar
            def _(eng):
                run("act", eng)

            @block.vector
            def _(eng):
                run("dve", eng)

            @block.gpsimd
            def _(eng):
                run("pool", eng)

            @block.sync
            def _(eng):
                run("sp", eng)
        self._reset()


D = 1024
DFF = 2816
NIN = 5136
EPS = 1e-6


def MM(out, lhsT, rhs, start, stop):
    return lambda e: e.matmul(out, lhsT, rhs, start=bool(start), stop=bool(stop))


def TR(out, in_, ident):
    return lambda e: e.transpose(out, in_, ident)


def ACTF(out, in_, func, bias=None, scale=None, accum_out=None):
    kw = {}
    if bias is not None:
        kw["bias"] = bias
    if scale is not None:
        kw["scale"] = scale
    if accum_out is not None:
        kw["accum_out"] = accum_out
    return lambda e: e.activation(out, in_, func, **kw)


def ACP(out, in_):
    return lambda e: e.copy(out, in_)


def SQRT(out, in_):
    return lambda e: e.sqrt(out, in_)


def CP(out, in_):
    return lambda e: e.tensor_copy(out, in_)


def RECIP(out, in_):
    return lambda e: e.reciprocal(out, in_)


def TTOP(out, a, b, op):
    return lambda e: e.tensor_tensor(out, a, b, op)


def TS(out, in0, s1, s2, op0, op1):
    return lambda e: e.tensor_scalar(out, in0, s1, s2, op0, op1)


def STT(out, in0, scalar, in1, op0, op1):
    return lambda e: e.scalar_tensor_tensor(out, in0, scalar, in1, op0, op1)


def MEMSET(ap, v):
    return lambda e: e.memset(ap, v)


class KB:
    def __init__(self, seqs, debug=False):
        self.seqs = list(seqs)
        self.T = sum(seqs)
        self.debug = debug
        self.nc = bass.Bass("TRN2", target_bir_lowering=False)
        self.es = ExitStack()
        self.S = Sched(self.nc, self.es)
        self.dram = {}
        self.rr = 0

    def din(self, name, shape, dt=F32):
        t = self.nc.dram_tensor(name, list(shape), dt, kind="ExternalInput").ap()
        self.dram[name] = t
        return t

    def dout(self, name, shape, dt=F32):
        t = self.nc.dram_tensor(name, list(shape), dt, kind="ExternalOutput").ap()
        self.dram[name] = t
        return t

    def dscr(self, name, shape, dt=F32):
        kind = "ExternalOutput" if self.debug else "Internal"
        t = self.nc.dram_tensor(name, list(shape), dt, kind=kind).ap()
        self.dram[name] = t
        return t

    def sb(self, st, name, shape, dt):
        self.uid = getattr(self, "uid", 0) + 1
        return st.enter_context(self.nc.sbuf_tensor("%s_%d" % (name, self.uid), list(shape), dt))

    def ps(self, st, name, shape, dt):
        self.uid = getattr(self, "uid", 0) + 1
        return st.enter_context(self.nc.psum_tensor("%s_%d" % (name, self.uid), list(shape), dt))

    def op(self, *a, **k):
        return self.S.op(*a, **k)

    def dma(self, q, out, in_, reads=(), writes=()):
        return self.S.op(q, lambda e: e.dma_start(out=out, in_=in_), reads=reads, writes=writes, dma=True)

    def tap(self, name, ap, reads, dt=F32):
        if not self.debug:
            return
        t = self.dout(name, list(ap.shape), dt)
        self.dma("sp", t, ap, reads=reads)

    def cast_eng(self):
        self.rr += 1
        return ("dve", "pool", "act")[self.rr % 3]

    def copy(self, eng, out, in_, reads, writes):
        if eng == "act":
            return self.op("act", ACP(out, in_), reads=reads, writes=writes)
        return self.op(eng, CP(out, in_), reads=reads, writes=writes)

    def load_weight(self, st, name, src2d, KC, N, stage=None):
        w = self.sb(st, name, [128, KC, N], BF16)
        for kc in range(KC):
            self.dma("pool", w[:, kc, :], src2d[kc * 128:(kc + 1) * 128, :], writes=[(name, kc)])
        return w

    def rmsnorm_tok(self, xb, xkey, wbc, junk, ss1, rs1, out, outkey, skey, wkey=("wbc",)):
        self.op("act", ACTF(junk, xb, AF.Square, accum_out=ss1), reads=[xkey], writes=[("junk",), skey])
        self.op("act", ACTF(rs1, ss1, AF.Sqrt, bias=EPS, scale=1.0 / D), reads=[skey], writes=[skey + ("r",)], strict=True)
        self.op("dve", RECIP(rs1, rs1), reads=[skey + ("r",)], writes=[skey + ("r",)])
        self.op("dve", STT(out, xb, rs1, wbc, ALU.mult, ALU.mult), reads=[xkey, skey + ("r",), wkey], writes=[outkey], strict=True)

    def consts(self, st, ident_f32):
        ident = self.sb(st, "ident", [128, 128], BF16)
        idf = self.sb(st, "identf", [128, 128], F32)
        self.dma("sp", idf[:, :], ident_f32, writes=[("idf",)])
        self.op("dve", CP(ident[:, :], idf[:, :]), reads=[("idf",)], writes=[("ident",)])
        return ident, idf

    def bcast_vec(self, st, name, vec):
        n = vec.shape[0]
        t = self.sb(st, name, [128, n], F32)
        self.dma("sp", t[:, :], vec.partition_broadcast(128), writes=[(name,)])
        return t

    def phase_ffn(self, xin, yout, w_gu, w_dn, nw_ffn, nw_fin, ident_f32, TT=256):
        T = self.T
        NB = TT // 128
        KF = DFF // 128
        with ExitStack() as st:
            ident, _ = self.consts(st, ident_f32)
            wbc = self.bcast_vec(st, "wbc", nw_ffn)
            wbc2 = self.bcast_vec(st, "wbc2", nw_fin)
            Wgu = self.load_weight(st, "Wgu", w_gu, 8, 2 * DFF)
            Wd = self.load_weight(st, "Wd", w_dn, KF, D)
            NX = 3
            xs = self.sb(st, "xs", [128, NX, NB, D], F32)
            h = self.sb(st, "h", [128, 2, D], BF16)
            hT = self.sb(st, "hT", [128, 2, 8, TT], BF16)
            act = self.sb(st, "act", [128, KF, TT], BF16)
            sg = self.sb(st, "sg", [128, 3, TT], F32)
            junk = self.sb(st, "junk", [128, D], BF16)
            ss = self.sb(st, "ss", [128, NX, NB], F32)
            rstd = self.sb(st, "rstd", [128, NX, NB], F32)
            ss2 = self.sb(st, "ss2", [128, NX, NB], F32)
            rstd2 = self.sb(st, "rstd2", [128, NX, NB], F32)
            pT = [self.ps(st, "pT%d" % i, [128, 8, 128], BF16) for i in range(2)]
            pG = [self.ps(st, "pG%d" % i, [128, 512], F32) for i in range(2)]
            pU = [self.ps(st, "pU%d" % i, [128, 512], F32) for i in range(2)]
            pD = [self.ps(st, "pD%d" % i, [128, 512], F32) for i in range(2)]
            ntiles = T // TT
            cT = cG = cD = 0

            def load(i):
                sl = i % NX
                src = xin[i * TT:(i + 1) * TT, :].rearrange("(b p) d -> p b d", p=128)
                self.dma("sp", xs[:, sl], src, writes=[("xs", sl)])

            def normA(i):
                sl = i % NX
                for b in range(NB):
                    hb = (i * NB + b) % 2
                    self.rmsnorm_tok(xs[:, sl, b, :], ("xs", sl), wbc[:, :], junk[:, :], ss[:, sl, b:b + 1],
                                     rstd[:, sl, b:b + 1], h[:, hb, :], ("h", hb), ("ss", sl, b))

            def normB(i):
                nonlocal cT
                hs = i % 2
                for b in range(NB):
                    hb = (i * NB + b) % 2
                    p = pT[cT % 2]
                    pk = ("pT", cT % 2)
                    cT += 1
                    for kc in range(8):
                        self.op("pe", TR(p[:, kc, :], h[:, hb, kc * 128:(kc + 1) * 128], ident[:, :]),
                                reads=[("h", hb), ("ident",)], writes=[pk])
                    self.op("act", ACP(hT[:, hs, :, b * 128:(b + 1) * 128], p[:, :, :]),
                            reads=[pk], writes=[("hT", hs, b)])

            load(0)
            if ntiles > 1:
                load(1)
            normA(0)
            normB(0)
            for i in range(ntiles):
                if i + 2 < ntiles:
                    load(i + 2)
                sl = i % NX
                hs = i % 2
                if i + 1 < ntiles:
                    normA(i + 1)
                hTk = [("hT", hs, b) for b in range(NB)]
                if i == 0:
                    self.tap("t_h", h[:, :, :], [("h", 0), ("h", 1)], BF16)
                    self.tap("t_hT", hT[:, hs], hTk, BF16)
                for j in range(KF):
                    g = pG[cG % 2]
                    u = pU[cG % 2]
                    gk = ("pG", cG % 2)
                    uk = ("pU", cG % 2)
                    sgs = cG % 3
                    cG += 1
                    for kc in range(8):
                        self.op("pe", MM(g[:, 0:TT], Wgu[:, kc, j * 128:(j + 1) * 128], hT[:, hs, kc, :], kc == 0, kc == 7),
                                reads=hTk + [("Wgu", kc)], writes=[gk])
                    for kc in range(8):
                        self.op("pe", MM(u[:, 0:TT], Wgu[:, kc, DFF + j * 128:DFF + (j + 1) * 128], hT[:, hs, kc, :], kc == 0, kc == 7),
                                reads=hTk + [("Wgu", kc)], writes=[uk])
                    self.op("act", ACTF(sg[:, sgs, :], g[:, 0:TT], AF.Silu), reads=[gk], writes=[("sg", sgs)])
                    if i == 0 and j == 0 and self.debug:
                        gcp = self.sb(st, "gcp", [128, 2, TT], F32)
                        self.op("act", ACP(gcp[:, 0, :], g[:, 0:TT]), reads=[gk], writes=[("gcp",)])
                        self.op("dve", CP(gcp[:, 1, :], u[:, 0:TT]), reads=[uk], writes=[("gcp2",)])
                        self.tap("t_g", gcp[:, :, :], [("gcp",), ("gcp2",)])
                        self.tap("t_sg", sg[:, sgs, :], [("sg", sgs)])
                    self.op("dve", TTOP(act[:, j, :], sg[:, sgs, :], u[:, 0:TT], ALU.mult),
                            reads=[uk, ("sg", sgs)], writes=[("act", j)])
                if i == 0:
                    self.tap("t_act", act[:, :, :], [("act", j) for j in range(KF)], BF16)
                if i + 1 < ntiles:
                    normB(i + 1)
                for b in range(NB):
                    for nh in range(2):
                        pd = pD[cD % 2]
                        dk = ("pD", cD % 2)
                        cD += 1
                        for kc in range(KF):
                            self.op("pe", MM(pd[:, :], act[:, kc, b * 128:(b + 1) * 128], Wd[:, kc, nh * 512:(nh + 1) * 512],
                                             kc == 0, kc == KF - 1),
                                    reads=[("act", kc), ("Wd", kc)], writes=[dk])
                        xsl = xs[:, sl, b, nh * 512:(nh + 1) * 512]
                        self.op("dve", TTOP(xsl, xsl, pd[:, :], ALU.add), reads=[dk, ("xs", sl)], writes=[("xs", sl)])
                    self.rmsnorm_tok(xs[:, sl, b, :], ("xs", sl), wbc2[:, :], junk[:, :], ss2[:, sl, b:b + 1],
                                     rstd2[:, sl, b:b + 1], xs[:, sl, b, :], ("xs", sl), ("fs", sl, b), wkey=("wbc2",))
                dst = yout[i * TT:(i + 1) * TT, :].rearrange("(b p) d -> p b d", p=128)
                self.dma("pool", dst, xs[:, sl], reads=[("xs", sl)])
            self.S.flush()

    def phase_in(self, x, w_in, nw_mix, ln_w, ln_b, sg_w, sg_b, w_up_b, a_log, dt_bias, ident_f32, scr, TT=512):
        T = self.T
        NB = TT // 128
        qkvT, bg, gdT, gaT, sbT = scr["qkvT"], scr["bg"], scr["gdT"], scr["gaT"], scr["sbT"]
        with ExitStack() as st:
            ident, _ = self.consts(st, ident_f32)
            wbc = self.bcast_vec(st, "wbc", nw_mix)
            Win = self.load_weight(st, "Win", w_in, 8, NIN)
            Wub = self.load_weight(st, "Wub", w_up_b, 4, D)
            wsn = self.sb(st, "wsn", [128, 4, 128], BF16)
            self.dma("pool", wsn[:, :, :], sg_w.rearrange("g t s -> t g s"), writes=[("wsn",)])
            WsT = self.sb(st, "WsT", [128, 4, 128], BF16)
            bsrow = self.sb(st, "bsrow", [1, 512], BF16)
            self.dma("pool", bsrow[:, :], sg_b.rearrange("(o g) t -> o (g t)", o=1), writes=[("bsrow",)])
            ones1 = self.sb(st, "ones1", [1, 128], BF16)
            self.op("dve", MEMSET(ones1[:, :], 1.0), writes=[("ones1",)])
            onesb = self.sb(st, "onesb", [128, 128], BF16)
            self.op("dve", MEMSET(onesb[:, :], 1.0), writes=[("onesb",)])
            lnw = self.sb(st, "lnw", [128, 4], F32)
            lnb = self.sb(st, "lnb", [128, 4], F32)
            self.op("sp", lambda e: e.dma_start(out=lnw[:, :], in_=ln_w.rearrange("(g p) -> p g", p=128), allow_slow_non_contiguous=True),
                    writes=[("lnw",)], dma=True)
            self.op("sp", lambda e: e.dma_start(out=lnb[:, :], in_=ln_b.rearrange("(g p) -> p g", p=128), allow_slow_non_contiguous=True),
                    writes=[("lnb",)], dma=True)
            alc = self.sb(st, "alc", [8, 1], F32)
            dtb = self.sb(st, "dtb", [8, 1], F32)
            negA = self.sb(st, "negA", [8, 1], F32)
            self.dma("sp", alc[:, :], a_log.rearrange("(p o) -> p o", o=1), writes=[("alc",)])
            self.dma("sp", dtb[:, :], dt_bias.rearrange("(p o) -> p o", o=1), writes=[("dtb",)])
            self.op("act", ACTF(negA[:, :], alc[:, :], AF.Exp), reads=[("alc",)], writes=[("negA",)])
            self.op("dve", TS(negA[:, :], negA[:, :], -1.0, None, ALU.mult, ALU.bypass) if False else
                    (lambda e: e.tensor_scalar_mul(negA[:, :], negA[:, :], -1.0)), reads=[("negA",)], writes=[("negA",)])
            xs = self.sb(st, "xs", [128, NB, D], F32)
            h = self.sb(st, "h", [128, 4, D], BF16)
            hT = self.sb(st, "hT", [128, 2, 8, TT], BF16)
            junk = self.sb(st, "junk", [128, D], BF16)
            ss = self.sb(st, "ss", [128, NB], F32)
            rstd = self.sb(st, "rstd", [128, NB], F32)
            ost = self.sb(st, "ost", [128, 4, TT], BF16)
            bst = self.sb(st, "bst", [8, 2, TT], F32)
            est = self.sb(st, "est", [8, TT], F32)
            uT = self.sb(st, "uT", [128, 4, TT], BF16)
            vT = self.sb(st, "vT", [128, 4, TT], F32)
            vb = self.sb(st, "vb", [128, 4, TT], BF16)
            vq = self.sb(st, "vq", [128, 4, TT], BF16)
            gbT = self.sb(st, "gbT", [128, 8, TT], BF16)
            vn = self.sb(st, "vn", [128, 4, TT], BF16)
            vtmp = self.sb(st, "vtmp", [128, 2, TT], F32)
            vtok = self.sb(st, "vtok", [128, 4, NB, 128], BF16)
            ubT = self.sb(st, "ubT", [128, 4, TT], BF16)
            mu = self.sb(st, "mu", [128, TT], F32)
            msq = self.sb(st, "msq", [128, TT], F32)
            lrs = self.sb(st, "lrs", [128, TT], F32)
            pT = [self.ps(st, "pT%d" % i, [128, 8, 128], BF16) for i in range(2)]
            pP = [self.ps(st, "pP%d" % i, [128, 512], F32) for i in range(4)]
            pS = [self.ps(st, "pS%d" % i, [128, 512], F32) for i in range(2)]
            for g in range(4):
                self.op("pe", TR(pT[0][:, g, :], wsn[:, g, :], ident[:, :]), reads=[("wsn",), ("ident",)], writes=[("pT", 0)])
            self.op("act", ACP(WsT[:, :, :], pT[0][:, 0:4, :]), reads=[("pT", 0)], writes=[("WsT",)])
            ntiles = T // TT
            cnt = dict(T=0, P=0, O=0, S=0)

            def load(i):
                for b in range(NB):
                    self.dma("sp", xs[:, b, :], x[i * TT + b * 128:i * TT + (b + 1) * 128, :], writes=[("xs", b)])

            def proj(i, hs, c0, M):
                b = cnt["P"] % 4
                cnt["P"] += 1
                p = pP[b]
                for kc in range(8):
                    self.op("pe", MM(p[0:M, 0:TT], Win[:, kc, c0:c0 + M], hT[:, hs, kc, :], kc == 0, kc == 7),
                            reads=[("hT", hs, bb) for bb in range(NB)] + [("Win", kc)], writes=[("pP", b)])
                return p[0:M, 0:TT], ("pP", b)

            def store(dst, src_ap, key):
                self.dma("sp", dst, src_ap, reads=[key])

            def ostage():
                o = cnt["O"] % 4
                cnt["O"] += 1
                return ost[:, o, :], ("ost", o)

            def normA(i):
                for b in range(NB):
                    self.rmsnorm_tok(xs[:, b, :], ("xs", b), wbc[:, :], junk[:, :], ss[:, b:b + 1],
                                     rstd[:, b:b + 1], h[:, b, :], ("h", b), ("ss", b))
                if i + 1 < ntiles:
                    load(i + 1)

            def normB(i):
                hs = i % 2
                for b in range(NB):
                    p = pT[cnt["T"] % 2]
                    pk = ("pT", cnt["T"] % 2)
                    cnt["T"] += 1
                    for kc in range(8):
                        self.op("pe", TR(p[:, kc, :], h[:, b, kc * 128:(kc + 1) * 128], ident[:, :]),
                                reads=[("h", b), ("ident",)], writes=[pk])
                    self.op("act", ACP(hT[:, hs, :, b * 128:(b + 1) * 128], p[:, :, :]), reads=[pk], writes=[("hT", hs, b)])

            load(0)
            normA(0)
            normB(0)
            for i in range(ntiles):
                hs = i % 2
                t0 = i * TT
                if i + 1 < ntiles:
                    normA(i + 1)
                for j in range(4):
                    p, pk = proj(i, hs, 2576 + j * 128, 128)
                    self.op("act", ACTF(vT[:, j, :], p, AF.Gelu_apprx_tanh), reads=[pk], writes=[("vT", j)])
                    self.op("pool", CP(vb[:, j, :], vT[:, j, :]), reads=[("vT", j)], writes=[("vb", j)])
                    self.op("pool", TTOP(vq[:, j, :], vT[:, j, :], vT[:, j, :], ALU.mult), reads=[("vT", j)], writes=[("vq", j)])
                for j in range(4):
                    p, pk = proj(i, hs, 2064 + j * 128, 128)
                    self.op("act", ACTF(uT[:, j, :], p, AF.Gelu_apprx_tanh), reads=[pk], writes=[("uT", j)])
                for c in range(12):
                    p, pk = proj(i, hs, c * 128, 128)
                    o, ok = ostage()
                    if c % 2 == 0:
                        self.op("act", ACP(o, p), reads=[pk], writes=[ok])
                    else:
                        self.op("dve", CP(o, p), reads=[pk], writes=[ok])
                    store(qkvT[c * 128:(c + 1) * 128, t0:t0 + TT], o, ok)
                s1, s2 = pS[0], pS[1]
                for j in range(4):
                    self.op("pe", MM(s1[:, 0:TT], onesb[:, :], vb[:, j, :], j == 0, j == 3), reads=[("vb", j), ("onesb",)], writes=[("pS", 0)])
                for j in range(4):
                    self.op("pe", MM(s2[:, 0:TT], onesb[:, :], vq[:, j, :], j == 0, j == 3), reads=[("vq", j), ("onesb",)], writes=[("pS", 1)])
                self.op("act", ACTF(mu[:, :], s1[:, 0:TT], AF.Copy, scale=1.0 / 512), reads=[("pS", 0)], writes=[("mu",)])
                self.op("pool", TTOP(msq[:, :], mu[:, :], mu[:, :], ALU.mult), reads=[("mu",)], writes=[("msq",)])
                self.op("dve", STT(lrs[:, :], s2[:, 0:TT], 1.0 / 512, msq[:, :], ALU.mult, ALU.subtract),
                        reads=[("pS", 1), ("msq",)], writes=[("lrs",)])
                self.op("act", ACTF(lrs[:, :], lrs[:, :], AF.Ln, bias=EPS, scale=1.0), reads=[("lrs",)], writes=[("lrs",)])
                self.op("act", ACTF(lrs[:, :], lrs[:, :], AF.Exp, scale=-0.5), reads=[("lrs",)], writes=[("lrs",)])
                p, pk = proj(i, hs, 1536, 8)
                self.op("act", ACTF(bst[:, 0, :], p, AF.Sigmoid), reads=[pk], writes=[("bst", 0)])
                store(bg[0:8, t0:t0 + TT], bst[:, 0, :], ("bst", 0))
                p, pk = proj(i, hs, 1544, 8)
                self.op("act", ACTF(est[:, :], p, AF.Exp, bias=dtb[:, 0:1], scale=1.0), reads=[pk, ("dtb",)], writes=[("est",)])
                self.op("act", ACTF(est[:, :], est[:, :], AF.Ln, bias=1.0, scale=1.0), reads=[("est",)], writes=[("est",)])
                self.op("dve", (lambda o_=bst[:, 1, :], i_=est[:, :], s_=negA[:, 0:1]: lambda e: e.tensor_scalar_mul(o_, i_, s_))(),
                        reads=[("est",), ("negA",)], writes=[("bst", 1)])
                store(bg[8:16, t0:t0 + TT], bst[:, 1, :], ("bst", 1))
                for j in range(4):
                    p, pk = proj(i, hs, 1552 + j * 128, 128)
                    o, ok = ostage()
                    self.op("act", ACTF(o, p, AF.Silu), reads=[pk], writes=[ok])
                    store(gdT[j * 128:(j + 1) * 128, t0:t0 + TT], o, ok)
                for j in range(8):
                    p, pk = proj(i, hs, 3088 + j * 128, 128)
                    o, ok = ostage()
                    self.op("act", ACTF(o, p, AF.Sigmoid), reads=[pk], writes=[ok])
                    store(gaT[j * 128:(j + 1) * 128, t0:t0 + TT], o, ok)
                for j in range(4):
                    tb = j % 2
                    self.op("pool", TTOP(vtmp[:, tb, :], vT[:, j, :], mu[:, :], ALU.subtract), reads=[("vT", j), ("mu",)], writes=[("vtmp", tb)])
                    self.op("dve", TTOP(vtmp[:, tb, :], vtmp[:, tb, :], lrs[:, :], ALU.mult), reads=[("vtmp", tb), ("lrs",)], writes=[("vtmp", tb)])
                    self.op("dve", TS(vn[:, j, :], vtmp[:, tb, :], lnw[:, j:j + 1], lnb[:, j:j + 1], ALU.mult, ALU.add),
                            reads=[("vtmp", tb), ("lnw",), ("lnb",)], writes=[("vn", j)])
                for j in range(8):
                    p, pk = proj(i, hs, 4112 + j * 128, 128)
                    self.op("act", ACTF(gbT[:, j, :], p, AF.Sigmoid), reads=[pk], writes=[("gbT", j)])
                if i + 1 < ntiles:
                    normB(i + 1)
                for b in range(NB):
                    p = pT[cnt["T"] % 2]
                    pk = ("pT", cnt["T"] % 2)
                    cnt["T"] += 1
                    for j in range(4):
                        self.op("pe", TR(p[:, j, :], vn[:, j, b * 128:(b + 1) * 128], ident[:, :]), reads=[("vn", j), ("ident",)], writes=[pk])
                    self.op("act", ACP(vtok[:, :, b, :], p[:, 0:4, :]), reads=[pk], writes=[("vtok", b)])
                for j in range(4):
                    sb_ = cnt["S"] % 2
                    cnt["S"] += 1
                    pm = pS[sb_]
                    for b in range(NB):
                        self.op("pe", MM(pm[:, b * 128:(b + 1) * 128], vtok[:, j, b, :], WsT[:, j, :], True, False),
                                reads=[("vtok", b), ("WsT",)], writes=[("pS", sb_)])
                        self.op("pe", MM(pm[:, b * 128:(b + 1) * 128], ones1[0:1, :], bsrow[0:1, j * 128:(j + 1) * 128], False, True),
                                reads=[("ones1",), ("bsrow",)], writes=[("pS", sb_)])
                    self.op("dve", TTOP(ubT[:, j, :], uT[:, j, :], pm[:, 0:TT], ALU.mult), reads=[("pS", sb_), ("uT", j)], writes=[("ubT", j)])
                for j in range(8):
                    b = cnt["P"] % 4
                    cnt["P"] += 1
                    p = pP[b]
                    for kc in range(4):
                        self.op("pe", MM(p[:, 0:TT], Wub[:, kc, j * 128:(j + 1) * 128], ubT[:, kc, :], kc == 0, kc == 3),
                                reads=[("ubT", kc), ("Wub", kc)], writes=[("pP", b)])
                    o, ok = ostage()
                    self.op("dve", TTOP(o, gbT[:, j, :], p[:, 0:TT], ALU.mult), reads=[("pP", b), ("gbT", j)], writes=[ok])
                    store(sbT[j * 128:(j + 1) * 128, t0:t0 + TT], o, ok)
            self.S.flush()

    def phase_prep(self, scr, conv_w, cst, pre):
        qkvT = scr["qkvT"]
        qn, kn, ktok, vtok = pre["qn"], pre["kn"], pre["ktok"], pre["vtok"]
        GT = 512
        SCALE = 128 ** -0.5
        with ExitStack() as st:
            C = self.sb(st, "C", [128, 128], F32)
            self.dma("sp", C[:, :], cst[0], writes=[("C",)])
            ident = self.sb(st, "ident", [128, 128], BF16)
            self.op("dve", CP(ident[:, :], C[:, :]), reads=[("C",)], writes=[("ident",)])
            onesb = self.sb(st, "onesb", [128, 128], BF16)
            self.op("dve", MEMSET(onesb[:, :], 1.0), writes=[("onesb",)])
            cw = self.sb(st, "cw", [128, 5, 12], F32)
            self.op("sp", lambda e: e.dma_start(out=cw[:, :, :], in_=conv_w.rearrange("j (c p) -> p j c", p=128), allow_slow_non_contiguous=True),
                    writes=[("cw",)], dma=True)
            Cd = self.sb(st, "Cd", [128, 12, 5, 128], BF16)
            for cc in range(12):
                for j in range(5):
                    self.op("dve", (lambda o_=Cd[:, cc, j, :], s_=cw[:, j, cc:cc + 1]: lambda e: e.tensor_scalar_mul(o_, C[:, :], s_))(),
                            reads=[("C",), ("cw",)], writes=[("Cd", cc)])
            NR = 4
            raw = self.sb(st, "raw", [128, NR, GT + 4], BF16)
            sil = self.sb(st, "sil", [128, 4, GT], F32)
            sq = self.sb(st, "sq", [128, 3, GT], BF16)
            rsn = self.sb(st, "rsn", [128, 3, GT], F32)
            fm = self.sb(st, "fm", [128, 8, GT], BF16)
            VT = self.sb(st, "VT", [128, 4, GT], BF16)
            tk = self.sb(st, "tk", [128, 2, 4, 512], BF16)
            pcv = [self.ps(st, "pcv%d" % i, [128, 512], F32) for i in range(3)]
            pss = [self.ps(st, "pss%d" % i, [128, 512], F32) for i in range(2)]
            ptr = [self.ps(st, "ptr%d" % i, [128, 1024], BF16) for i in range(2)]
            op = self.op
            cn = dict(r=0, c=0, s=0, f=0, t=0, x=0)
            its = []
            t_base = 0
            for L in self.seqs:
                ng = L // GT
                for gi in range(ng):
                    for qi in (1, 2, 0):
                        for h in range(4):
                            its.append((t_base + gi * GT, gi, ng, qi, h))
                t_base += L

            def load(n):
                t0, gi, ng, qi, h = its[n]
                cc = qi * 4 + h
                rs_ = n % NR
                rk = ("raw", rs_)
                lo = 2 if gi == 0 else 0
                hi = GT + 2 if gi == ng - 1 else GT + 4
                if gi == 0 or gi == ng - 1:
                    op("pool", MEMSET(raw[:, rs_, :], 0.0), writes=[rk])
                self.dma("sp", raw[:, rs_, lo:hi], qkvT[cc * 128:(cc + 1) * 128, t0 - 2 + lo:t0 - 2 + hi], writes=[rk])

            deferred = []

            stA = {}
            late = []

            def computeA(n):
                t0, gi, ng, qi, h = its[n]
                cc = qi * 4 + h
                rs_ = n % NR
                rk = ("raw", rs_)
                pb_ = cn["c"] % 3
                cn["c"] += 1
                pf, pfk = pcv[pb_], ("pcv", pb_)
                for j in range(5):
                    op("pe", MM(pf[:, :], Cd[:, cc, j, :], raw[:, rs_, j:j + GT], j == 0, j == 4), reads=[rk, ("Cd", cc)], writes=[pfk])
                if qi == 2:
                    op("act", ACTF(VT[:, h, :], pf[:, :], AF.Silu), reads=[pfk], writes=[("VT", h)])
                else:
                    op("act", ACTF(sil[:, h, :], pf[:, :], AF.Silu), reads=[pfk], writes=[("sil", h)])

            def computeB(n):
                t0, gi, ng, qi, h = its[n]
                if qi == 2:
                    src, srck = VT[:, h, :], ("VT", h)
                else:
                    ss_ = h % 3
                    sk = ("sil", h)
                    op("pool", TTOP(sq[:, ss_, :], sil[:, h, :], sil[:, h, :], ALU.mult), reads=[sk], writes=[("sq", ss_)])
                    ps_ = h % 2
                    p2, p2k = pss[ps_], ("pss", ps_)
                    op("pe", MM(p2[:, :], onesb[:, :], sq[:, ss_, :], True, True), reads=[("sq", ss_), ("onesb",)], writes=[p2k])
                    op("act", ACTF(rsn[:, ss_, :], p2[:, :], AF.Ln, bias=EPS, scale=1.0), reads=[p2k], writes=[("rsn", ss_)])
                    op("act", ACTF(rsn[:, ss_, :], rsn[:, ss_, :], AF.Exp, scale=-0.5), reads=[("rsn", ss_)], writes=[("rsn", ss_)])
                    fs = cn["f"] % 8
                    cn["f"] += 1
                    if qi == 0:
                        op("dve", STT(fm[:, fs, :], sil[:, h, :], SCALE, rsn[:, ss_, :], ALU.mult, ALU.mult),
                           reads=[sk, ("rsn", ss_)], writes=[("fm", fs)])
                        deferred.append((qn[h * 128:(h + 1) * 128, t0:t0 + GT], fm[:, fs, :], [("fm", fs)]))
                        return
                    op("dve", TTOP(fm[:, fs, :], sil[:, h, :], rsn[:, ss_, :], ALU.mult), reads=[sk, ("rsn", ss_)], writes=[("fm", fs)])
                    deferred.append((kn[h * 128:(h + 1) * 128, t0:t0 + GT], fm[:, fs, :], [("fm", fs)]))
                    src, srck = fm[:, fs, :], ("fm", fs)
                late.append((n, src, srck))

            def computeB2(n, src, srck):
                t0, gi, ng, qi, h = its[n]
                kind = 0 if qi == 1 else 1
                tb = cn["t"] % 2
                cn["t"] += 1
                pt, ptk = ptr[tb], ("ptr", tb)
                for c in range(4):
                    op("pe", TR(pt[:, c * 128:(c + 1) * 128], src[:, c * 128:(c + 1) * 128], ident[:, :]), reads=[srck, ("ident",)], writes=[ptk])
                evk = ("tk", kind, h)
                op("dve", CP(tk[:, kind, :, h * 128:(h + 1) * 128], pt[:, 0:512].rearrange("p (c d) -> p c d", c=4)),
                   reads=[ptk], writes=[evk])
                if h == 3:
                    dst = (ktok, vtok)[kind]
                    deferred.append((dst[t0:t0 + GT, :].rearrange("(c p) f -> p c f", p=128), tk[:, kind, :, :],
                                     [("tk", kind, hh) for hh in range(4)]))

            nit = len(its)
            load(0)
            if nit > 1:
                load(1)
            pending = []
            late_prev = []
            for blk in range(nit // 4):
                for n in range(blk * 4, blk * 4 + 4):
                    if n + 2 < nit:
                        load(n + 2)
                    computeA(n)
                for (n_, src_, srck_) in late_prev:
                    computeB2(n_, src_, srck_)
                for n in range(blk * 4, blk * 4 + 4):
                    computeB(n)
                late_prev = list(late)
                del late[:]
                for (dst_, src_, rd_) in pending:
                    self.dma("sp", dst_, src_, reads=rd_)
                pending = list(deferred)
                del deferred[:]
            for (n_, src_, srck_) in late_prev:
                computeB2(n_, src_, srck_)
            pending += list(deferred)
            for (dst_, src_, rd_) in pending:
                self.dma("sp", dst_, src_, reads=rd_)
            self.S.flush()

    def phase_dn(self, scr, pre, cst, of, ob, offset=0, ndummy=1):
        bg = scr["bg"]
        qn, kn, ktok, vtok = pre["qn"], pre["kn"], pre["ktok"], pre["vtok"]
        GT = 512
        IDX = dict(PA=(1, 3), PB=(2, 4), BD=5, M32=(8, 6), M64=(9, 7), TRI=(10, 11))
        SCALE = 128 ** -0.5
        with ExitStack() as st:
            C = self.sb(st, "C", [128, 12, 128], F32)
            self.dma("sp", C[:, :, :], cst.rearrange("n p f -> p n f"), writes=[("C",)])
            identf = C[:, 0, :]
            Cb = self.sb(st, "Cb", [128, 12, 128], BF16)
            self.op("dve", CP(Cb[:, :, :], C[:, :, :]), reads=[("C",)], writes=[("Cb",)])
            ident = Cb[:, 0, :]
            onesb = self.sb(st, "onesb", [128, 128], BF16)
            self.op("dve", MEMSET(onesb[:, :], 1.0), writes=[("onesb",)])
            onesf = self.sb(st, "onesf", [128, 128], F32)
            self.op("dve", MEMSET(onesf[:, :], 1.0), writes=[("onesf",)])
            def bc_col(a):
                return a.unsqueeze(2).to_broadcast([128, 4, 128])

            def bc_mat(a):
                return a.unsqueeze(1).to_broadcast([128, 4, 128])

            class Stream:
                pass

            streams = []
            for d in (0, 1):
                Z = Stream()
                Z.d = d
                n_ = "d%d_" % d
                Z.bgrow = self.sb(st, n_ + "bgrow", [16, GT], F32)
                Z.QTs = self.sb(st, n_ + "QT", [128, 2, 4, GT], BF16)
                Z.KTs = self.sb(st, n_ + "KT", [128, 2, 4, GT], BF16)
                Z.Ktoks = self.sb(st, n_ + "Ktok", [128, 2, 4, 512], BF16)
                Z.Vtoks = self.sb(st, n_ + "Vtok", [128, 2, 4, 512], BF16)
                Z.btoks = self.sb(st, n_ + "btoks", [128, 2, 4, 16], F32)
                Z.gslot = 0
                Z.cols = self.sb(st, n_ + "cols", [128, 2, 8, 4], F32)
                Z.ostg = self.sb(st, n_ + "ostg", [128, 2, 512], F32)
                Z.S = self.sb(st, n_ + "S", [128, 4, 128], F32)
                Z.Sd = self.sb(st, n_ + "Sd", [128, 4, 128], F32)
                Z.Sb = self.sb(st, n_ + "Sb", [128, 4, 128], BF16)
                Z.bufs = {}
                PAR2 = ("U", "Wt", "Aq", "qg", "kg")
                for nm in ("Gs", "T0", "Y1", "Y2", "E1", "E2", "Eg", "E1n", "U"):
                    Z.bufs[nm] = self.sb(st, n_ + nm, [128, 2 if nm in PAR2 else 1, 4, 128], F32)
                for nm in ("N", "N32", "N64", "Nd", "P0", "P1", "Q0", "Q1", "R", "Yb", "Xb", "tm", "vb", "kbg", "kg", "qg", "Aq", "Wt", "vn"):
                    Z.bufs[nm] = self.sb(st, n_ + nm, [128, 2 if nm in PAR2 else 1, 4, 128], BF16)
                Z.pA = self.ps(st, n_ + "pA", [128, 512], F32)
                Z.pB = self.ps(st, n_ + "pB", [128, 512], F32)
                Z.pC = self.ps(st, n_ + "pC", [128, 512], F32)
                Z.pT = Z.pC[:, :].bitcast(BF16)
                Z.PA = C[:, IDX["PA"][d], :]
                Z.PB = C[:, IDX["PB"][d], :]
                Z.BD = Cb[:, IDX["BD"], :]
                Z.M32 = Cb[:, IDX["M32"][1 - d], :]
                Z.M64 = Cb[:, IDX["M64"][1 - d], :]
                Z.TRI = C[:, IDX["TRI"][d], :]
                Z.oscr = (of, ob)[d]
                streams.append(Z)
            op = self.op
            pDum = self.ps(st, "pDum", [128, 512], F32)
            dsrc = self.sb(st, "dsrc", [128, 512], BF16)
            self.op("dve", MEMSET(dsrc[:, :], 0.001), writes=[("dsrc",)])

            DUMN = 128

            DUMEVERY = 2
            dcount = [0]

            def dummies(n=None):
                dcount[0] += 1
                if dcount[0] % DUMEVERY:
                    return
                for _ in range(ndummy if n is None else n):
                    op("pe", MM(pDum[:, 0:DUMN], onesb[:, :], dsrc[:, 0:DUMN], True, True), reads=[("dsrc",), ("onesb",)], writes=[("pDum",)])

            def K(Z, name, par=0):
                return ("dn", Z.d, name, par)

            def Bf(Z, name, par=0):
                return Z.bufs[name][:, par], K(Z, name, par)

            def v4(ps):
                return ps[:, :].rearrange("p (h c) -> p h c", h=4)

            def prefetch(Z, gi, ng, t0):
                d = Z.d
                gs = Z.gslot ^ 1
                pk = lambda nm: ("dn", d, "ps" + nm)
                self.dma("sp", Z.QTs[:, gs], qn[:, t0:t0 + GT].rearrange("(h p) t -> p h t", p=128), writes=[K(Z, "QT", gs)])
                self.dma("sp", Z.KTs[:, gs], kn[:, t0:t0 + GT].rearrange("(h p) t -> p h t", p=128), writes=[K(Z, "KT", gs)])
                self.dma("sp", Z.Ktoks[:, gs], ktok[t0:t0 + GT, :].rearrange("(c p) f -> p c f", p=128), writes=[K(Z, "Ktok", gs)])
                self.dma("sp", Z.Vtoks[:, gs], vtok[t0:t0 + GT, :].rearrange("(c p) f -> p c f", p=128), writes=[K(Z, "Vtok", gs)])
                self.dma("sp", Z.bgrow[:, :], bg[:, t0:t0 + GT], writes=[K(Z, "bgrow")])
                for c in range(4):
                    op("pe", TR(Z.pC[:, c * 16:(c + 1) * 16], Z.bgrow[0:16, c * 128:(c + 1) * 128], C[0:16, 0, 0:16]),
                       reads=[K(Z, "bgrow"), ("C",)], writes=[pk("C")])
                op("act", ACP(Z.btoks[:, gs], Z.pC[:, 0:64].rearrange("p (a b) -> p a b", a=4)), reads=[pk("C")], writes=[K(Z, "btok", gs)])

            def prologue(Z, gi, ng, t0):
                Z.gslot ^= 1
                gs = Z.gslot
                Z.QT = Z.QTs[:, gs]
                Z.KT = Z.KTs[:, gs]
                Z.Ktok = Z.Ktoks[:, gs].rearrange("p c (h d) -> p h c d", h=4)
                Z.Vtok = Z.Vtoks[:, gs].rearrange("p c (h d) -> p h c d", h=4)
                Z.btok = Z.btoks[:, gs]
                Z.gk = gs

            KGC, KGL, KNB, KEG, KBGE, KEGL, KKGC = range(7)

            def unit(Z, c):
                d = Z.d
                par = c % 2
                QT_, KT_, Ktok_, Vtok_, btok_, gk_ = Z.QT, Z.KT, Z.Ktok, Z.Vtok, Z.btok, Z.gk
                pk = lambda nm: ("dn", d, "ps" + nm)
                cs = slice(c * 128, (c + 1) * 128)
                gt4 = btok_[:, c, 8 + 4 * d:12 + 4 * d]
                be4 = btok_[:, c, 4 * d:4 * d + 4]
                cl = Z.cols[:, par]
                ck = K(Z, "cols", par)
                bk = K(Z, "btok", gk_)
                A4, B4, C4 = v4(Z.pA), v4(Z.pB), v4(Z.pC)
                T4 = Z.pT[:, 0:512].rearrange("p (h c) -> p h c", h=4)
                op("pe", MM(Z.pC[:, 0:4], Z.TRI, gt4, True, True), reads=[bk, ("C",)], writes=[pk("C")])
                op("pe", MM(Z.pC[:, 4:8], onesf[:, :], gt4, True, True), reads=[bk, ("onesf",)], writes=[pk("C")])
                op("act", ACP(cl[:, 0:2, :], Z.pC[:, 0:8].rearrange("p (a b) -> p a b", a=2)), reads=[pk("C")], writes=[ck])
                for h in range(4):
                    op("pe", MM(A4[:, h, :], gt4[:, h:h + 1].to_broadcast([128, 128]), Z.TRI, True, True), reads=[bk, ("C",)], writes=[pk("A")])
                dummies()
                yield
                op("dve", (lambda o_=cl[:, KNB, :], i_=be4: lambda e: e.tensor_scalar_mul(o_, i_, -1.0))(), reads=[bk], writes=[ck])
                op("act", ACTF(cl[:, KEG, :], cl[:, KGC, :], AF.Exp), reads=[ck], writes=[ck])
                op("act", ACTF(cl[:, KEGL, :], cl[:, KGL, :], AF.Exp), reads=[ck], writes=[ck])
                op("dve", TTOP(cl[:, KKGC, :], cl[:, KGL, :], cl[:, KGC, :], ALU.subtract), reads=[ck], writes=[ck])
                op("act", ACTF(cl[:, KKGC, :], cl[:, KKGC, :], AF.Exp), reads=[ck], writes=[ck])
                op("dve", TTOP(cl[:, KBGE, :], cl[:, KEG, :], be4, ALU.mult), reads=[ck, bk], writes=[ck])
                Gs, Gsk = Bf(Z, "Gs")
                op("act", ACP(Gs, A4), reads=[pk("A")], writes=[Gsk])
                op("pool", TTOP(Z.Sd[:, :, :], Z.S[:, :, :], bc_col(cl[:, KEGL, :]), ALU.mult), reads=[K(Z, "S"), ck], writes=[K(Z, "Sd")])
                dummies()
                yield
                T0, T0k = Bf(Z, "T0")
                Y1, Y1k = Bf(Z, "Y1")
                Y2, Y2k = Bf(Z, "Y2")
                Eg, Egk = Bf(Z, "Eg")
                E1, E1k = Bf(Z, "E1")
                E2, E2k = Bf(Z, "E2")
                op("act", ACTF(Eg, Gs, AF.Exp), reads=[Gsk], writes=[Egk])
                op("dve", TTOP(T0, Gs, bc_col(cl[:, KGC, :]), ALU.subtract), reads=[Gsk, ck], writes=[T0k])
                dummies()
                yield
                op("dve", TTOP(Y1, T0, bc_mat(Z.PA), ALU.add), reads=[T0k, ("C",)], writes=[Y1k])
                op("pool", TTOP(Y2, T0, bc_mat(Z.PB), ALU.add), reads=[T0k, ("C",)], writes=[Y2k])
                for h in range(4):
                    op("pe", MM(B4[:, h, :], KT_[:, h, cs], KT_[:, h, cs], True, True), reads=[K(Z, "KT", gk_)], writes=[pk("B")])
                for h in range(4):
                    op("pe", MM(C4[:, h, :], KT_[:, h, cs], QT_[:, h, cs], True, True), reads=[K(Z, "KT", gk_), K(Z, "QT", gk_)], writes=[pk("C")])
                dummies()
                yield
                op("act", ACTF(E1, Y1, AF.Exp, scale=-1.0), reads=[Y1k], writes=[E1k])
                op("act", ACTF(E2, Y2, AF.Exp), reads=[Y2k], writes=[E2k])
                qg, qgk = Bf(Z, "qg", par)
                op("pool", TTOP(qg, QT_[:, :, cs], Eg, ALU.mult), reads=[K(Z, "QT", gk_), Egk], writes=[qgk])
                dummies()
                yield
                E1n, E1nk = Bf(Z, "E1n")
                op("dve", TTOP(E1n, E1, bc_col(cl[:, KNB, :]), ALU.mult), reads=[E1k, ck], writes=[E1nk])
                N, Nk = Bf(Z, "N")
                op("dve", TTOP(N, B4, E1n, ALU.mult), reads=[pk("B"), E1nk], writes=[Nk])
                dummies()
                yield
                Nd, Ndk = Bf(Z, "Nd")
                op("dve", TTOP(Nd, N, bc_mat(Z.BD), ALU.mult), reads=[Nk, ("Cb",)], writes=[Ndk])
                N32, N32k = Bf(Z, "N32")
                N64, N64k = Bf(Z, "N64")
                op("pool", TTOP(N32, N, bc_mat(Z.M32), ALU.mult), reads=[Nk, ("Cb",)], writes=[N32k])
                op("pool", TTOP(N64, N, bc_mat(Z.M64), ALU.mult), reads=[Nk, ("Cb",)], writes=[N64k])
                Aq, Aqk = Bf(Z, "Aq", par)
                op("dve", TTOP(Aq, C4, E2, ALU.mult), reads=[pk("C"), E2k], writes=[Aqk])
                vb, vbk = Bf(Z, "vb")
                kbg, kbgk = Bf(Z, "kbg")
                kg, kgk = Bf(Z, "kg", par)
                op("pool", TTOP(vb, Vtok_[:, :, c, :], bc_col(be4), ALU.mult), reads=[K(Z, "Vtok", gk_), bk], writes=[vbk])
                op("pool", TTOP(kbg, Ktok_[:, :, c, :], bc_col(cl[:, KBGE, :]), ALU.mult), reads=[K(Z, "Ktok", gk_), ck], writes=[kbgk])
                op("pool", TTOP(kg, Ktok_[:, :, c, :], bc_col(cl[:, KKGC, :]), ALU.mult), reads=[K(Z, "Ktok", gk_), ck], writes=[kgk])
                dummies()
                yield
                for h in range(4):
                    op("pe", TR(T4[:, h, :], Nd[:, h, :], ident), reads=[Ndk, ("Cb",)], writes=[pk("C")])
                dummies()
                yield
                Q0, Q0k = Bf(Z, "Q0")
                op("act", ACP(Q0, T4), reads=[pk("C")], writes=[Q0k])
                dummies()
                yield
                R, Rk = Bf(Z, "R")
                op("dve", TTOP(R, Q0, bc_mat(ident), ALU.add), reads=[Q0k, ("Cb",)], writes=[Rk])
                Pc, Pck = Nd, Ndk
                Qc, Qck = Q0, Q0k
                for j in range(1, 5):
                    Pn, Pnk = Bf(Z, "P%d" % (j % 2))
                    for h in range(4):
                        op("pe", MM(A4[:, h, :], Qc[:, h, :], Pc[:, h, :], True, True), reads=[Qck, Pck], writes=[pk("A")])
                    if j < 4:
                        for h in range(4):
                            op("pe", MM(B4[:, h, :], Pc[:, h, :], Qc[:, h, :], True, True), reads=[Qck, Pck], writes=[pk("B")])
                    dummies()
                    yield
                    op("act", ACP(Pn, A4), reads=[pk("A")], writes=[Pnk])
                    if j < 4:
                        Qn, Qnk = Bf(Z, "Q%d" % (j % 2))
                        op("dve", CP(Qn, B4), reads=[pk("B")], writes=[Qnk])
                    dummies()
                    yield
                    for h in range(4):
                        op("pe", MM(C4[:, h, :], Pn[:, h, :], R[:, h, :], True, True), reads=[Pnk, Rk], writes=[pk("C")])
                    dummies()
                    yield
                    op("dve", TTOP(R, C4, R, ALU.add), reads=[pk("C"), Rk], writes=[Rk])
                    dummies()
                    yield
                    Pc, Pck = Pn, Pnk
                    if j < 4:
                        Qc, Qck = Qn, Qnk
                for (NM, NMk) in ((N32, N32k), (N64, N64k)):
                    Yb, Ybk = Bf(Z, "Yb")
                    Xb, Xbk = Bf(Z, "Xb")
                    for h in range(4):
                        op("pe", MM(A4[:, h, :], NM[:, h, :], R[:, h, :], True, True), reads=[NMk, Rk], writes=[pk("A")])
                    for h in range(4):
                        op("pe", TR(T4[:, h, :], R[:, h, :], ident), reads=[Rk, ("Cb",)], writes=[pk("C")])
                    dummies()
                    yield
                    op("act", ACP(Yb, A4), reads=[pk("A")], writes=[Ybk])
                    op("dve", CP(Xb, T4), reads=[pk("C")], writes=[Xbk])
                    dummies()
                    yield
                    for h in range(4):
                        op("pe", MM(B4[:, h, :], Xb[:, h, :], Yb[:, h, :], True, True), reads=[Xbk, Ybk], writes=[pk("B")])
                    dummies()
                    yield
                    op("dve", TTOP(R, B4, R, ALU.add), reads=[pk("B"), Rk], writes=[Rk])
                    dummies()
                    yield
                for h in range(4):
                    op("pe", MM(A4[:, h, :], R[:, h, :], vb[:, h, :], True, True), reads=[Rk, vbk], writes=[pk("A")])
                for h in range(4):
                    op("pe", MM(B4[:, h, :], kbg[:, h, :], R[:, h, :], True, True), reads=[Rk, kbgk], writes=[pk("B")])
                dummies()
                yield
                U, Uk = Bf(Z, "U", par)
                Wt, Wtk = Bf(Z, "Wt", par)
                op("act", ACP(U, A4), reads=[pk("A")], writes=[Uk])
                op("dve", CP(Wt, B4), reads=[pk("B")], writes=[Wtk])
                dummies()
                yield

            def scan(Z, c, t0):
                d = Z.d
                par = c % 2
                pk = lambda nm: ("dn", d, "ps" + nm)
                A4, B4, C4 = v4(Z.pA), v4(Z.pB), v4(Z.pC)
                U, Uk = Bf(Z, "U", par)
                Wt, Wtk = Bf(Z, "Wt", par)
                Aq, Aqk = Bf(Z, "Aq", par)
                qg, qgk = Bf(Z, "qg", par)
                kg, kgk = Bf(Z, "kg", par)
                vn, vnk = Bf(Z, "vn")
                for h in range(4):
                    op("pe", MM(A4[:, h, :], Wt[:, h, :], Z.Sb[:, h, :], True, True), reads=[Wtk, K(Z, "Sb")], writes=[pk("A")])
                dummies()
                yield
                op("dve", TTOP(vn, U, A4, ALU.subtract), reads=[Uk, pk("A")], writes=[vnk])
                dummies()
                yield
                for h in range(4):
                    op("pe", MM(C4[:, h, :], kg[:, h, :], vn[:, h, :], True, True), reads=[kgk, vnk], writes=[pk("C")])
                for h in range(4):
                    op("pe", MM(B4[:, h, :], qg[:, h, :], Z.Sb[:, h, :], True, False), reads=[qgk, K(Z, "Sb")], writes=[pk("B")])
                    op("pe", MM(B4[:, h, :], Aq[:, h, :], vn[:, h, :], False, True), reads=[Aqk, vnk], writes=[pk("B")])
                dummies()
                yield
                op("dve", TTOP(Z.S[:, :, :], Z.Sd[:, :, :], C4, ALU.add), reads=[K(Z, "Sd"), pk("C")], writes=[K(Z, "S")])
                op("act", ACP(Z.ostg[:, par, :], Z.pB[:, :]), reads=[pk("B")], writes=[K(Z, "ostg", par)])
                dummies()
                yield
                op("act", ACP(Z.Sb[:, :, :], Z.S[:, :, :]), reads=[K(Z, "S")], writes=[K(Z, "Sb")])
                tc = t0 + c * 128
                self.dma("sp", Z.oscr[tc:tc + 128, :], Z.ostg[:, par, :], reads=[K(Z, "ostg", par)])
                dummies()
                yield

            def par(*gens):
                gens = list(gens)
                while gens:
                    nxt = []
                    for g_ in gens:
                        try:
                            next(g_)
                            nxt.append(g_)
                        except StopIteration:
                            pass
                    gens = nxt
                    if gens:
                        yield

            def stream_gen(Z, L, t_base):
                ng = L // GT
                order = []
                for step in range(ng):
                    gi = step if Z.d == 0 else ng - 1 - step
                    for ci in range(4):
                        order.append((gi, ci if Z.d == 0 else 3 - ci))
                groups = []
                for (gi, c) in order:
                    if not groups or groups[-1] != gi:
                        groups.append(gi)
                prefetch(Z, groups[0], ng, t_base + groups[0] * GT)
                cur_g = None
                gpos = -1
                for (gi, c) in order:
                    t0 = t_base + gi * GT
                    first = gi != cur_g
                    if first:
                        prologue(Z, gi, ng, t0)
                        cur_g = gi
                        gpos += 1
                    yield from unit(Z, c)
                    if first and gpos + 1 < len(groups):
                        prefetch(Z, groups[gpos + 1], ng, t_base + groups[gpos + 1] * GT)
                    yield from scan(Z, c, t0)

            t_base = 0
            for L in self.seqs:
                for Z in streams:
                    op("pool", MEMSET(Z.S[:, :, :], 0.0), writes=[K(Z, "S")])
                    op("pool", MEMSET(Z.Sb[:, :, :], 0.0), writes=[K(Z, "Sb")])
                ga = stream_gen(streams[0], L, t_base)
                gb = stream_gen(streams[1], L, t_base)
                for _ in range(offset):
                    next(ga)
                for _ in par(ga, gb):
                    pass
                t_base += L
            self.S.flush()

    def phase_mem(self, mem, nw_mem, w_kv, ident_f32, kT, vv):
        nS = len(self.seqs)
        with ExitStack() as st:
            ident, _ = self.consts(st, ident_f32)
            wbc = self.bcast_vec(st, "wbc", nw_mem)
            Wkv = self.load_weight(st, "Wkv", w_kv, 8, 2 * D)
            ms = self.sb(st, "ms", [128, 2, D], F32)
            h = self.sb(st, "h", [128, 2, D], BF16)
            mT = self.sb(st, "mT", [128, 8, 256], BF16)
            junk = self.sb(st, "junk", [128, D], BF16)
            ss = self.sb(st, "ss", [128, 2], F32)
            rstd = self.sb(st, "rstd", [128, 2], F32)
            ost = self.sb(st, "ost", [128, 4, 512], BF16)
            pT = [self.ps(st, "pT%d" % i, [128, 8, 128], BF16) for i in range(2)]
            pP = [self.ps(st, "pP%d" % i, [128, 512], F32) for i in range(4)]
            cP = cO = 0
            for s_ in range(nS):
                self.dma("sp", ms[:, :, :], mem[s_ * 256:(s_ + 1) * 256, :].rearrange("(b p) d -> p b d", p=128), writes=[("ms",)])
                for b in range(2):
                    self.rmsnorm_tok(ms[:, b, :], ("ms",), wbc[:, :], junk[:, :], ss[:, b:b + 1], rstd[:, b:b + 1],
                                     h[:, b, :], ("h", b), ("ss", b))
                    p = pT[b]
                    for kc in range(8):
                        self.op("pe", TR(p[:, kc, :], h[:, b, kc * 128:(kc + 1) * 128], ident[:, :]), reads=[("h", b), ("ident",)], writes=[("pT", b)])
                    self.op("act", ACP(mT[:, :, b * 128:(b + 1) * 128], p[:, :, :]), reads=[("pT", b)], writes=[("mT", b)])
                mk = [("mT", 0), ("mT", 1)]
                for j in range(8):
                    p = pP[cP % 4]; pk = ("pP", cP % 4); cP += 1
                    for kc in range(8):
                        self.op("pe", MM(p[:, 0:256], Wkv[:, kc, j * 128:(j + 1) * 128], mT[:, kc, :], kc == 0, kc == 7),
                                reads=mk + [("Wkv", kc)], writes=[pk])
                    o = cO % 4; cO += 1
                    self.op("act", ACP(ost[:, o, 0:256], p[:, 0:256]), reads=[pk], writes=[("ost", o)])
                    self.dma("sp", kT[s_, j * 128:(j + 1) * 128, :], ost[:, o, 0:256], reads=[("ost", o)])
                for mb in range(2):
                    for nh in range(2):
                        p = pP[cP % 4]; pk = ("pP", cP % 4); cP += 1
                        for kc in range(8):
                            self.op("pe", MM(p[:, :], mT[:, kc, mb * 128:(mb + 1) * 128], Wkv[:, kc, D + nh * 512:D + (nh + 1) * 512], kc == 0, kc == 7),
                                    reads=mk + [("Wkv", kc)], writes=[pk])
                        o = cO % 4; cO += 1
                        self.op("dve", CP(ost[:, o, :], p[:, :]), reads=[pk], writes=[("ost", o)])
                        self.dma("sp", vv[s_, mb * 128:(mb + 1) * 128, nh * 512:(nh + 1) * 512], ost[:, o, :], reads=[("ost", o)])
            self.S.flush()

    def phase_mid(self, x, scr, of, ob, kT, vv, dn_nw, w_up_a, w_out, nw_xa, w_q, w_o, ident_f32, x2out, TT=512):
        T = self.T
        NB = TT // 128
        gdT, gaT, sbT = scr["gdT"], scr["gaT"], scr["sbT"]
        nS = len(self.seqs)
        with ExitStack() as st:
            ident, _ = self.consts(st, ident_f32)
            wbc = self.bcast_vec(st, "wbc", nw_xa)
            Wua = self.load_weight(st, "Wua", w_up_a, 4, D)
            Wout = self.load_weight(st, "Wout", w_out, 8, D)
            Wq = self.load_weight(st, "Wq", w_q, 8, D)
            Wo = self.load_weight(st, "Wo", w_o, 8, D)
            KT1 = self.sb(st, "KT", [128, 8, 256], BF16)
            VV1 = self.sb(st, "VV", [128, 2, D], BF16)
            nwc = self.sb(st, "nwc", [128, 1], F32)
            self.dma("sp", nwc[:, :], dn_nw.rearrange("(p o) -> p o", o=1), writes=[("nwc",)])
            onesb = self.sb(st, "onesb", [128, 128], BF16)
            self.op("dve", MEMSET(onesb[:, :], 1.0), writes=[("onesb",)])
            xs2 = self.sb(st, "xs", [128, 2, NB, D], F32)
            ofs = self.sb(st, "ofs", [128, NB, 512], F32)
            obs = self.sb(st, "obs", [128, NB, 512], F32)
            on = self.sb(st, "on", [128, NB, 512], BF16)
            gds = self.sb(st, "gds", [128, 4, TT], BF16)
            gas = self.sb(st, "gas", [128, 8, TT], BF16)
            sbs = self.sb(st, "sbs", [128, 8, TT], BF16)
            aT = self.sb(st, "aT", [128, 4, TT], BF16)
            mg = self.sb(st, "mg", [128, 8, TT], BF16)
            tmpm = self.sb(st, "tmpm", [128, 2, TT], BF16)
            ss16 = self.sb(st, "ss16", [128, 16], F32)
            rs16 = self.sb(st, "rs16", [128, 16], F32)
            h = self.sb(st, "h", [128, 4, D], BF16)
            hT = self.sb(st, "hT", [128, 8, TT], BF16)
            qT = self.sb(st, "qT", [128, 8, TT], BF16)
            pex = self.sb(st, "pex", [128, 2, 2, TT], BF16)
            rinv = self.sb(st, "rinv", [128, 1, TT], F32)
            oT = self.sb(st, "oT", [128, 8, TT], BF16)
            junk = self.sb(st, "junk", [128, D], BF16)
            ss = self.sb(st, "ss", [128, NB], F32)
            rstd = self.sb(st, "rstd", [128, NB], F32)
            pT = [self.ps(st, "pT%d" % i, [128, 8, 128], BF16) for i in range(2)]
            pP = [self.ps(st, "pP%d" % i, [128, 512], F32) for i in range(6)]
            cnt = dict(T=0, P=0)

            def nP():
                b = cnt["P"] % 6
                cnt["P"] += 1
                return pP[b], ("pP", b)

            def nT():
                b = cnt["T"] % 2
                cnt["T"] += 1
                return pT[b], ("pT", b)

            seq_of_tile = []
            for si, L in enumerate(self.seqs):
                seq_of_tile += [si] * (L // TT)
            ntiles = T // TT

            def load_x(i):
                sl = i % 2
                self.dma("sp", xs2[:, sl], x[i * TT:(i + 1) * TT, :].rearrange("(b p) d -> p b d", p=128), writes=[("xs", sl)])

            def load_scr(i):
                t0 = i * TT
                self.dma("sp", ofs[:, :, :], of[t0:t0 + TT, :].rearrange("(b p) d -> p b d", p=128), writes=[("ofs",)])
                self.dma("sp", obs[:, :, :], ob[t0:t0 + TT, :].rearrange("(b p) d -> p b d", p=128), writes=[("obs",)])
                self.dma("sp", gds[:, :, :], gdT[:, t0:t0 + TT].rearrange("(j p) t -> p j t", p=128), writes=[("gds",)])
                self.dma("sp", gas[:, :, :], gaT[:, t0:t0 + TT].rearrange("(j p) t -> p j t", p=128), writes=[("gas",)])
                self.dma("sp", sbs[:, :, :], sbT[:, t0:t0 + TT].rearrange("(j p) t -> p j t", p=128), writes=[("sbs",)])

            load_x(0)
            load_scr(0)
            cur_seq = -1
            cur = dict(seq=-1)

            def tile_ctx(i):
                return i * TT, seq_of_tile[i], xs2[:, i % 2], ("xs", i % 2)

            def stageBD(i):
                t0, si, xs, xk = tile_ctx(i)
                self.op("dve", TTOP(ofs[:, :, :], ofs[:, :, :], obs[:, :, :], ALU.add), reads=[("ofs",), ("obs",)], writes=[("ofs",)])
                for b_ in range(NB):
                    for hh_ in range(4):
                        self.op("act", ACTF(junk[:, 0:128], ofs[:, b_, hh_ * 128:(hh_ + 1) * 128], AF.Square,
                                            accum_out=ss16[:, b_ * 4 + hh_:b_ * 4 + hh_ + 1]),
                                reads=[("ofs",)], writes=[("junk",), ("ss16",)])
                self.op("act", ACTF(rs16[:, :], ss16[:, :], AF.Sqrt, bias=EPS, scale=1.0 / 128), reads=[("ss16",)], writes=[("rs16",)])
                self.op("dve", RECIP(rs16[:, :], rs16[:, :]), reads=[("rs16",)], writes=[("rs16",)])
                self.op("dve", TTOP(on[:, :, :].rearrange("p b (h e) -> p (b h) e", e=128),
                                    ofs[:, :, :].rearrange("p b (h e) -> p (b h) e", e=128),
                                    rs16[:, :].unsqueeze(2).to_broadcast([128, 16, 128]), ALU.mult),
                        reads=[("ofs",), ("rs16",)], writes=[("on",)])
                for hp in range(2):
                    p, pk = nT()
                    for hh in range(2):
                        hd = hp * 2 + hh
                        for b in range(NB):
                            self.op("pe", TR(p[:, hh * 4 + b, :], on[:, b, hd * 128:(hd + 1) * 128], ident[:, :]),
                                    reads=[("on",), ("ident",)], writes=[pk])
                    for hh in range(2):
                        hd = hp * 2 + hh
                        self.op("dve", STT(aT[:, hd, :], p[:, hh * 4:hh * 4 + 4, :].rearrange("p a b -> p (a b)"), nwc[:, 0:1], gds[:, hd, :], ALU.mult, ALU.mult),
                                reads=[pk, ("nwc",), ("gds",)], writes=[("aT", hd)])
                for j in range(8):
                    p, pk = nP()
                    for kc in range(4):
                        self.op("pe", MM(p[:, 0:TT], Wua[:, kc, j * 128:(j + 1) * 128], aT[:, kc, :], kc == 0, kc == 3),
                                reads=[("aT", kc), ("Wua", kc)], writes=[pk])
                    tb = j % 2
                    self.op("dve", TTOP(tmpm[:, tb, :], p[:, 0:TT], gas[:, j, :], ALU.mult), reads=[pk, ("gas",)], writes=[("tmpm", tb)])
                    self.op("dve", TTOP(mg[:, j, :], tmpm[:, tb, :], sbs[:, j, :], ALU.add),
                            reads=[("tmpm", tb), ("sbs",)], writes=[("mg", j)])

                if i + 1 < ntiles:
                    load_scr(i + 1)

            def stageE(i):
                t0, si, xs, xk = tile_ctx(i)
                if i + 1 < ntiles:
                    load_x(i + 1)
                self.resid_proj(xs, xk, mg, "mg", Wout, "Wout", 8, nP, NB)

            def stageFG(i):
                t0, si, xs, xk = tile_ctx(i)
                if si != cur["seq"]:
                    cur["seq"] = si
                    self.dma("sp", KT1[:, :, :], kT[si].rearrange("(j p) m -> p j m", p=128), writes=[("KT",)])
                    self.dma("sp", VV1[:, :, :], vv[si].rearrange("(b p) d -> p b d", p=128), writes=[("VV",)])
                for b in range(NB):
                    self.rmsnorm_tok(xs[:, b, :], xk, wbc[:, :], junk[:, :], ss[:, b:b + 1], rstd[:, b:b + 1],
                                     h[:, b, :], ("h", b), ("ss", b))
                for b in range(NB):
                    p, pk = nT()
                    for kc in range(8):
                        self.op("pe", TR(p[:, kc, :], h[:, b, kc * 128:(kc + 1) * 128], ident[:, :]), reads=[("h", b), ("ident",)], writes=[pk])
                    self.op("act", ACP(hT[:, :, b * 128:(b + 1) * 128], p[:, :, :]), reads=[pk], writes=[("hT", b)])
                hTk = [("hT", b) for b in range(NB)]
                for j in range(8):
                    p, pk = nP()
                    for kc in range(8):
                        self.op("pe", MM(p[:, 0:TT], Wq[:, kc, j * 128:(j + 1) * 128], hT[:, kc, :], kc == 0, kc == 7),
                                reads=hTk + [("Wq", kc)], writes=[pk])
                    self.op("act", ACTF(qT[:, j, :], p[:, 0:TT], AF.Copy, scale=1.0 / 16), reads=[pk], writes=[("qT", j)])
                def att_scores(hd):
                    ps_ = hd % 2
                    for mb in range(2):
                        p, pk = nP()
                        for dc in range(2):
                            self.op("pe", MM(p[:, 0:TT], KT1[:, 2 * hd + dc, mb * 128:(mb + 1) * 128], qT[:, 2 * hd + dc, :], dc == 0, dc == 1),
                                    reads=[("KT",), ("qT", 2 * hd + dc)], writes=[pk])
                        self.op("act", ACTF(pex[:, ps_, mb, :], p[:, 0:TT], AF.Exp), reads=[pk], writes=[("pex", ps_, mb)])

                def att_out(hd):
                    ps_ = hd % 2
                    p, pk = nP()
                    for mb in range(2):
                        self.op("pe", MM(p[:, 0:TT], onesb[:, :], pex[:, ps_, mb, :], mb == 0, mb == 1),
                                reads=[("onesb",), ("pex", ps_, mb)], writes=[pk])
                    self.op("act", ACTF(rinv[:, 0, :], p[:, 0:TT], AF.Ln), reads=[pk], writes=[("rinv", 0)])
                    self.op("act", ACTF(rinv[:, 0, :], rinv[:, 0, :], AF.Exp, scale=-1.0), reads=[("rinv", 0)], writes=[("rinv", 0)])
                    for dc in range(2):
                        p, pk = nP()
                        for mb in range(2):
                            self.op("pe", MM(p[:, 0:TT], VV1[:, mb, (2 * hd + dc) * 128:(2 * hd + dc + 1) * 128], pex[:, ps_, mb, :], mb == 0, mb == 1),
                                    reads=[("VV",), ("pex", ps_, mb)], writes=[pk])
                        self.op("dve", TTOP(oT[:, 2 * hd + dc, :], p[:, 0:TT], rinv[:, 0, :], ALU.mult), reads=[pk, ("rinv", 0)], writes=[("oT", 2 * hd + dc)])

                att_scores(0)
                for hd in range(4):
                    if hd + 1 < 4:
                        att_scores(hd + 1)
                    att_out(hd)
                self.resid_proj(xs, xk, oT, "oT", Wo, "Wo", 8, nP, NB)
                self.dma("sp", x2out[t0:t0 + TT, :].rearrange("(b p) d -> p b d", p=128), xs, reads=[xk])

            stageBD(0)
            for i in range(ntiles):
                stageE(i)
                if i + 1 < ntiles:
                    stageBD(i + 1)
                stageFG(i)
            self.S.flush()

    def resid_proj(self, xs, xkey, aT, aname, W, wname, KC, nP, NB):
        for b in range(NB):
            for nh in range(2):
                p, pk = nP()
                for kc in range(KC):
                    self.op("pe", MM(p[:, :], aT[:, kc, b * 128:(b + 1) * 128], W[:, kc, nh * 512:(nh + 1) * 512], kc == 0, kc == KC - 1),
                            reads=[(aname, kc), (wname, kc)], writes=[pk])
                xsl = xs[:, b, nh * 512:(nh + 1) * 512]
                self.op("dve", TTOP(xsl, xsl, p[:, :], ALU.add), reads=[pk, xkey], writes=[xkey])


def make_consts():
    p = np.arange(128)[:, None]; f = np.arange(128)[None, :]
    BIG = 30000.0
    c = np.zeros((12, 128, 128), np.float32)
    c[0] = (p == f)
    c[1] = np.where(f >= p, BIG, 0.0)
    c[2] = np.where(f < p, -BIG, 0.0)
    c[3] = np.where(f <= p, BIG, 0.0)
    c[4] = np.where(f > p, -BIG, 0.0)
    c[5] = (p // 32 == f // 32)
    c[6] = (p // 64 == f // 64) & ((p % 64) // 32 == 1) & ((f % 64) // 32 == 0)
    c[7] = (p // 64 == 1) & (f // 64 == 0)
    c[8] = c[6].T
    c[9] = c[7].T
    c[10] = (p <= f)
    c[11] = (p >= f)
    return c


SEQS = (8192, 2048, 2048)
NCORES = 8
WNAMES = ["norm_mix_w", "w_in", "conv_w", "dn_a_log", "dn_dt_bias", "dn_norm_w", "w_up_a", "sg_ln_w", "sg_ln_b",
          "sg_w", "sg_b", "w_up_b", "w_out", "norm_xa_w", "norm_mem_w", "xa_w_q", "xa_w_kv", "xa_w_o",
          "norm_ffn_w", "ffn_w_gate_up", "ffn_w_down", "final_norm_w"]


def build_program(seqs=SEQS, wshapes=None):
    k = KB(seqs, debug=False)
    T = k.T
    nS = len(seqs)
    x = k.din("x", [T, D])
    mem = k.din("mem", [nS * 256, D])
    W = {n: k.din(n, list(wshapes[n])) for n in WNAMES}
    idf = k.din("idf", [128, 128])
    cst = k.din("cst", [12, 128, 128])
    y = k.dout("y", [T, D])
    scr = dict(qkvT=k.dscr("qkvT", [1536, T], BF16), bg=k.dscr("bg", [16, T]), gdT=k.dscr("gdT", [512, T], BF16),
               gaT=k.dscr("gaT", [1024, T], BF16), sbT=k.dscr("sbT", [1024, T], BF16))
    of = k.dscr("of", [T, 512])
    ob = k.dscr("ob", [T, 512])
    kT = k.dscr("kT", [nS, 1024, 256], BF16)
    vv = k.dscr("vv", [nS, 256, 1024], BF16)
    k.phase_mem(mem, W["norm_mem_w"], W["xa_w_kv"], idf, kT, vv)
    k.phase_in(x, W["w_in"], W["norm_mix_w"], W["sg_ln_w"], W["sg_ln_b"], W["sg_w"], W["sg_b"], W["w_up_b"],
               W["dn_a_log"], W["dn_dt_bias"], idf, scr)
    pre = dict(qn=k.dscr("qn", [512, T], BF16), kn=k.dscr("kn", [512, T], BF16),
               ktok=k.dscr("ktok", [T, 512], BF16), vtok=k.dscr("vtok", [T, 512], BF16))
    k.phase_prep(scr, W["conv_w"], cst, pre)
    k.phase_dn(scr, pre, cst, of, ob)
    k.phase_mid(x, scr, of, ob, kT, vv, W["dn_norm_w"], W["w_up_a"], W["w_out"], W["norm_xa_w"], W["xa_w_q"], W["xa_w_o"], idf, y)
    k.phase_ffn(y, y, W["ffn_w_gate_up"], W["ffn_w_down"], W["norm_ffn_w"], W["final_norm_w"], idf)
    k.es.close()
    return k


def kernel(**inputs):
    f32 = np.float32
    xp = np.asarray(inputs["x_prompt"], dtype=f32)
    xsm = np.asarray(inputs["x_sample"], dtype=f32)
    mp = np.asarray(inputs["mem_prompt"], dtype=f32)
    msm = np.asarray(inputs["mem_sample"], dtype=f32)
    w = {}
    for n in WNAMES:
        a = np.asarray(inputs[n], dtype=f32)
        if n != "final_norm_w":
            a = a[0]
        if n in ("dn_a_log", "dn_dt_bias"):
            a = a.reshape(8)
        w[n] = np.ascontiguousarray(a)
    k = build_program(SEQS, {n: w[n].shape for n in WNAMES})
    idf = np.eye(128, dtype=f32)
    cst = make_consts()
    in_maps = []
    for c in range(NCORES):
        m = dict(w)
        m["x"] = np.ascontiguousarray(np.concatenate([xp[c], xsm[2 * c], xsm[2 * c + 1]], 0))
        m["mem"] = np.ascontiguousarray(np.concatenate([mp[c], msm[2 * c], msm[2 * c + 1]], 0))
        m["idf"] = idf
        m["cst"] = cst
        in_maps.append(m)
    res = run_bass_kernel_spmd(k.nc, in_maps, core_ids=list(range(NCORES)))
    yp = np.empty(xp.shape, f32)
    ys = np.empty(xsm.shape, f32)
    for c in range(NCORES):
        y = np.asarray(res.results[c]["y"], dtype=f32)
        yp[c] = y[:8192]
        ys[2 * c] = y[8192:10240]
        ys[2 * c + 1] = y[10240:12288]
    return (yp, ys)
```

```python
import numpy as np
import concourse.bass as bass
import concourse.mybir as mybir
from concourse.bass_utils import run_bass_kernel_spmd
from contextlib import ExitStack

F32 = mybir.dt.float32
BF16 = mybir.dt.bfloat16
AF = mybir.ActivationFunctionType
ALU = mybir.AluOpType
AX = mybir.AxisListType

COMPUTE = ("pe", "act", "dve", "pool")
ALLENG = ("pe", "act", "dve", "pool", "sp")
SAME_ENGINE_SYNC = False
SAME_ENGINE_RAW = True


class Sched:
    def __init__(self, nc, es, n_dma_sems=12):
        self.nc = nc
        self.n_dma_sems = n_dma_sems
        self.csem = {e: es.enter_context(nc.semaphore("c_" + e)) for e in COMPUTE}
        self.dsem = [es.enter_context(nc.semaphore("d_%d" % s)) for s in range(n_dma_sems)]
        self.cnt = {e: 0 for e in COMPUTE}
        self.dma_cnt = [0] * n_dma_sems
        self.dma_rr = 0
        self.dma_rr_sw = 0
        self.seen = {e: {f: 0 for f in COMPUTE} for e in ALLENG}
        self.seen_d = {e: [0] * n_dma_sems for e in ALLENG}
        self.n_ops = 0
        self.n_waits = 0
        self._reset()

    def _reset(self):
        self.ops = []
        self.lastw = {}
        self.readers = {}

    def op(self, eng, fn, reads=(), writes=(), dma=False, strict=False):
        idx = len(self.ops)
        deps = set()
        raw = set()
        for r in reads:
            w = self.lastw.get(r)
            if w is not None:
                deps.add(w)
                raw.add(w)
        for k in writes:
            w = self.lastw.get(k)
            if w is not None:
                deps.add(w)
            rs = self.readers.get(k)
            if rs:
                deps.update(rs.values())
        rkey = ("dma", idx) if dma else eng
        for r in reads:
            self.readers.setdefault(r, {})[rkey] = idx
        for k in writes:
            self.lastw[k] = idx
            self.readers[k] = {}
        o = dict(eng=eng, fn=fn, deps=deps, raw=raw, dma=dma, sig=False, strict=strict)
        if dma:
            nsw = 4
            if eng == "pool":
                s = self.n_dma_sems - nsw + self.dma_rr_sw
                self.dma_rr_sw = (self.dma_rr_sw + 1) % nsw
            else:
                s = self.dma_rr
                self.dma_rr = (self.dma_rr + 1) % (self.n_dma_sems - nsw)
            self.dma_cnt[s] += 1
            o["dsem"] = s
            o["dval"] = 16 * self.dma_cnt[s]
        self.ops.append(o)
        return idx

    def flush(self):
        nc = self.nc
        ops = self.ops
        if not ops:
            return
        last = {}
        for i, o in enumerate(ops):
            if not o["dma"]:
                last[o["eng"]] = i
        for e, i in last.items():
            ops[i]["sig"] = True
        for o in ops:
            for d in o["deps"]:
                od = ops[d]
                if not od["dma"] and (od["eng"] != o["eng"] or SAME_ENGINE_SYNC or o["strict"]
                                      or (SAME_ENGINE_RAW and d in o["raw"] and o["eng"] != "pe")):
                    od["sig"] = True
        for o in ops:
            if not o["dma"] and o["sig"]:
                self.cnt[o["eng"]] += 1
                o["sval"] = self.cnt[o["eng"]]
        streams = {e: [] for e in ALLENG}
        seen, seen_d = self.seen, self.seen_d
        for o in ops:
            e = o["eng"]
            wc, wd = {}, {}
            for d in o["deps"]:
                od = ops[d]
                if od["dma"]:
                    wd[od["dsem"]] = max(wd.get(od["dsem"], 0), od["dval"])
                else:
                    f = od["eng"]
                    if f == e and not (SAME_ENGINE_SYNC or o["strict"] or (SAME_ENGINE_RAW and d in o["raw"] and e != "pe")):
                        continue
                    wc[f] = max(wc.get(f, 0), od["sval"])
            if o["dma"]:
                prev = o["dval"] - 16
                if prev > 0:
                    wd[o["dsem"]] = max(wd.get(o["dsem"], 0), prev)
            wl = []
            for f, v in wc.items():
                if v > seen[e][f]:
                    seen[e][f] = v
                    wl.append((self.csem[f], v))
            for s, v in wd.items():
                if v > seen_d[e][s]:
                    seen_d[e][s] = v
                    wl.append((self.dsem[s], v))
            streams[e].append((o, wl))
        end_wl = {}
        for e in ALLENG:
            wl = []
            for f in COMPUTE:
                v = self.cnt[f]
                if f != e and v > seen[e][f]:
                    seen[e][f] = v
                    wl.append((self.csem[f], v))
            for s in range(self.n_dma_sems):
                v = 16 * self.dma_cnt[s]
                if v > seen_d[e][s]:
                    seen_d[e][s] = v
                    wl.append((self.dsem[s], v))
            end_wl[e] = wl
        self.n_ops += len(ops)
        self.n_waits += sum(len(wl) for st in streams.values() for _, wl in st)
        csem, dsem = self.csem, self.dsem

        def run(eng_name, engine):
            for o, wl in streams[eng_name]:
                for sem, v in wl:
                    engine.wait_ge(sem, v)
                ins = o["fn"](engine)
                if o["dma"]:
                    ins.then_inc(dsem[o["dsem"]], 16)
                elif o["sig"]:
                    ins.then_inc(csem[eng_name], 1)
            for sem, v in end_wl[eng_name]:
                engine.wait_ge(sem, v)

        with nc.Block() as block:
            @block.tensor
            def _(eng):
                run("pe", eng)

            @block.scalar
            def _(eng):
                run("act", eng)

            @block.vector
            def _(eng):
                run("dve", eng)

            @block.gpsimd
            def _(eng):
                run("pool", eng)

            @block.sync
            def _(eng):
                run("sp", eng)
        self._reset()


D = 1024
DFF = 2816
NIN = 5136
EPS = 1e-6


def MM(out, lhsT, rhs, start, stop):
    return lambda e: e.matmul(out, lhsT, rhs, start=bool(start), stop=bool(stop))


def TR(out, in_, ident):
    return lambda e: e.transpose(out, in_, ident)


def ACTF(out, in_, func, bias=None, scale=None, accum_out=None):
    kw = {}
    if bias is not None:
        kw["bias"] = bias
    if scale is not None:
        kw["scale"] = scale
    if accum_out is not None:
        kw["accum_out"] = accum_out
    return lambda e: e.activation(out, in_, func, **kw)


def ACP(out, in_):
    return lambda e: e.copy(out, in_)


def SQRT(out, in_):
    return lambda e: e.sqrt(out, in_)


def CP(out, in_):
    return lambda e: e.tensor_copy(out, in_)


def RECIP(out, in_):
    return lambda e: e.reciprocal(out, in_)


def TTOP(out, a, b, op):
    return lambda e: e.tensor_tensor(out, a, b, op)


def TS(out, in0, s1, s2, op0, op1):
    return lambda e: e.tensor_scalar(out, in0, s1, s2, op0, op1)


def STT(out, in0, scalar, in1, op0, op1):
    return lambda e: e.scalar_tensor_tensor(out, in0, scalar, in1, op0, op1)


def MEMSET(ap, v):
    return lambda e: e.memset(ap, v)


class KB:
    def __init__(self, seqs, debug=False):
        self.seqs = list(seqs)
        self.T = sum(seqs)
        self.debug = debug
        self.nc = bass.Bass("TRN2", target_bir_lowering=False)
        self.es = ExitStack()
        self.S = Sched(self.nc, self.es)
        self.dram = {}
        self.rr = 0

    def din(self, name, shape, dt=F32):
        t = self.nc.dram_tensor(name, list(shape), dt, kind="ExternalInput").ap()
        self.dram[name] = t
        return t

    def dout(self, name, shape, dt=F32):
        t = self.nc.dram_tensor(name, list(shape), dt, kind="ExternalOutput").ap()
        self.dram[name] = t
        return t

    def dscr(self, name, shape, dt=F32):
        kind = "ExternalOutput" if self.debug else "Internal"
        t = self.nc.dram_tensor(name, list(shape), dt, kind=kind).ap()
        self.dram[name] = t
        return t

    def sb(self, st, name, shape, dt):
        self.uid = getattr(self, "uid", 0) + 1
        return st.enter_context(self.nc.sbuf_tensor("%s_%d" % (name, self.uid), list(shape), dt))

    def ps(self, st, name, shape, dt):
        self.uid = getattr(self, "uid", 0) + 1
        return st.enter_context(self.nc.psum_tensor("%s_%d" % (name, self.uid), list(shape), dt))

    def op(self, *a, **k):
        return self.S.op(*a, **k)

    def dma(self, q, out, in_, reads=(), writes=()):
        return self.S.op(q, lambda e: e.dma_start(out=out, in_=in_), reads=reads, writes=writes, dma=True)

    def tap(self, name, ap, reads, dt=F32):
        if not self.debug:
            return
        t = self.dout(name, list(ap.shape), dt)
        self.dma("sp", t, ap, reads=reads)

    def cast_eng(self):
        self.rr += 1
        return ("dve", "pool", "act")[self.rr % 3]

    def copy(self, eng, out, in_, reads, writes):
        if eng == "act":
            return self.op("act", ACP(out, in_), reads=reads, writes=writes)
        return self.op(eng, CP(out, in_), reads=reads, writes=writes)

    def load_weight(self, st, name, src2d, KC, N, stage=None):
        w = self.sb(st, name, [128, KC, N], BF16)
        for kc in range(KC):
            self.dma("pool", w[:, kc, :], src2d[kc * 128:(kc + 1) * 128, :], writes=[(name, kc)])
        return w

    def rmsnorm_tok(self, xb, xkey, wbc, junk, ss1, rs1, out, outkey, skey, wkey=("wbc",)):
        self.op("act", ACTF(junk, xb, AF.Square, accum_out=ss1), reads=[xkey], writes=[("junk",), skey])
        self.op("act", ACTF(rs1, ss1, AF.Sqrt, bias=EPS, scale=1.0 / D), reads=[skey], writes=[skey + ("r",)], strict=True)
        self.op("dve", RECIP(rs1, rs1), reads=[skey + ("r",)], writes=[skey + ("r",)])
        self.op("dve", STT(out, xb, rs1, wbc, ALU.mult, ALU.mult), reads=[xkey, skey + ("r",), wkey], writes=[outkey], strict=True)

    def consts(self, st, ident_f32):
        ident = self.sb(st, "ident", [128, 128], BF16)
        idf = self.sb(st, "identf", [128, 128], F32)
        self.dma("sp", idf[:, :], ident_f32, writes=[("idf",)])
        self.op("dve", CP(ident[:, :], idf[:, :]), reads=[("idf",)], writes=[("ident",)])
        return ident, idf

    def bcast_vec(self, st, name, vec):
        n = vec.shape[0]
        t = self.sb(st, name, [128, n], F32)
        self.dma("sp", t[:, :], vec.partition_broadcast(128), writes=[(name,)])
        return t

    def phase_ffn(self, xin, yout, w_gu, w_dn, nw_ffn, nw_fin, ident_f32, TT=256):
        T = self.T
        NB = TT // 128
        KF = DFF // 128
        with ExitStack() as st:
            ident, _ = self.consts(st, ident_f32)
            wbc = self.bcast_vec(st, "wbc", nw_ffn)
            wbc2 = self.bcast_vec(st, "wbc2", nw_fin)
            Wgu = self.load_weight(st, "Wgu", w_gu, 8, 2 * DFF)
            Wd = self.load_weight(st, "Wd", w_dn, KF, D)
            NX = 3
            xs = self.sb(st, "xs", [128, NX, NB, D], F32)
            h = self.sb(st, "h", [128, 2, D], BF16)
            hT = self.sb(st, "hT", [128, 2, 8, TT], BF16)
            act = self.sb(st, "act", [128, KF, TT], BF16)
            sg = self.sb(st, "sg", [128, 3, TT], F32)
            junk = self.sb(st, "junk", [128, D], BF16)
            ss = self.sb(st, "ss", [128, NX, NB], F32)
            rstd = self.sb(st, "rstd", [128, NX, NB], F32)
            ss2 = self.sb(st, "ss2", [128, NX, NB], F32)
            rstd2 = self.sb(st, "rstd2", [128, NX, NB], F32)
            pT = [self.ps(st, "pT%d" % i, [128, 8, 128], BF16) for i in range(2)]
            pG = [self.ps(st, "pG%d" % i, [128, 512], F32) for i in range(2)]
            pU = [self.ps(st, "pU%d" % i, [128, 512], F32) for i in range(2)]
            pD = [self.ps(st, "pD%d" % i, [128, 512], F32) for i in range(2)]
            ntiles = T // TT
            cT = cG = cD = 0

            def load(i):
                sl = i % NX
                src = xin[i * TT:(i + 1) * TT, :].rearrange("(b p) d -> p b d", p=128)
                self.dma("sp", xs[:, sl], src, writes=[("xs", sl)])

            def normA(i):
                sl = i % NX
                for b in range(NB):
                    hb = (i * NB + b) % 2
                    self.rmsnorm_tok(xs[:, sl, b, :], ("xs", sl), wbc[:, :], junk[:, :], ss[:, sl, b:b + 1],
                                     rstd[:, sl, b:b + 1], h[:, hb, :], ("h", hb), ("ss", sl, b))

            def normB(i):
                nonlocal cT
                hs = i % 2
                for b in range(NB):
                    hb = (i * NB + b) % 2
                    p = pT[cT % 2]
                    pk = ("pT", cT % 2)
                    cT += 1
                    for kc in range(8):
                        self.op("pe", TR(p[:, kc, :], h[:, hb, kc * 128:(kc + 1) * 128], ident[:, :]),
                                reads=[("h", hb), ("ident",)], writes=[pk])
                    self.op("act", ACP(hT[:, hs, :, b * 128:(b + 1) * 128], p[:, :, :]),
                            reads=[pk], writes=[("hT", hs, b)])

            load(0)
            if ntiles > 1:
                load(1)
            normA(0)
            normB(0)
            for i in range(ntiles):
                if i + 2 < ntiles:
                    load(i + 2)
                sl = i % NX
                hs = i % 2
                if i + 1 < ntiles:
                    normA(i + 1)
                hTk = [("hT", hs, b) for b in range(NB)]
                if i == 0:
                    self.tap("t_h", h[:, :, :], [("h", 0), ("h", 1)], BF16)
                    self.tap("t_hT", hT[:, hs], hTk, BF16)
                for j in range(KF):
                    g = pG[cG % 2]
                    u = pU[cG % 2]
                    gk = ("pG", cG % 2)
                    uk = ("pU", cG % 2)
                    sgs = cG % 3
                    cG += 1
                    for kc in range(8):
                        self.op("pe", MM(g[:, 0:TT], Wgu[:, kc, j * 128:(j + 1) * 128], hT[:, hs, kc, :], kc == 0, kc == 7),
                                reads=hTk + [("Wgu", kc)], writes=[gk])
                    for kc in range(8):
                        self.op("pe", MM(u[:, 0:TT], Wgu[:, kc, DFF + j * 128:DFF + (j + 1) * 128], hT[:, hs, kc, :], kc == 0, kc == 7),
                                reads=hTk + [("Wgu", kc)], writes=[uk])
                    self.op("act", ACTF(sg[:, sgs, :], g[:, 0:TT], AF.Silu), reads=[gk], writes=[("sg", sgs)])
                    if i == 0 and j == 0 and self.debug:
                        gcp = self.sb(st, "gcp", [128, 2, TT], F32)
                        self.op("act", ACP(gcp[:, 0, :], g[:, 0:TT]), reads=[gk], writes=[("gcp",)])
                        self.op("dve", CP(gcp[:, 1, :], u[:, 0:TT]), reads=[uk], writes=[("gcp2",)])
                        self.tap("t_g", gcp[:, :, :], [("gcp",), ("gcp2",)])
                        self.tap("t_sg", sg[:, sgs, :], [("sg", sgs)])
                    self.op("dve", TTOP(act[:, j, :], sg[:, sgs, :], u[:, 0:TT], ALU.mult),
                            reads=[uk, ("sg", sgs)], writes=[("act", j)])
                if i == 0:
                    self.tap("t_act", act[:, :, :], [("act", j) for j in range(KF)], BF16)
                if i + 1 < ntiles:
                    normB(i + 1)
                for b in range(NB):
                    for nh in range(2):
                        pd = pD[cD % 2]
                        dk = ("pD", cD % 2)
                        cD += 1
                        for kc in range(KF):
                            self.op("pe", MM(pd[:, :], act[:, kc, b * 128:(b + 1) * 128], Wd[:, kc, nh * 512:(nh + 1) * 512],
                                             kc == 0, kc == KF - 1),
                                    reads=[("act", kc), ("Wd", kc)], writes=[dk])
                        xsl = xs[:, sl, b, nh * 512:(nh + 1) * 512]
                        self.op("dve", TTOP(xsl, xsl, pd[:, :], ALU.add), reads=[dk, ("xs", sl)], writes=[("xs", sl)])
                    self.rmsnorm_tok(xs[:, sl, b, :], ("xs", sl), wbc2[:, :], junk[:, :], ss2[:, sl, b:b + 1],
                                     rstd2[:, sl, b:b + 1], xs[:, sl, b, :], ("xs", sl), ("fs", sl, b), wkey=("wbc2",))
                dst = yout[i * TT:(i + 1) * TT, :].rearrange("(b p) d -> p b d", p=128)
                self.dma("pool", dst, xs[:, sl], reads=[("xs", sl)])
            self.S.flush()

    def phase_in(self, x, w_in, nw_mix, ln_w, ln_b, sg_w, sg_b, w_up_b, a_log, dt_bias, ident_f32, scr, TT=512):
        T = self.T
        NB = TT // 128
        qkvT, bg, gdT, gaT, sbT = scr["qkvT"], scr["bg"], scr["gdT"], scr["gaT"], scr["sbT"]
        with ExitStack() as st:
            ident, _ = self.consts(st, ident_f32)
            wbc = self.bcast_vec(st, "wbc", nw_mix)
            Win = self.load_weight(st, "Win", w_in, 8, NIN)
            Wub = self.load_weight(st, "Wub", w_up_b, 4, D)
            wsn = self.sb(st, "wsn", [128, 4, 128], BF16)
            self.dma("pool", wsn[:, :, :], sg_w.rearrange("g t s -> t g s"), writes=[("wsn",)])
            WsT = self.sb(st, "WsT", [128, 4, 128], BF16)
            bsrow = self.sb(st, "bsrow", [1, 512], BF16)
            self.dma("pool", bsrow[:, :], sg_b.rearrange("(o g) t -> o (g t)", o=1), writes=[("bsrow",)])
            ones1 = self.sb(st, "ones1", [1, 128], BF16)
            self.op("dve", MEMSET(ones1[:, :], 1.0), writes=[("ones1",)])
            onesb = self.sb(st, "onesb", [128, 128], BF16)
            self.op("dve", MEMSET(onesb[:, :], 1.0), writes=[("onesb",)])
            lnw = self.sb(st, "lnw", [128, 4], F32)
            lnb = self.sb(st, "lnb", [128, 4], F32)
            self.op("sp", lambda e: e.dma_start(out=lnw[:, :], in_=ln_w.rearrange("(g p) -> p g", p=128), allow_slow_non_contiguous=True),
                    writes=[("lnw",)], dma=True)
            self.op("sp", lambda e: e.dma_start(out=lnb[:, :], in_=ln_b.rearrange("(g p) -> p g", p=128), allow_slow_non_contiguous=True),
                    writes=[("lnb",)], dma=True)
            alc = self.sb(st, "alc", [8, 1], F32)
            dtb = self.sb(st, "dtb", [8, 1], F32)
            negA = self.sb(st, "negA", [8, 1], F32)
            self.dma("sp", alc[:, :], a_log.rearrange("(p o) -> p o", o=1), writes=[("alc",)])
            self.dma("sp", dtb[:, :], dt_bias.rearrange("(p o) -> p o", o=1), writes=[("dtb",)])
            self.op("act", ACTF(negA[:, :], alc[:, :], AF.Exp), reads=[("alc",)], writes=[("negA",)])
            self.op("dve", TS(negA[:, :], negA[:, :], -1.0, None, ALU.mult, ALU.bypass) if False else
                    (lambda e: e.tensor_scalar_mul(negA[:, :], negA[:, :], -1.0)), reads=[("negA",)], writes=[("negA",)])
            xs = self.sb(st, "xs", [128, NB, D], F32)
            h = self.sb(st, "h", [128, 4, D], BF16)
            hT = self.sb(st, "hT", [128, 2, 8, TT], BF16)
            junk = self.sb(st, "junk", [128, D], BF16)
            ss = self.sb(st, "ss", [128, NB], F32)
            rstd = self.sb(st, "rstd", [128, NB], F32)
            ost = self.sb(st, "ost", [128, 4, TT], BF16)
            bst = self.sb(st, "bst", [8, 2, TT], F32)
            est = self.sb(st, "est", [8, TT], F32)
            uT = self.sb(st, "uT", [128, 4, TT], BF16)
            vT = self.sb(st, "vT", [128, 4, TT], F32)
            vb = self.sb(st, "vb", [128, 4, TT], BF16)
            vq = self.sb(st, "vq", [128, 4, TT], BF16)
            gbT = self.sb(st, "gbT", [128, 8, TT], BF16)
            vn = self.sb(st, "vn", [128, 4, TT], BF16)
            vtmp = self.sb(st, "vtmp", [128, 2, TT], F32)
            vtok = self.sb(st, "vtok", [128, 4, NB, 128], BF16)
            ubT = self.sb(st, "ubT", [128, 4, TT], BF16)
            mu = self.sb(st, "mu", [128, TT], F32)
            msq = self.sb(st, "msq", [128, TT], F32)
            lrs = self.sb(st, "lrs", [128, TT], F32)
            pT = [self.ps(st, "pT%d" % i, [128, 8, 128], BF16) for i in range(2)]
            pP = [self.ps(st, "pP%d" % i, [128, 512], F32) for i in range(4)]
            pS = [self.ps(st, "pS%d" % i, [128, 512], F32) for i in range(2)]
            for g in range(4):
                self.op("pe", TR(pT[0][:, g, :], wsn[:, g, :], ident[:, :]), reads=[("wsn",), ("ident",)], writes=[("pT", 0)])
            self.op("act", ACP(WsT[:, :, :], pT[0][:, 0:4, :]), reads=[("pT", 0)], writes=[("WsT",)])
            ntiles = T // TT
            cnt = dict(T=0, P=0, O=0, S=0)

            def load(i):
                for b in range(NB):
                    self.dma("sp", xs[:, b, :], x[i * TT + b * 128:i * TT + (b + 1) * 128, :], writes=[("xs", b)])

            def proj(i, hs, c0, M):
                b = cnt["P"] % 4
                cnt["P"] += 1
                p = pP[b]
                for kc in range(8):
                    self.op("pe", MM(p[0:M, 0:TT], Win[:, kc, c0:c0 + M], hT[:, hs, kc, :], kc == 0, kc == 7),
                            reads=[("hT", hs, bb) for bb in range(NB)] + [("Win", kc)], writes=[("pP", b)])
                return p[0:M, 0:TT], ("pP", b)

            def store(dst, src_ap, key):
                self.dma("sp", dst, src_ap, reads=[key])

            def ostage():
                o = cnt["O"] % 4
                cnt["O"] += 1
                return ost[:, o, :], ("ost", o)

            def normA(i):
                for b in range(NB):
                    self.rmsnorm_tok(xs[:, b, :], ("xs", b), wbc[:, :], junk[:, :], ss[:, b:b + 1],
                                     rstd[:, b:b + 1], h[:, b, :], ("h", b), ("ss", b))
                if i + 1 < ntiles:
                    load(i + 1)

            def normB(i):
                hs = i % 2
                for b in range(NB):
                    p = pT[cnt["T"] % 2]
                    pk = ("pT", cnt["T"] % 2)
                    cnt["T"] += 1
                    for kc in range(8):
                        self.op("pe", TR(p[:, kc, :], h[:, b, kc * 128:(kc + 1) * 128], ident[:, :]),
                                reads=[("h", b), ("ident",)], writes=[pk])
                    self.op("act", ACP(hT[:, hs, :, b * 128:(b + 1) * 128], p[:, :, :]), reads=[pk], writes=[("hT", hs, b)])

            load(0)
            normA(0)
            normB(0)
            for i in range(ntiles):
                hs = i % 2
                t0 = i * TT
                if i + 1 < ntiles:
                    normA(i + 1)
                for j in range(4):
                    p, pk = proj(i, hs, 2576 + j * 128, 128)
                    self.op("act", ACTF(vT[:, j, :], p, AF.Gelu_apprx_tanh), reads=[pk], writes=[("vT", j)])
                    self.op("pool", CP(vb[:, j, :], vT[:, j, :]), reads=[("vT", j)], writes=[("vb", j)])
                    self.op("pool", TTOP(vq[:, j, :], vT[:, j, :], vT[:, j, :], ALU.mult), reads=[("vT", j)], writes=[("vq", j)])
                for j in range(4):
                    p, pk = proj(i, hs, 2064 + j * 128, 128)
                    self.op("act", ACTF(uT[:, j, :], p, AF.Gelu_apprx_tanh), reads=[pk], writes=[("uT", j)])
                for c in range(12):
                    p, pk = proj(i, hs, c * 128, 128)
                    o, ok = ostage()
                    if c % 2 == 0:
                        self.op("act", ACP(o, p), reads=[pk], writes=[ok])
                    else:
                        self.op("dve", CP(o, p), reads=[pk], writes=[ok])
                    store(qkvT[c * 128:(c + 1) * 128, t0:t0 + TT], o, ok)
                s1, s2 = pS[0], pS[1]
                for j in range(4):
                    self.op("pe", MM(s1[:, 0:TT], onesb[:, :], vb[:, j, :], j == 0, j == 3), reads=[("vb", j), ("onesb",)], writes=[("pS", 0)])
                for j in range(4):
                    self.op("pe", MM(s2[:, 0:TT], onesb[:, :], vq[:, j, :], j == 0, j == 3), reads=[("vq", j), ("onesb",)], writes=[("pS", 1)])
                self.op("act", ACTF(mu[:, :], s1[:, 0:TT], AF.Copy, scale=1.0 / 512), reads=[("pS", 0)], writes=[("mu",)])
                self.op("pool", TTOP(msq[:, :], mu[:, :], mu[:, :], ALU.mult), reads=[("mu",)], writes=[("msq",)])
                self.op("dve", STT(lrs[:, :], s2[:, 0:TT], 1.0 / 512, msq[:, :], ALU.mult, ALU.subtract),
                        reads=[("pS", 1), ("msq",)], writes=[("lrs",)])
                self.op("act", ACTF(lrs[:, :], lrs[:, :], AF.Ln, bias=EPS, scale=1.0), reads=[("lrs",)], writes=[("lrs",)])
                self.op("act", ACTF(lrs[:, :], lrs[:, :], AF.Exp, scale=-0.5), reads=[("lrs",)], writes=[("lrs",)])
                p, pk = proj(i, hs, 1536, 8)
                self.op("act", ACTF(bst[:, 0, :], p, AF.Sigmoid), reads=[pk], writes=[("bst", 0)])
                store(bg[0:8, t0:t0 + TT], bst[:, 0, :], ("bst", 0))
                p, pk = proj(i, hs, 1544, 8)
                self.op("act", ACTF(est[:, :], p, AF.Exp, bias=dtb[:, 0:1], scale=1.0), reads=[pk, ("dtb",)], writes=[("est",)])
                self.op("act", ACTF(est[:, :], est[:, :], AF.Ln, bias=1.0, scale=1.0), reads=[("est",)], writes=[("est",)])
                self.op("dve", (lambda o_=bst[:, 1, :], i_=est[:, :], s_=negA[:, 0:1]: lambda e: e.tensor_scalar_mul(o_, i_, s_))(),
                        reads=[("est",), ("negA",)], writes=[("bst", 1)])
                store(bg[8:16, t0:t0 + TT], bst[:, 1, :], ("bst", 1))
                for j in range(4):
                    p, pk = proj(i, hs, 1552 + j * 128, 128)
                    o, ok = ostage()
                    self.op("act", ACTF(o, p, AF.Silu), reads=[pk], writes=[ok])
                    store(gdT[j * 128:(j + 1) * 128, t0:t0 + TT], o, ok)
                for j in range(8):
                    p, pk = proj(i, hs, 3088 + j * 128, 128)
                    o, ok = ostage()
                    self.op("act", ACTF(o, p, AF.Sigmoid), reads=[pk], writes=[ok])
                    store(gaT[j * 128:(j + 1) * 128, t0:t0 + TT], o, ok)
                for j in range(4):
                    tb = j % 2
                    self.op("pool", TTOP(vtmp[:, tb, :], vT[:, j, :], mu[:, :], ALU.subtract), reads=[("vT", j), ("mu",)], writes=[("vtmp", tb)])
                    self.op("dve", TTOP(vtmp[:, tb, :], vtmp[:, tb, :], lrs[:, :], ALU.mult), reads=[("vtmp", tb), ("lrs",)], writes=[("vtmp", tb)])
                    self.op("dve", TS(vn[:, j, :], vtmp[:, tb, :], lnw[:, j:j + 1], lnb[:, j:j + 1], ALU.mult, ALU.add),
                            reads=[("vtmp", tb), ("lnw",), ("lnb",)], writes=[("vn", j)])
                for j in range(8):
                    p, pk = proj(i, hs, 4112 + j * 128, 128)
                    self.op("act", ACTF(gbT[:, j, :], p, AF.Sigmoid), reads=[pk], writes=[("gbT", j)])
                if i + 1 < ntiles:
                    normB(i + 1)
                for b in range(NB):
                    p = pT[cnt["T"] % 2]
                    pk = ("pT", cnt["T"] % 2)
                    cnt["T"] += 1
                    for j in range(4):
                        self.op("pe", TR(p[:, j, :], vn[:, j, b * 128:(b + 1) * 128], ident[:, :]), reads=[("vn", j), ("ident",)], writes=[pk])
                    self.op("act", ACP(vtok[:, :, b, :], p[:, 0:4, :]), reads=[pk], writes=[("vtok", b)])
                for j in range(4):
                    sb_ = cnt["S"] % 2
                    cnt["S"] += 1
                    pm = pS[sb_]
                    for b in range(NB):
                        self.op("pe", MM(pm[:, b * 128:(b + 1) * 128], vtok[:, j, b, :], WsT[:, j, :], True, False),
                                reads=[("vtok", b), ("WsT",)], writes=[("pS", sb_)])
                        self.op("pe", MM(pm[:, b * 128:(b + 1) * 128], ones1[0:1, :], bsrow[0:1, j * 128:(j + 1) * 128], False, True),
                                reads=[("ones1",), ("bsrow",)], writes=[("pS", sb_)])
                    self.op("dve", TTOP(ubT[:, j, :], uT[:, j, :], pm[:, 0:TT], ALU.mult), reads=[("pS", sb_), ("uT", j)], writes=[("ubT", j)])
                for j in range(8):
                    b = cnt["P"] % 4
                    cnt["P"] += 1
                    p = pP[b]
                    for kc in range(4):
                        self.op("pe", MM(p[:, 0:TT], Wub[:, kc, j * 128:(j + 1) * 128], ubT[:, kc, :], kc == 0, kc == 3),
                                reads=[("ubT", kc), ("Wub", kc)], writes=[("pP", b)])
                    o, ok = ostage()
                    self.op("dve", TTOP(o, gbT[:, j, :], p[:, 0:TT], ALU.mult), reads=[("pP", b), ("gbT", j)], writes=[ok])
                    store(sbT[j * 128:(j + 1) * 128, t0:t0 + TT], o, ok)
            self.S.flush()

    def phase_prep(self, scr, conv_w, cst, pre):
        qkvT = scr["qkvT"]
        qn, kn, ktok, vtok = pre["qn"], pre["kn"], pre["ktok"], pre["vtok"]
        GT = 512
        SCALE = 128 ** -0.5
        with ExitStack() as st:
            C = self.sb(st, "C", [128, 128], F32)
            self.dma("sp", C[:, :], cst[0], writes=[("C",)])
            ident = self.sb(st, "ident", [128, 128], BF16)
            self.op("dve", CP(ident[:, :], C[:, :]), reads=[("C",)], writes=[("ident",)])
            onesb = self.sb(st, "onesb", [128, 128], BF16)
            self.op("dve", MEMSET(onesb[:, :], 1.0), writes=[("onesb",)])
            cw = self.sb(st, "cw", [128, 5, 12], F32)
            self.op("sp", lambda e: e.dma_start(out=cw[:, :, :], in_=conv_w.rearrange("j (c p) -> p j c", p=128), allow_slow_non_contiguous=True),
                    writes=[("cw",)], dma=True)
            Cd = self.sb(st, "Cd", [128, 12, 5, 128], BF16)
            for cc in range(12):
                for j in range(5):
                    self.op("dve", (lambda o_=Cd[:, cc, j, :], s_=cw[:, j, cc:cc + 1]: lambda e: e.tensor_scalar_mul(o_, C[:, :], s_))(),
                            reads=[("C",), ("cw",)], writes=[("Cd", cc)])
            NR = 4
            raw = self.sb(st, "raw", [128, NR, GT + 4], BF16)
            sil = self.sb(st, "sil", [128, 4, GT], F32)
            sq = self.sb(st, "sq", [128, 3, GT], BF16)
            rsn = self.sb(st, "rsn", [128, 3, GT], F32)
            fm = self.sb(st, "fm", [128, 8, GT], BF16)
            VT = self.sb(st, "VT", [128, 4, GT], BF16)
            tk = self.sb(st, "tk", [128, 2, 4, 512], BF16)
            pcv = [self.ps(st, "pcv%d" % i, [128, 512], F32) for i in range(3)]
            pss = [self.ps(st, "pss%d" % i, [128, 512], F32) for i in range(2)]
            ptr = [self.ps(st, "ptr%d" % i, [128, 1024], BF16) for i in range(2)]
            op = self.op
            cn = dict(r=0, c=0, s=0, f=0, t=0, x=0)
            its = []
            t_base = 0
            for L in self.seqs:
                ng = L // GT
                for gi in range(ng):
                    for qi in (1, 2, 0):
                        for h in range(4):
                            its.append((t_base + gi * GT, gi, ng, qi, h))
                t_base += L

            def load(n):
                t0, gi, ng, qi, h = its[n]
                cc = qi * 4 + h
                rs_ = n % NR
                rk = ("raw", rs_)
                lo = 2 if gi == 0 else 0
                hi = GT + 2 if gi == ng - 1 else GT + 4
                if gi == 0 or gi == ng - 1:
                    op("pool", MEMSET(raw[:, rs_, :], 0.0), writes=[rk])
                self.dma("sp", raw[:, rs_, lo:hi], qkvT[cc * 128:(cc + 1) * 128, t0 - 2 + lo:t0 - 2 + hi], writes=[rk])

            deferred = []

            stA = {}
            late = []

            def computeA(n):
                t0, gi, ng, qi, h = its[n]
                cc = qi * 4 + h
                rs_ = n % NR
                rk = ("raw", rs_)
                pb_ = cn["c"] % 3
                cn["c"] += 1
                pf, pfk = pcv[pb_], ("pcv", pb_)
                for j in range(5):
                    op("pe", MM(pf[:, :], Cd[:, cc, j, :], raw[:, rs_, j:j + GT], j == 0, j == 4), reads=[rk, ("Cd", cc)], writes=[pfk])
                if qi == 2:
                    op("act", ACTF(VT[:, h, :], pf[:, :], AF.Silu), reads=[pfk], writes=[("VT", h)])
                else:
                    op("act", ACTF(sil[:, h, :], pf[:, :], AF.Silu), reads=[pfk], writes=[("sil", h)])

            def computeB(n):
                t0, gi, ng, qi, h = its[n]
                if qi == 2:
                    src, srck = VT[:, h, :], ("VT", h)
                else:
                    ss_ = h % 3
                    sk = ("sil", h)
                    op("pool", TTOP(sq[:, ss_, :], sil[:, h, :], sil[:, h, :], ALU.mult), reads=[sk], writes=[("sq", ss_)])
                    ps_ = h % 2
                    p2, p2k = pss[ps_], ("pss", ps_)
                    op("pe", MM(p2[:, :], onesb[:, :], sq[:, ss_, :], True, True), reads=[("sq", ss_), ("onesb",)], writes=[p2k])
                    op("act", ACTF(rsn[:, ss_, :], p2[:, :], AF.Ln, bias=EPS, scale=1.0), reads=[p2k], writes=[("rsn", ss_)])
                    op("act", ACTF(rsn[:, ss_, :], rsn[:, ss_, :], AF.Exp, scale=-0.5), reads=[("rsn", ss_)], writes=[("rsn", ss_)])
                    fs = cn["f"] % 8
                    cn["f"] += 1
                    if qi == 0:
                        op("dve", STT(fm[:, fs, :], sil[:, h, :], SCALE, rsn[:, ss_, :], ALU.mult, ALU.mult),
                           reads=[sk, ("rsn", ss_)], writes=[("fm", fs)])
                        deferred.append((qn[h * 128:(h + 1) * 128, t0:t0 + GT], fm[:, fs, :], [("fm", fs)]))
                        return
                    op("dve", TTOP(fm[:, fs, :], sil[:, h, :], rsn[:, ss_, :], ALU.mult), reads=[sk, ("rsn", ss_)], writes=[("fm", fs)])
                    deferred.append((kn[h * 128:(h + 1) * 128, t0:t0 + GT], fm[:, fs, :], [("fm", fs)]))
                    src, srck = fm[:, fs, :], ("fm", fs)
                late.append((n, src, srck))

            def computeB2(n, src, srck):
                t0, gi, ng, qi, h = its[n]
                kind = 0 if qi == 1 else 1
                tb = cn["t"] % 2
                cn["t"] += 1
                pt, ptk = ptr[tb], ("ptr", tb)
                for c in range(4):
                    op("pe", TR(pt[:, c * 128:(c + 1) * 128], src[:, c * 128:(c + 1) * 128], ident[:, :]), reads=[srck, ("ident",)], writes=[ptk])
                evk = ("tk", kind, h)
                op("dve", CP(tk[:, kind, :, h * 128:(h + 1) * 128], pt[:, 0:512].rearrange("p (c d) -> p c d", c=4)),
                   reads=[ptk], writes=[evk])
                if h == 3:
                    dst = (ktok, vtok)[kind]
                    deferred.append((dst[t0:t0 + GT, :].rearrange("(c p) f -> p c f", p=128), tk[:, kind, :, :],
                                     [("tk", kind, hh) for hh in range(4)]))

            nit = len(its)
            load(0)
            if nit > 1:
                load(1)
            pending = []
            late_prev = []
            for blk in range(nit // 4):
                for n in range(blk * 4, blk * 4 + 4):
                    if n + 2 < nit:
                        load(n + 2)
                    computeA(n)
                for (n_, src_, srck_) in late_prev:
                    computeB2(n_, src_, srck_)
                for n in range(blk * 4, blk * 4 + 4):
                    computeB(n)
                late_prev = list(late)
                del late[:]
                for (dst_, src_, rd_) in pending:
                    self.dma("sp", dst_, src_, reads=rd_)
                pending = list(deferred)
                del deferred[:]
            for (n_, src_, srck_) in late_prev:
                computeB2(n_, src_, srck_)
            pending += list(deferred)
            for (dst_, src_, rd_) in pending:
                self.dma("sp", dst_, src_, reads=rd_)
            self.S.flush()

    def phase_dn(self, scr, pre, cst, of, ob, offset=0, ndummy=1):
        bg = scr["bg"]
        qn, kn, ktok, vtok = pre["qn"], pre["kn"], pre["ktok"], pre["vtok"]
        GT = 512
        IDX = dict(PA=(1, 3), PB=(2, 4), BD=5, M32=(8, 6), M64=(9, 7), TRI=(10, 11))
        SCALE = 128 ** -0.5
        with ExitStack() as st:
            C = self.sb(st, "C", [128, 12, 128], F32)
            self.dma("sp", C[:, :, :], cst.rearrange("n p f -> p n f"), writes=[("C",)])
            identf = C[:, 0, :]
            Cb = self.sb(st, "Cb", [128, 12, 128], BF16)
            self.op("dve", CP(Cb[:, :, :], C[:, :, :]), reads=[("C",)], writes=[("Cb",)])
            ident = Cb[:, 0, :]
            onesb = self.sb(st, "onesb", [128, 128], BF16)
            self.op("dve", MEMSET(onesb[:, :], 1.0), writes=[("onesb",)])
            onesf = self.sb(st, "onesf", [128, 128], F32)
            self.op("dve", MEMSET(onesf[:, :], 1.0), writes=[("onesf",)])
            def bc_col(a):
                return a.unsqueeze(2).to_broadcast([128, 4, 128])

            def bc_mat(a):
                return a.unsqueeze(1).to_broadcast([128, 4, 128])

            class Stream:
                pass

            streams = []
            for d in (0, 1):
                Z = Stream()
                Z.d = d
                n_ = "d%d_" % d
                Z.bgrow = self.sb(st, n_ + "bgrow", [16, GT], F32)
                Z.QTs = self.sb(st, n_ + "QT", [128, 2, 4, GT], BF16)
                Z.KTs = self.sb(st, n_ + "KT", [128, 2, 4, GT], BF16)
                Z.Ktoks = self.sb(st, n_ + "Ktok", [128, 2, 4, 512], BF16)
                Z.Vtoks = self.sb(st, n_ + "Vtok", [128, 2, 4, 512], BF16)
                Z.btoks = self.sb(st, n_ + "btoks", [128, 2, 4, 16], F32)
                Z.gslot = 0
                Z.cols = self.sb(st, n_ + "cols", [128, 2, 8, 4], F32)
                Z.ostg = self.sb(st, n_ + "ostg", [128, 2, 512], F32)
                Z.S = self.sb(st, n_ + "S", [128, 4, 128], F32)
                Z.Sd = self.sb(st, n_ + "Sd", [128, 4, 128], F32)
                Z.Sb = self.sb(st, n_ + "Sb", [128, 4, 128], BF16)
                Z.bufs = {}
                PAR2 = ("U", "Wt", "Aq", "qg", "kg")
                for nm in ("Gs", "T0", "Y1", "Y2", "E1", "E2", "Eg", "E1n", "U"):
                    Z.bufs[nm] = self.sb(st, n_ + nm, [128, 2 if nm in PAR2 else 1, 4, 128], F32)
                for nm in ("N", "N32", "N64", "Nd", "P0", "P1", "Q0", "Q1", "R", "Yb", "Xb", "tm", "vb", "kbg", "kg", "qg", "Aq", "Wt", "vn"):
                    Z.bufs[nm] = self.sb(st, n_ + nm, [128, 2 if nm in PAR2 else 1, 4, 128], BF16)
                Z.pA = self.ps(st, n_ + "pA", [128, 512], F32)
                Z.pB = self.ps(st, n_ + "pB", [128, 512], F32)
                Z.pC = self.ps(st, n_ + "pC", [128, 512], F32)
                Z.pT = Z.pC[:, :].bitcast(BF16)
                Z.PA = C[:, IDX["PA"][d], :]
                Z.PB = C[:, IDX["PB"][d], :]
                Z.BD = Cb[:, IDX["BD"], :]
                Z.M32 = Cb[:, IDX["M32"][1 - d], :]
                Z.M64 = Cb[:, IDX["M64"][1 - d], :]
                Z.TRI = C[:, IDX["TRI"][d], :]
                Z.oscr = (of, ob)[d]
                streams.append(Z)
            op = self.op
            pDum = self.ps(st, "pDum", [128, 512], F32)
            dsrc = self.sb(st, "dsrc", [128, 512], BF16)
            self.op("dve", MEMSET(dsrc[:, :], 0.001), writes=[("dsrc",)])

            DUMN = 128

            DUMEVERY = 2
            dcount = [0]

            def dummies(n=None):
                dcount[0] += 1
                if dcount[0] % DUMEVERY:
                    return
                for _ in range(ndummy if n is None else n):
                    op("pe", MM(pDum[:, 0:DUMN], onesb[:, :], dsrc[:, 0:DUMN], True, True), reads=[("dsrc",), ("onesb",)], writes=[("pDum",)])

            def K(Z, name, par=0):
                return ("dn", Z.d, name, par)

            def Bf(Z, name, par=0):
                return Z.bufs[name][:, par], K(Z, name, par)

            def v4(ps):
                return ps[:, :].rearrange("p (h c) -> p h c", h=4)

            def prefetch(Z, gi, ng, t0):
                d = Z.d
                gs = Z.gslot ^ 1
                pk = lambda nm: ("dn", d, "ps" + nm)
                self.dma("sp", Z.QTs[:, gs], qn[:, t0:t0 + GT].rearrange("(h p) t -> p h t", p=128), writes=[K(Z, "QT", gs)])
                self.dma("sp", Z.KTs[:, gs], kn[:, t0:t0 + GT].rearrange("(h p) t -> p h t", p=128), writes=[K(Z, "KT", gs)])
                self.dma("sp", Z.Ktoks[:, gs], ktok[t0:t0 + GT, :].rearrange("(c p) f -> p c f", p=128), writes=[K(Z, "Ktok", gs)])
                self.dma("sp", Z.Vtoks[:, gs], vtok[t0:t0 + GT, :].rearrange("(c p) f -> p c f", p=128), writes=[K(Z, "Vtok", gs)])
                self.dma("sp", Z.bgrow[:, :], bg[:, t0:t0 + GT], writes=[K(Z, "bgrow")])
                for c in range(4):
                    op("pe", TR(Z.pC[:, c * 16:(c + 1) * 16], Z.bgrow[0:16, c * 128:(c + 1) * 128], C[0:16, 0, 0:16]),
                       reads=[K(Z, "bgrow"), ("C",)], writes=[pk("C")])
                op("act", ACP(Z.btoks[:, gs], Z.pC[:, 0:64].rearrange("p (a b) -> p a b", a=4)), reads=[pk("C")], writes=[K(Z, "btok", gs)])

            def prologue(Z, gi, ng, t0):
                Z.gslot ^= 1
                gs = Z.gslot
                Z.QT = Z.QTs[:, gs]
                Z.KT = Z.KTs[:, gs]
                Z.Ktok = Z.Ktoks[:, gs].rearrange("p c (h d) -> p h c d", h=4)
                Z.Vtok = Z.Vtoks[:, gs].rearrange("p c (h d) -> p h c d", h=4)
                Z.btok = Z.btoks[:, gs]
                Z.gk = gs

            KGC, KGL, KNB, KEG, KBGE, KEGL, KKGC = range(7)

            def unit(Z, c):
                d = Z.d
                par = c % 2
                QT_, KT_, Ktok_, Vtok_, btok_, gk_ = Z.QT, Z.KT, Z.Ktok, Z.Vtok, Z.btok, Z.gk
                pk = lambda nm: ("dn", d, "ps" + nm)
                cs = slice(c * 128, (c + 1) * 128)
                gt4 = btok_[:, c, 8 + 4 * d:12 + 4 * d]
                be4 = btok_[:, c, 4 * d:4 * d + 4]
                cl = Z.cols[:, par]
                ck = K(Z, "cols", par)
                bk = K(Z, "btok", gk_)
                A4, B4, C4 = v4(Z.pA), v4(Z.pB), v4(Z.pC)
                T4 = Z.pT[:, 0:512].rearrange("p (h c) -> p h c", h=4)
                op("pe", MM(Z.pC[:, 0:4], Z.TRI, gt4, True, True), reads=[bk, ("C",)], writes=[pk("C")])
                op("pe", MM(Z.pC[:, 4:8], onesf[:, :], gt4, True, True), reads=[bk, ("onesf",)], writes=[pk("C")])
                op("act", ACP(cl[:, 0:2, :], Z.pC[:, 0:8].rearrange("p (a b) -> p a b", a=2)), reads=[pk("C")], writes=[ck])
                for h in range(4):
                    op("pe", MM(A4[:, h, :], gt4[:, h:h + 1].to_broadcast([128, 128]), Z.TRI, True, True), reads=[bk, ("C",)], writes=[pk("A")])
                dummies()
                yield
                op("dve", (lambda o_=cl[:, KNB, :], i_=be4: lambda e: e.tensor_scalar_mul(o_, i_, -1.0))(), reads=[bk], writes=[ck])
                op("act", ACTF(cl[:, KEG, :], cl[:, KGC, :], AF.Exp), reads=[ck], writes=[ck])
                op("act", ACTF(cl[:, KEGL, :], cl[:, KGL, :], AF.Exp), reads=[ck], writes=[ck])
                op("dve", TTOP(cl[:, KKGC, :], cl[:, KGL, :], cl[:, KGC, :], ALU.subtract), reads=[ck], writes=[ck])
                op("act", ACTF(cl[:, KKGC, :], cl[:, KKGC, :], AF.Exp), reads=[ck], writes=[ck])
                op("dve", TTOP(cl[:, KBGE, :], cl[:, KEG, :], be4, ALU.mult), reads=[ck, bk], writes=[ck])
                Gs, Gsk = Bf(Z, "Gs")
                op("act", ACP(Gs, A4), reads=[pk("A")], writes=[Gsk])
                op("pool", TTOP(Z.Sd[:, :, :], Z.S[:, :, :], bc_col(cl[:, KEGL, :]), ALU.mult), reads=[K(Z, "S"), ck], writes=[K(Z, "Sd")])
                dummies()
                yield
                T0, T0k = Bf(Z, "T0")
                Y1, Y1k = Bf(Z, "Y1")
                Y2, Y2k = Bf(Z, "Y2")
                Eg, Egk = Bf(Z, "Eg")
                E1, E1k = Bf(Z, "E1")
                E2, E2k = Bf(Z, "E2")
                op("act", ACTF(Eg, Gs, AF.Exp), reads=[Gsk], writes=[Egk])
                op("dve", TTOP(T0, Gs, bc_col(cl[:, KGC, :]), ALU.subtract), reads=[Gsk, ck], writes=[T0k])
                dummies()
                yield
                op("dve", TTOP(Y1, T0, bc_mat(Z.PA), ALU.add), reads=[T0k, ("C",)], writes=[Y1k])
                op("pool", TTOP(Y2, T0, bc_mat(Z.PB), ALU.add), reads=[T0k, ("C",)], writes=[Y2k])
                for h in range(4):
                    op("pe", MM(B4[:, h, :], KT_[:, h, cs], KT_[:, h, cs], True, True), reads=[K(Z, "KT", gk_)], writes=[pk("B")])
                for h in range(4):
                    op("pe", MM(C4[:, h, :], KT_[:, h, cs], QT_[:, h, cs], True, True), reads=[K(Z, "KT", gk_), K(Z, "QT", gk_)], writes=[pk("C")])
                dummies()
                yield
                op("act", ACTF(E1, Y1, AF.Exp, scale=-1.0), reads=[Y1k], writes=[E1k])
                op("act", ACTF(E2, Y2, AF.Exp), reads=[Y2k], writes=[E2k])
                qg, qgk = Bf(Z, "qg", par)
                op("pool", TTOP(qg, QT_[:, :, cs], Eg, ALU.mult), reads=[K(Z, "QT", gk_), Egk], writes=[qgk])
                dummies()
                yield
                E1n, E1nk = Bf(Z, "E1n")
                op("dve", TTOP(E1n, E1, bc_col(cl[:, KNB, :]), ALU.mult), reads=[E1k, ck], writes=[E1nk])
                N, Nk = Bf(Z, "N")
                op("dve", TTOP(N, B4, E1n, ALU.mult), reads=[pk("B"), E1nk], writes=[Nk])
                dummies()
                yield
                Nd, Ndk = Bf(Z, "Nd")
                op("dve", TTOP(Nd, N, bc_mat(Z.BD), ALU.mult), reads=[Nk, ("Cb",)], writes=[Ndk])
                N32, N32k = Bf(Z, "N32")
                N64, N64k = Bf(Z, "N64")
                op("pool", TTOP(N32, N, bc_mat(Z.M32), ALU.mult), reads=[Nk, ("Cb",)], writes=[N32k])
                op("pool", TTOP(N64, N, bc_mat(Z.M64), ALU.mult), reads=[Nk, ("Cb",)], writes=[N64k])
                Aq, Aqk = Bf(Z, "Aq", par)
                op("dve", TTOP(Aq, C4, E2, ALU.mult), reads=[pk("C"), E2k], writes=[Aqk])
                vb, vbk = Bf(Z, "vb")
                kbg, kbgk = Bf(Z, "kbg")
                kg, kgk = Bf(Z, "kg", par)
                op("pool", TTOP(vb, Vtok_[:, :, c, :], bc_col(be4), ALU.mult), reads=[K(Z, "Vtok", gk_), bk], writes=[vbk])
                op("pool", TTOP(kbg, Ktok_[:, :, c, :], bc_col(cl[:, KBGE, :]), ALU.mult), reads=[K(Z, "Ktok", gk_), ck], writes=[kbgk])
                op("pool", TTOP(kg, Ktok_[:, :, c, :], bc_col(cl[:, KKGC, :]), ALU.mult), reads=[K(Z, "Ktok", gk_), ck], writes=[kgk])
                dummies()
                yield
                for h in range(4):
                    op("pe", TR(T4[:, h, :], Nd[:, h, :], ident), reads=[Ndk, ("Cb",)], writes=[pk("C")])
                dummies()
                yield
                Q0, Q0k = Bf(Z, "Q0")
                op("act", ACP(Q0, T4), reads=[pk("C")], writes=[Q0k])
                dummies()
                yield
                R, Rk = Bf(Z, "R")
                op("dve", TTOP(R, Q0, bc_mat(ident), ALU.add), reads=[Q0k, ("Cb",)], writes=[Rk])
                Pc, Pck = Nd, Ndk
                Qc, Qck = Q0, Q0k
                for j in range(1, 5):
                    Pn, Pnk = Bf(Z, "P%d" % (j % 2))
                    for h in range(4):
                        op("pe", MM(A4[:, h, :], Qc[:, h, :], Pc[:, h, :], True, True), reads=[Qck, Pck], writes=[pk("A")])
                    if j < 4:
                        for h in range(4):
                            op("pe", MM(B4[:, h, :], Pc[:, h, :], Qc[:, h, :], True, True), reads=[Qck, Pck], writes=[pk("B")])
                    dummies()
                    yield
                    op("act", ACP(Pn, A4), reads=[pk("A")], writes=[Pnk])
                    if j < 4:
                        Qn, Qnk = Bf(Z, "Q%d" % (j % 2))
                        op("dve", CP(Qn, B4), reads=[pk("B")], writes=[Qnk])
                    dummies()
                    yield
                    for h in range(4):
                        op("pe", MM(C4[:, h, :], Pn[:, h, :], R[:, h, :], True, True), reads=[Pnk, Rk], writes=[pk("C")])
                    dummies()
                    yield
                    op("dve", TTOP(R, C4, R, ALU.add), reads=[pk("C"), Rk], writes=[Rk])
                    dummies()
                    yield
                    Pc, Pck = Pn, Pnk
                    if j < 4:
                        Qc, Qck = Qn, Qnk
                for (NM, NMk) in ((N32, N32k), (N64, N64k)):
                    Yb, Ybk = Bf(Z, "Yb")
                    Xb, Xbk = Bf(Z, "Xb")
                    for h in range(4):
                        op("pe", MM(A4[:, h, :], NM[:, h, :], R[:, h, :], True, True), reads=[NMk, Rk], writes=[pk("A")])
                    for h in range(4):
                        op("pe", TR(T4[:, h, :], R[:, h, :], ident), reads=[Rk, ("Cb",)], writes=[pk("C")])
                    dummies()
                    yield
                    op("act", ACP(Yb, A4), reads=[pk("A")], writes=[Ybk])
                    op("dve", CP(Xb, T4), reads=[pk("C")], writes=[Xbk])
                    dummies()
                    yield
                    for h in range(4):
                        op("pe", MM(B4[:, h, :], Xb[:, h, :], Yb[:, h, :], True, True), reads=[Xbk, Ybk], writes=[pk("B")])
                    dummies()
                    yield
                    op("dve", TTOP(R, B4, R, ALU.add), reads=[pk("B"), Rk], writes=[Rk])
                    dummies()
                    yield
                for h in range(4):
                    op("pe", MM(A4[:, h, :], R[:, h, :], vb[:, h, :], True, True), reads=[Rk, vbk], writes=[pk("A")])
                for h in range(4):
                    op("pe", MM(B4[:, h, :], kbg[:, h, :], R[:, h, :], True, True), reads=[Rk, kbgk], writes=[pk("B")])
                dummies()
                yield
                U, Uk = Bf(Z, "U", par)
                Wt, Wtk = Bf(Z, "Wt", par)
                op("act", ACP(U, A4), reads=[pk("A")], writes=[Uk])
                op("dve", CP(Wt, B4), reads=[pk("B")], writes=[Wtk])
                dummies()
                yield

            def scan(Z, c, t0):
                d = Z.d
                par = c % 2
                pk = lambda nm: ("dn", d, "ps" + nm)
                A4, B4, C4 = v4(Z.pA), v4(Z.pB), v4(Z.pC)
                U, Uk = Bf(Z, "U", par)
                Wt, Wtk = Bf(Z, "Wt", par)
                Aq, Aqk = Bf(Z, "Aq", par)
                qg, qgk = Bf(Z, "qg", par)
                kg, kgk = Bf(Z, "kg", par)
                vn, vnk = Bf(Z, "vn")
                for h in range(4):
                    op("pe", MM(A4[:, h, :], Wt[:, h, :], Z.Sb[:, h, :], True, True), reads=[Wtk, K(Z, "Sb")], writes=[pk("A")])
                dummies()
                yield
                op("dve", TTOP(vn, U, A4, ALU.subtract), reads=[Uk, pk("A")], writes=[vnk])
                dummies()
                yield
                for h in range(4):
                    op("pe", MM(C4[:, h, :], kg[:, h, :], vn[:, h, :], True, True), reads=[kgk, vnk], writes=[pk("C")])
                for h in range(4):
                    op("pe", MM(B4[:, h, :], qg[:, h, :], Z.Sb[:, h, :], True, False), reads=[qgk, K(Z, "Sb")], writes=[pk("B")])
                    op("pe", MM(B4[:, h, :], Aq[:, h, :], vn[:, h, :], False, True), reads=[Aqk, vnk], writes=[pk("B")])
                dummies()
                yield
                op("dve", TTOP(Z.S[:, :, :], Z.Sd[:, :, :], C4, ALU.add), reads=[K(Z, "Sd"), pk("C")], writes=[K(Z, "S")])
                op("act", ACP(Z.ostg[:, par, :], Z.pB[:, :]), reads=[pk("B")], writes=[K(Z, "ostg", par)])
                dummies()
                yield
                op("act", ACP(Z.Sb[:, :, :], Z.S[:, :, :]), reads=[K(Z, "S")], writes=[K(Z, "Sb")])
                tc = t0 + c * 128
                self.dma("sp", Z.oscr[tc:tc + 128, :], Z.ostg[:, par, :], reads=[K(Z, "ostg", par)])
                dummies()
                yield

            def par(*gens):
                gens = list(gens)
                while gens:
                    nxt = []
                    for g_ in gens:
                        try:
                            next(g_)
                            nxt.append(g_)
                        except StopIteration:
                            pass
                    gens = nxt
                    if gens:
                        yield

            def stream_gen(Z, L, t_base):
                ng = L // GT
                order = []
                for step in range(ng):
                    gi = step if Z.d == 0 else ng - 1 - step
                    for ci in range(4):
                        order.append((gi, ci if Z.d == 0 else 3 - ci))
                groups = []
                for (gi, c) in order:
                    if not groups or groups[-1] != gi:
                        groups.append(gi)
                prefetch(Z, groups[0], ng, t_base + groups[0] * GT)
                cur_g = None
                gpos = -1
                for (gi, c) in order:
                    t0 = t_base + gi * GT
                    first = gi != cur_g
                    if first:
                        prologue(Z, gi, ng, t0)
                        cur_g = gi
                        gpos += 1
                    if first and gpos + 1 < len(groups):
                        prefetch(Z, groups[gpos + 1], ng, t_base + groups[gpos + 1] * GT)
                    yield from unit(Z, c)
                    yield from scan(Z, c, t0)

            t_base = 0
            for L in self.seqs:
                for Z in streams:
                    op("pool", MEMSET(Z.S[:, :, :], 0.0), writes=[K(Z, "S")])
                    op("pool", MEMSET(Z.Sb[:, :, :], 0.0), writes=[K(Z, "Sb")])
                ga = stream_gen(streams[0], L, t_base)
                gb = stream_gen(streams[1], L, t_base)
                for _ in range(offset):
                    next(ga)
                for _ in par(ga, gb):
                    pass
                t_base += L
            self.S.flush()

    def phase_mem(self, mem, nw_mem, w_kv, ident_f32, kT, vv):
        nS = len(self.seqs)
        with ExitStack() as st:
            ident, _ = self.consts(st, ident_f32)
            wbc = self.bcast_vec(st, "wbc", nw_mem)
            Wkv = self.load_weight(st, "Wkv", w_kv, 8, 2 * D)
            ms = self.sb(st, "ms", [128, 2, D], F32)
            h = self.sb(st, "h", [128, 2, D], BF16)
            mT = self.sb(st, "mT", [128, 8, 256], BF16)
            junk = self.sb(st, "junk", [128, D], BF16)
            ss = self.sb(st, "ss", [128, 2], F32)
            rstd = self.sb(st, "rstd", [128, 2], F32)
            ost = self.sb(st, "ost", [128, 4, 512], BF16)
            pT = [self.ps(st, "pT%d" % i, [128, 8, 128], BF16) for i in range(2)]
            pP = [self.ps(st, "pP%d" % i, [128, 512], F32) for i in range(4)]
            cP = cO = 0
            for s_ in range(nS):
                self.dma("sp", ms[:, :, :], mem[s_ * 256:(s_ + 1) * 256, :].rearrange("(b p) d -> p b d", p=128), writes=[("ms",)])
                for b in range(2):
                    self.rmsnorm_tok(ms[:, b, :], ("ms",), wbc[:, :], junk[:, :], ss[:, b:b + 1], rstd[:, b:b + 1],
                                     h[:, b, :], ("h", b), ("ss", b))
                    p = pT[b]
                    for kc in range(8):
                        self.op("pe", TR(p[:, kc, :], h[:, b, kc * 128:(kc + 1) * 128], ident[:, :]), reads=[("h", b), ("ident",)], writes=[("pT", b)])
                    self.op("act", ACP(mT[:, :, b * 128:(b + 1) * 128], p[:, :, :]), reads=[("pT", b)], writes=[("mT", b)])
                mk = [("mT", 0), ("mT", 1)]
                for j in range(8):
                    p = pP[cP % 4]; pk = ("pP", cP % 4); cP += 1
                    for kc in range(8):
                        self.op("pe", MM(p[:, 0:256], Wkv[:, kc, j * 128:(j + 1) * 128], mT[:, kc, :], kc == 0, kc == 7),
                                reads=mk + [("Wkv", kc)], writes=[pk])
                    o = cO % 4; cO += 1
                    self.op("act", ACP(ost[:, o, 0:256], p[:, 0:256]), reads=[pk], writes=[("ost", o)])
                    self.dma("sp", kT[s_, j * 128:(j + 1) * 128, :], ost[:, o, 0:256], reads=[("ost", o)])
                for mb in range(2):
                    for nh in range(2):
                        p = pP[cP % 4]; pk = ("pP", cP % 4); cP += 1
                        for kc in range(8):
                            self.op("pe", MM(p[:, :], mT[:, kc, mb * 128:(mb + 1) * 128], Wkv[:, kc, D + nh * 512:D + (nh + 1) * 512], kc == 0, kc == 7),
                                    reads=mk + [("Wkv", kc)], writes=[pk])
                        o = cO % 4; cO += 1
                        self.op("dve", CP(ost[:, o, :], p[:, :]), reads=[pk], writes=[("ost", o)])
                        self.dma("sp", vv[s_, mb * 128:(mb + 1) * 128, nh * 512:(nh + 1) * 512], ost[:, o, :], reads=[("ost", o)])
            self.S.flush()

    def phase_mid(self, x, scr, of, ob, kT, vv, dn_nw, w_up_a, w_out, nw_xa, w_q, w_o, ident_f32, x2out, TT=512):
        T = self.T
        NB = TT // 128
        gdT, gaT, sbT = scr["gdT"], scr["gaT"], scr["sbT"]
        nS = len(self.seqs)
        with ExitStack() as st:
            ident, _ = self.consts(st, ident_f32)
            wbc = self.bcast_vec(st, "wbc", nw_xa)
            Wua = self.load_weight(st, "Wua", w_up_a, 4, D)
            Wout = self.load_weight(st, "Wout", w_out, 8, D)
            Wq = self.load_weight(st, "Wq", w_q, 8, D)
            Wo = self.load_weight(st, "Wo", w_o, 8, D)
            KT1 = self.sb(st, "KT", [128, 8, 256], BF16)
            VV1 = self.sb(st, "VV", [128, 2, D], BF16)
            nwc = self.sb(st, "nwc", [128, 1], F32)
            self.dma("sp", nwc[:, :], dn_nw.rearrange("(p o) -> p o", o=1), writes=[("nwc",)])
            onesb = self.sb(st, "onesb", [128, 128], BF16)
            self.op("dve", MEMSET(onesb[:, :], 1.0), writes=[("onesb",)])
            xs2 = self.sb(st, "xs", [128, 2, NB, D], F32)
            ofs = self.sb(st, "ofs", [128, NB, 512], F32)
            obs = self.sb(st, "obs", [128, NB, 512], F32)
            on = self.sb(st, "on", [128, NB, 512], BF16)
            gds = self.sb(st, "gds", [128, 4, TT], BF16)
            gas = self.sb(st, "gas", [128, 8, TT], BF16)
            sbs = self.sb(st, "sbs", [128, 8, TT], BF16)
            aT = self.sb(st, "aT", [128, 4, TT], BF16)
            mg = self.sb(st, "mg", [128, 8, TT], BF16)
            tmpm = self.sb(st, "tmpm", [128, 2, TT], BF16)
            ss16 = self.sb(st, "ss16", [128, 16], F32)
            rs16 = self.sb(st, "rs16", [128, 16], F32)
            h = self.sb(st, "h", [128, 4, D], BF16)
            hT = self.sb(st, "hT", [128, 8, TT], BF16)
            qT = self.sb(st, "qT", [128, 8, TT], BF16)
            pex = self.sb(st, "pex", [128, 2, 2, TT], BF16)
            rinv = self.sb(st, "rinv", [128, 1, TT], F32)
            oT = self.sb(st, "oT", [128, 8, TT], BF16)
            junk = self.sb(st, "junk", [128, D], BF16)
            ss = self.sb(st, "ss", [128, NB], F32)
            rstd = self.sb(st, "rstd", [128, NB], F32)
            pT = [self.ps(st, "pT%d" % i, [128, 8, 128], BF16) for i in range(2)]
            pP = [self.ps(st, "pP%d" % i, [128, 512], F32) for i in range(6)]
            cnt = dict(T=0, P=0)

            def nP():
                b = cnt["P"] % 6
                cnt["P"] += 1
                return pP[b], ("pP", b)

            def nT():
                b = cnt["T"] % 2
                cnt["T"] += 1
                return pT[b], ("pT", b)

            seq_of_tile = []
            for si, L in enumerate(self.seqs):
                seq_of_tile += [si] * (L // TT)
            ntiles = T // TT

            def load_x(i):
                sl = i % 2
                self.dma("sp", xs2[:, sl], x[i * TT:(i + 1) * TT, :].rearrange("(b p) d -> p b d", p=128), writes=[("xs", sl)])

            def load_scr(i):
                t0 = i * TT
                self.dma("sp", ofs[:, :, :], of[t0:t0 + TT, :].rearrange("(b p) d -> p b d", p=128), writes=[("ofs",)])
                self.dma("sp", obs[:, :, :], ob[t0:t0 + TT, :].rearrange("(b p) d -> p b d", p=128), writes=[("obs",)])
                self.dma("sp", gds[:, :, :], gdT[:, t0:t0 + TT].rearrange("(j p) t -> p j t", p=128), writes=[("gds",)])
                self.dma("sp", gas[:, :, :], gaT[:, t0:t0 + TT].rearrange("(j p) t -> p j t", p=128), writes=[("gas",)])
                self.dma("sp", sbs[:, :, :], sbT[:, t0:t0 + TT].rearrange("(j p) t -> p j t", p=128), writes=[("sbs",)])

            load_x(0)
            load_scr(0)
            cur_seq = -1
            cur = dict(seq=-1)

            def tile_ctx(i):
                return i * TT, seq_of_tile[i], xs2[:, i % 2], ("xs", i % 2)

            def stageBD(i):
                t0, si, xs, xk = tile_ctx(i)
                self.op("dve", TTOP(ofs[:, :, :], ofs[:, :, :], obs[:, :, :], ALU.add), reads=[("ofs",), ("obs",)], writes=[("ofs",)])
                for b_ in range(NB):
                    for hh_ in range(4):
                        self.op("act", ACTF(junk[:, 0:128], ofs[:, b_, hh_ * 128:(hh_ + 1) * 128], AF.Square,
                                            accum_out=ss16[:, b_ * 4 + hh_:b_ * 4 + hh_ + 1]),
                                reads=[("ofs",)], writes=[("junk",), ("ss16",)])
                self.op("act", ACTF(rs16[:, :], ss16[:, :], AF.Sqrt, bias=EPS, scale=1.0 / 128), reads=[("ss16",)], writes=[("rs16",)])
                self.op("dve", RECIP(rs16[:, :], rs16[:, :]), reads=[("rs16",)], writes=[("rs16",)])
                self.op("dve", TTOP(on[:, :, :].rearrange("p b (h e) -> p (b h) e", e=128),
                                    ofs[:, :, :].rearrange("p b (h e) -> p (b h) e", e=128),
                                    rs16[:, :].unsqueeze(2).to_broadcast([128, 16, 128]), ALU.mult),
                        reads=[("ofs",), ("rs16",)], writes=[("on",)])
                for hp in range(2):
                    p, pk = nT()
                    for hh in range(2):
                        hd = hp * 2 + hh
                        for b in range(NB):
                            self.op("pe", TR(p[:, hh * 4 + b, :], on[:, b, hd * 128:(hd + 1) * 128], ident[:, :]),
                                    reads=[("on",), ("ident",)], writes=[pk])
                    for hh in range(2):
                        hd = hp * 2 + hh
                        self.op("dve", STT(aT[:, hd, :], p[:, hh * 4:hh * 4 + 4, :].rearrange("p a b -> p (a b)"), nwc[:, 0:1], gds[:, hd, :], ALU.mult, ALU.mult),
                                reads=[pk, ("nwc",), ("gds",)], writes=[("aT", hd)])
                for j in range(8):
                    p, pk = nP()
                    for kc in range(4):
                        self.op("pe", MM(p[:, 0:TT], Wua[:, kc, j * 128:(j + 1) * 128], aT[:, kc, :], kc == 0, kc == 3),
                                reads=[("aT", kc), ("Wua", kc)], writes=[pk])
                    tb = j % 2
                    self.op("dve", TTOP(tmpm[:, tb, :], p[:, 0:TT], gas[:, j, :], ALU.mult), reads=[pk, ("gas",)], writes=[("tmpm", tb)])
                    self.op("dve", TTOP(mg[:, j, :], tmpm[:, tb, :], sbs[:, j, :], ALU.add),
                            reads=[("tmpm", tb), ("sbs",)], writes=[("mg", j)])

                if i + 1 < ntiles:
                    load_scr(i + 1)

            def stageE(i):
                t0, si, xs, xk = tile_ctx(i)
                if i + 1 < ntiles:
                    load_x(i + 1)
                self.resid_proj(xs, xk, mg, "mg", Wout, "Wout", 8, nP, NB)

            def stageFG(i):
                t0, si, xs, xk = tile_ctx(i)
                if si != cur["seq"]:
                    cur["seq"] = si
                    self.dma("sp", KT1[:, :, :], kT[si].rearrange("(j p) m -> p j m", p=128), writes=[("KT",)])
                    self.dma("sp", VV1[:, :, :], vv[si].rearrange("(b p) d -> p b d", p=128), writes=[("VV",)])
                for b in range(NB):
                    self.rmsnorm_tok(xs[:, b, :], xk, wbc[:, :], junk[:, :], ss[:, b:b + 1], rstd[:, b:b + 1],
                                     h[:, b, :], ("h", b), ("ss", b))
                for b in range(NB):
                    p, pk = nT()
                    for kc in range(8):
                        self.op("pe", TR(p[:, kc, :], h[:, b, kc * 128:(kc + 1) * 128], ident[:, :]), reads=[("h", b), ("ident",)], writes=[pk])
                    self.op("act", ACP(hT[:, :, b * 128:(b + 1) * 128], p[:, :, :]), reads=[pk], writes=[("hT", b)])
                hTk = [("hT", b) for b in range(NB)]
                for j in range(8):
                    p, pk = nP()
                    for kc in range(8):
                        self.op("pe", MM(p[:, 0:TT], Wq[:, kc, j * 128:(j + 1) * 128], hT[:, kc, :], kc == 0, kc == 7),
                                reads=hTk + [("Wq", kc)], writes=[pk])
                    self.op("act", ACTF(qT[:, j, :], p[:, 0:TT], AF.Copy, scale=1.0 / 16), reads=[pk], writes=[("qT", j)])
                def att_scores(hd):
                    ps_ = hd % 2
                    for mb in range(2):
                        p, pk = nP()
                        for dc in range(2):
                            self.op("pe", MM(p[:, 0:TT], KT1[:, 2 * hd + dc, mb * 128:(mb + 1) * 128], qT[:, 2 * hd + dc, :], dc == 0, dc == 1),
                                    reads=[("KT",), ("qT", 2 * hd + dc)], writes=[pk])
                        self.op("act", ACTF(pex[:, ps_, mb, :], p[:, 0:TT], AF.Exp), reads=[pk], writes=[("pex", ps_, mb)])

                def att_out(hd):
                    ps_ = hd % 2
                    p, pk = nP()
                    for mb in range(2):
                        self.op("pe", MM(p[:, 0:TT], onesb[:, :], pex[:, ps_, mb, :], mb == 0, mb == 1),
                                reads=[("onesb",), ("pex", ps_, mb)], writes=[pk])
                    self.op("act", ACTF(rinv[:, 0, :], p[:, 0:TT], AF.Ln), reads=[pk], writes=[("rinv", 0)])
                    self.op("act", ACTF(rinv[:, 0, :], rinv[:, 0, :], AF.Exp, scale=-1.0), reads=[("rinv", 0)], writes=[("rinv", 0)])
                    for dc in range(2):
                        p, pk = nP()
                        for mb in range(2):
                            self.op("pe", MM(p[:, 0:TT], VV1[:, mb, (2 * hd + dc) * 128:(2 * hd + dc + 1) * 128], pex[:, ps_, mb, :], mb == 0, mb == 1),
                                    reads=[("VV",), ("pex", ps_, mb)], writes=[pk])
                        self.op("dve", TTOP(oT[:, 2 * hd + dc, :], p[:, 0:TT], rinv[:, 0, :], ALU.mult), reads=[pk, ("rinv", 0)], writes=[("oT", 2 * hd + dc)])

                att_scores(0)
                for hd in range(4):
                    if hd + 1 < 4:
                        att_scores(hd + 1)
                    att_out(hd)
                self.resid_proj(xs, xk, oT, "oT", Wo, "Wo", 8, nP, NB)
                self.dma("sp", x2out[t0:t0 + TT, :].rearrange("(b p) d -> p b d", p=128), xs, reads=[xk])

            stageBD(0)
            for i in range(ntiles):
                stageE(i)
                if i + 1 < ntiles:
                    stageBD(i + 1)
                stageFG(i)
            self.S.flush()

    def resid_proj(self, xs, xkey, aT, aname, W, wname, KC, nP, NB):
        for b in range(NB):
            for nh in range(2):
                p, pk = nP()
                for kc in range(KC):
                    self.op("pe", MM(p[:, :], aT[:, kc, b * 128:(b + 1) * 128], W[:, kc, nh * 512:(nh + 1) * 512], kc == 0, kc == KC - 1),
                            reads=[(aname, kc), (wname, kc)], writes=[pk])
                xsl = xs[:, b, nh * 512:(nh + 1) * 512]
                self.op("dve", TTOP(xsl, xsl, p[:, :], ALU.add), reads=[pk, xkey], writes=[xkey])


def make_consts():
    p = np.arange(128)[:, None]; f = np.arange(128)[None, :]
    BIG = 30000.0
    c = np.zeros((12, 128, 128), np.float32)
    c[0] = (p == f)
    c[1] = np.where(f >= p, BIG, 0.0)
    c[2] = np.where(f < p, -BIG, 0.0)
    c[3] = np.where(f <= p, BIG, 0.0)
    c[4] = np.where(f > p, -BIG, 0.0)
    c[5] = (p // 32 == f // 32)
    c[6] = (p // 64 == f // 64) & ((p % 64) // 32 == 1) & ((f % 64) // 32 == 0)
    c[7] = (p // 64 == 1) & (f // 64 == 0)
    c[8] = c[6].T
    c[9] = c[7].T
    c[10] = (p <= f)
    c[11] = (p >= f)
    return c


SEQS = (8192, 2048, 2048)
NCORES = 8
WNAMES = ["norm_mix_w", "w_in", "conv_w", "dn_a_log", "dn_dt_bias", "dn_norm_w", "w_up_a", "sg_ln_w", "sg_ln_b",
          "sg_w", "sg_b", "w_up_b", "w_out", "norm_xa_w", "norm_mem_w", "xa_w_q", "xa_w_kv", "xa_w_o",
          "norm_ffn_w", "ffn_w_gate_up", "ffn_w_down", "final_norm_w"]


def build_program(seqs=SEQS, wshapes=None):
    k = KB(seqs, debug=False)
    T = k.T
    nS = len(seqs)
    x = k.din("x", [T, D])
    mem = k.din("mem", [nS * 256, D])
    W = {n: k.din(n, list(wshapes[n])) for n in WNAMES}
    idf = k.din("idf", [128, 128])
    cst = k.din("cst", [12, 128, 128])
    y = k.dout("y", [T, D])
    scr = dict(qkvT=k.dscr("qkvT", [1536, T], BF16), bg=k.dscr("bg", [16, T]), gdT=k.dscr("gdT", [512, T], BF16),
               gaT=k.dscr("gaT", [1024, T], BF16), sbT=k.dscr("sbT", [1024, T], BF16))
    of = k.dscr("of", [T, 512])
    ob = k.dscr("ob", [T, 512])
    kT = k.dscr("kT", [nS, 1024, 256], BF16)
    vv = k.dscr("vv", [nS, 256, 1024], BF16)
    k.phase_mem(mem, W["norm_mem_w"], W["xa_w_kv"], idf, kT, vv)
    k.phase_in(x, W["w_in"], W["norm_mix_w"], W["sg_ln_w"], W["sg_ln_b"], W["sg_w"], W["sg_b"], W["w_up_b"],
               W["dn_a_log"], W["dn_dt_bias"], idf, scr)
    pre = dict(qn=k.dscr("qn", [512, T], BF16), kn=k.dscr("kn", [512, T], BF16),
               ktok=k.dscr("ktok", [T, 512], BF16), vtok=k.dscr("vtok", [T, 512], BF16))
    k.phase_prep(scr, W["conv_w"], cst, pre)
    k.phase_dn(scr, pre, cst, of, ob)
    k.phase_mid(x, scr, of, ob, kT, vv, W["dn_norm_w"], W["w_up_a"], W["w_out"], W["norm_xa_w"], W["xa_w_q"], W["xa_w_o"], idf, y)
    k.phase_ffn(y, y, W["ffn_w_gate_up"], W["ffn_w_down"], W["norm_ffn_w"], W["final_norm_w"], idf)
    k.es.close()
    return k


def kernel(**inputs):
    f32 = np.float32
    xp = np.asarray(inputs["x_prompt"], dtype=f32)
    xsm = np.asarray(inputs["x_sample"], dtype=f32)
    mp = np.asarray(inputs["mem_prompt"], dtype=f32)
    msm = np.asarray(inputs["mem_sample"], dtype=f32)
    w = {}
    for n in WNAMES:
        a = np.asarray(inputs[n], dtype=f32)
        if n != "final_norm_w":
            a = a[0]
        if n in ("dn_a_log", "dn_dt_bias"):
            a = a.reshape(8)
        w[n] = np.ascontiguousarray(a)
    k = build_program(SEQS, {n: w[n].shape for n in WNAMES})
    idf = np.eye(128, dtype=f32)
    cst = make_consts()
    in_maps = []
    for c in range(NCORES):
        m = dict(w)
        m["x"] = np.ascontiguousarray(np.concatenate([xp[c], xsm[2 * c], xsm[2 * c + 1]], 0))
        m["mem"] = np.ascontiguousarray(np.concatenate([mp[c], msm[2 * c], msm[2 * c + 1]], 0))
        m["idf"] = idf
        m["cst"] = cst
        in_maps.append(m)
    res = run_bass_kernel_spmd(k.nc, in_maps, core_ids=list(range(NCORES)))
    yp = np.empty(xp.shape, f32)
    ys = np.empty(xsm.shape, f32)
    for c in range(NCORES):
        y = np.asarray(res.results[c]["y"], dtype=f32)
        yp[c] = y[:8192]
        ys[2 * c] = y[8192:10240]
        ys[2 * c + 1] = y[10240:12288]
    return (yp, ys)
```
